# Optimizing a Trainium2 kernel written in Bass

```python
import jax, jax.numpy as jnp
from jax import lax
import numpy as np

D_MODEL = 1024
BATCH = 8
SEQ = 4096
DEPTH = 2

N_EVEN = (DEPTH + 1) // 2
N_ODD = DEPTH // 2
RMS_EPS = 1e-5
LN_EPS = 1e-5
D_FF = 4 * D_MODEL

GM_WIDTH = D_MODEL
GM_GROUPS = 8
GM_CH = GM_WIDTH // GM_GROUPS
GM_CHUNK = 128

SSM_D_INNER = D_MODEL
SSM_HEADDIM = 64
SSM_HEADS = SSM_D_INNER // SSM_HEADDIM
SSM_GROUPS = 4
SSM_STATE = 128
SSM_CONV = 4
SSD_CHUNK = 128
SSM_CONV_DIM = SSM_D_INNER + 2 * SSM_GROUPS * SSM_STATE

IN_EVEN = 2 * GM_WIDTH + SSM_D_INNER + SSM_CONV_DIM + SSM_HEADS
MIX_EVEN = GM_WIDTH + SSM_D_INNER

ATTN_HEADS = 16
ATTN_KV_HEADS = 2
ATTN_HEAD_DIM = 64
ATTN_WINDOW = 128
ATTN_BLOCK = ATTN_WINDOW
QKV_DIM = (ATTN_HEADS + 2 * ATTN_KV_HEADS) * ATTN_HEAD_DIM

kernel_name = "hybrid_gmlp_ssd_swa_sinks"


def rmsnorm(x, g, eps=RMS_EPS):
    xf = x.astype(jnp.float32)
    y = xf * lax.rsqrt(jnp.mean(xf * xf, axis=-1, keepdims=True) + eps)
    return (y * g.astype(jnp.float32)).astype(x.dtype)


def layernorm(x, g, b, eps=LN_EPS):
    xf = x.astype(jnp.float32)
    mu = jnp.mean(xf, axis=-1, keepdims=True)
    var = jnp.mean(jnp.square(xf - mu), axis=-1, keepdims=True)
    y = (xf - mu) * lax.rsqrt(var + eps)
    return (y * g.astype(jnp.float32) + b.astype(jnp.float32)).astype(x.dtype)


def gmlp_spatial_gating(u, v, ln_g, ln_b, w_s, b_s):
    b, s, _ = v.shape
    nc = s // GM_CHUNK
    vn = layernorm(v, ln_g, ln_b).reshape(b, nc, GM_CHUNK, GM_GROUPS, GM_CH)
    causal = jnp.tril(jnp.ones((GM_CHUNK, GM_CHUNK), dtype=bool))
    w = jnp.where(causal, w_s, 0.0).astype(v.dtype)
    mixed = jnp.einsum('gts,bcsgd->bctgd', w, vn) + b_s.T[None, None, :, :, None]
    return u * mixed.reshape(b, s, GM_WIDTH)


def causal_depthwise_conv(x, w, bias):
    k = w.shape[0]
    s = x.shape[1]
    xp = jnp.pad(x, ((0, 0), (k - 1, 0), (0, 0)))
    return sum(xp[:, i:i + s] * w[i] for i in range(k)) + bias


def ssd_chunked(x, dt, a_neg, bm, cm):
    b, s, h, p = x.shape
    g, n = bm.shape[2], bm.shape[3]
    r = h // g
    l = SSD_CHUNK
    c = s // l
    xc = x.reshape(b, c, l, g, r, p)
    bc = bm.reshape(b, c, l, g, n)
    cc = cm.reshape(b, c, l, g, n)
    dtc = dt.reshape(b, c, l, g, r)
    a_cum = jnp.cumsum(dtc * a_neg.reshape(g, r), axis=2)
    causal = jnp.tril(jnp.ones((l, l), dtype=bool))[:, :, None, None]
    seg = a_cum[:, :, :, None] - a_cum[:, :, None, :]
    decay = jnp.exp(jnp.where(causal, seg, -jnp.inf))
    cb = jnp.einsum('bclgn,bcsgn->bclsg', cc, bc)
    w_intra = (cb[..., None] * decay * dtc[:, :, None]).astype(x.dtype)
    y_diag = jnp.einsum('bclsgr,bcsgrp->bclgrp', w_intra, xc)
    to_end = (jnp.exp(a_cum[:, :, -1:] - a_cum) * dtc).astype(x.dtype)
    states = jnp.einsum('bclgn,bclgr,bclgrp->bcgrpn', bc, to_end, xc)
    chunk_decay = jnp.exp(a_cum[:, :, -1]).astype(x.dtype)

    def step(hstate, inp):
        dec, st = inp
        return dec[..., None, None] * hstate + st, hstate

    h0 = jnp.zeros((b, g, r, p, n), x.dtype)
    _, prev = lax.scan(step, h0, (jnp.moveaxis(chunk_decay, 1, 0), jnp.moveaxis(states, 1, 0)))
    prev = jnp.moveaxis(prev, 0, 1)
    y_off = jnp.einsum('bclgn,bcgrpn,bclgr->bclgrp', cc, prev, jnp.exp(a_cum).astype(x.dtype))
    return (y_diag + y_off).reshape(b, s, h, p)


def ssd_mixer(z, xbc, dt_raw, conv_w, conv_b, dt_bias, a_log, d_skip, norm_g):
    b, s, _ = z.shape
    xbc = jax.nn.silu(causal_depthwise_conv(xbc, conv_w, conv_b))
    xs, bm, cm = jnp.split(xbc, [SSM_D_INNER, SSM_D_INNER + SSM_GROUPS * SSM_STATE], axis=-1)
    xs = xs.reshape(b, s, SSM_HEADS, SSM_HEADDIM)
    bm = bm.reshape(b, s, SSM_GROUPS, SSM_STATE)
    cm = cm.reshape(b, s, SSM_GROUPS, SSM_STATE)
    dt = jax.nn.softplus(dt_raw.astype(jnp.float32) + dt_bias.astype(jnp.float32))
    a_neg = -jnp.exp(a_log.astype(jnp.float32))
    y = ssd_chunked(xs, dt, a_neg, bm, cm) + xs * d_skip[:, None]
    y = y.reshape(b, s, SSM_D_INNER) * jax.nn.silu(z)
    y = rmsnorm(y.reshape(b, s, SSM_GROUPS, -1), norm_g.reshape(SSM_GROUPS, -1))
    return y.reshape(b, s, SSM_D_INNER)


def even_mixer(h, w_in, w_out, gm_ln_g, gm_ln_b, gm_w_s, gm_b_s,
               conv_w, conv_b, dt_bias, a_log, d_skip, ssm_norm_g):
    proj = h @ w_in
    u, v, z, xbc, dt_raw = jnp.split(
        proj, [GM_WIDTH, 2 * GM_WIDTH, 2 * GM_WIDTH + SSM_D_INNER,
               2 * GM_WIDTH + SSM_D_INNER + SSM_CONV_DIM], axis=-1)
    a_out = gmlp_spatial_gating(jax.nn.gelu(u), jax.nn.gelu(v), gm_ln_g, gm_ln_b, gm_w_s, gm_b_s)
    b_out = ssd_mixer(z, xbc, dt_raw, conv_w, conv_b, dt_bias, a_log, d_skip, ssm_norm_g)
    return jnp.concatenate([a_out, b_out], axis=-1) @ w_out


def sliding_window_sink_attention(q, k, v, sinks):
    b, s, _, d = q.shape
    blk = ATTN_BLOCK
    nb = s // blk
    grp = ATTN_HEADS // ATTN_KV_HEADS
    qb = q.reshape(b, nb, blk, ATTN_KV_HEADS, grp, d)
    kp = jnp.pad(k, ((0, 0), (blk, 0), (0, 0), (0, 0))).reshape(b, nb + 1, blk, ATTN_KV_HEADS, d)
    vp = jnp.pad(v, ((0, 0), (blk, 0), (0, 0), (0, 0))).reshape(b, nb + 1, blk, ATTN_KV_HEADS, d)
    kband = jnp.concatenate([kp[:, :-1], kp[:, 1:]], axis=2)
    vband = jnp.concatenate([vp[:, :-1], vp[:, 1:]], axis=2)
    scores = jnp.einsum('bnqkgd,bnskd->bnkgqs', qb, kband).astype(jnp.float32) * (d ** -0.5)
    qpos = jnp.arange(nb)[:, None, None] * blk + jnp.arange(blk)[None, :, None]
    kpos = jnp.arange(nb)[:, None, None] * blk - blk + jnp.arange(2 * blk)[None, None, :]
    rel = qpos - kpos
    valid = (rel >= 0) & (rel < ATTN_WINDOW) & (kpos >= 0)
    scores = jnp.where(valid[None, :, None, None], scores, -jnp.inf)
    sink = sinks.astype(jnp.float32).reshape(ATTN_KV_HEADS, grp)[None, None, :, :, None, None]
    m = jnp.maximum(jnp.max(scores, axis=-1, keepdims=True), sink)
    pexp = jnp.exp(scores - m)
    denom = jnp.sum(pexp, axis=-1, keepdims=True) + jnp.exp(sink - m)
    probs = (pexp / denom).astype(v.dtype)
    out = jnp.einsum('bnkgqs,bnskd->bnqkgd', probs, vband)
    return out.reshape(b, s, ATTN_HEADS * d)


def odd_mixer(h, w_qkv, b_qkv, w_o, b_o, sinks):
    b, s, _ = h.shape
    qkv = h @ w_qkv + b_qkv
    q, k, v = jnp.split(qkv, [ATTN_HEADS * ATTN_HEAD_DIM, (ATTN_HEADS + ATTN_KV_HEADS) * ATTN_HEAD_DIM], axis=-1)
    q = q.reshape(b, s, ATTN_HEADS, ATTN_HEAD_DIM)
    k = k.reshape(b, s, ATTN_KV_HEADS, ATTN_HEAD_DIM)
    v = v.reshape(b, s, ATTN_KV_HEADS, ATTN_HEAD_DIM)
    return sliding_window_sink_attention(q, k, v, sinks) @ w_o + b_o


def squared_relu_mlp(h, w_up, w_down):
    return jnp.square(jax.nn.relu(h @ w_up)) @ w_down


def setup_inputs(seed: int = 0) -> dict:
    key = jax.random.key(seed)
    ks = jax.random.split(key, 24)
    nrm = jax.random.normal
    f32 = jnp.float32
    dt0 = jnp.exp(jax.random.uniform(ks[10], (N_EVEN, SSM_HEADS), f32, np.log(1e-3), np.log(1e-1)))
    return {
        "x": nrm(ks[0], (BATCH, SEQ, D_MODEL), f32),
        "norm_mix_g": 1.0 + 0.02 * nrm(ks[1], (DEPTH, D_MODEL), f32),
        "norm_mlp_g": 1.0 + 0.02 * nrm(ks[2], (DEPTH, D_MODEL), f32),
        "final_norm_g": 1.0 + 0.02 * nrm(ks[3], (D_MODEL,), f32),
        "w_in_even": nrm(ks[4], (N_EVEN, D_MODEL, IN_EVEN), f32) * D_MODEL ** -0.5,
        "w_out_even": nrm(ks[5], (N_EVEN, MIX_EVEN, D_MODEL), f32) * MIX_EVEN ** -0.5,
        "gm_ln_g": 1.0 + 0.02 * nrm(ks[6], (N_EVEN, GM_WIDTH), f32),
        "gm_ln_b": 0.02 * nrm(ks[7], (N_EVEN, GM_WIDTH), f32),
        "gm_w_s": nrm(ks[8], (N_EVEN, GM_GROUPS, GM_CHUNK, GM_CHUNK), f32) * 0.5 * GM_CHUNK ** -0.5,
        "gm_b_s": 1.0 + 0.02 * nrm(ks[9], (N_EVEN, GM_GROUPS, GM_CHUNK), f32),
        "ssm_conv_w": nrm(ks[11], (N_EVEN, SSM_CONV, SSM_CONV_DIM), f32) * SSM_CONV ** -0.5,
        "ssm_conv_b": 0.02 * nrm(ks[12], (N_EVEN, SSM_CONV_DIM), f32),
        "ssm_dt_bias": dt0 + jnp.log(-jnp.expm1(-dt0)),
        "ssm_a_log": jnp.log(jax.random.uniform(ks[13], (N_EVEN, SSM_HEADS), f32, 1.0, 16.0)),
        "ssm_d": 1.0 + 0.1 * nrm(ks[14], (N_EVEN, SSM_HEADS), f32),
        "ssm_norm_g": 1.0 + 0.02 * nrm(ks[15], (N_EVEN, SSM_D_INNER), f32),
        "w_qkv": nrm(ks[16], (N_ODD, D_MODEL, QKV_DIM), f32) * D_MODEL ** -0.5,
        "b_qkv": 0.02 * nrm(ks[17], (N_ODD, QKV_DIM), f32),
        "w_o": nrm(ks[18], (N_ODD, ATTN_HEADS * ATTN_HEAD_DIM, D_MODEL), f32) * (ATTN_HEADS * ATTN_HEAD_DIM) ** -0.5,
        "b_o": 0.02 * nrm(ks[19], (N_ODD, D_MODEL), f32),
        "attn_sinks": 0.5 * nrm(ks[20], (N_ODD, ATTN_HEADS), f32),
        "w_up": nrm(ks[21], (DEPTH, D_MODEL, D_FF), f32) * D_MODEL ** -0.5,
        "w_down": nrm(ks[22], (DEPTH, D_FF, D_MODEL), f32) * D_FF ** -0.5,
    }


def reference(x, norm_mix_g, norm_mlp_g, final_norm_g, w_in_even, w_out_even,
              gm_ln_g, gm_ln_b, gm_w_s, gm_b_s, ssm_conv_w, ssm_conv_b,
              ssm_dt_bias, ssm_a_log, ssm_d, ssm_norm_g,
              w_qkv, b_qkv, w_o, b_o, attn_sinks, w_up, w_down):
    h = x
    for i in range(DEPTH):
        j = i // 2
        y = rmsnorm(h, norm_mix_g[i])
        if i % 2 == 0:
            h = h + even_mixer(y, w_in_even[j], w_out_even[j], gm_ln_g[j], gm_ln_b[j],
                               gm_w_s[j], gm_b_s[j], ssm_conv_w[j], ssm_conv_b[j],
                               ssm_dt_bias[j], ssm_a_log[j], ssm_d[j], ssm_norm_g[j])
        else:
            h = h + odd_mixer(y, w_qkv[j], b_qkv[j], w_o[j], b_o[j], attn_sinks[j])
        y = rmsnorm(h, norm_mlp_g[i])
        h = h + squared_relu_mlp(y, w_up[i], w_down[i])
    return rmsnorm(h, final_norm_g)
```

```python
import numpy as np
from contextlib import ExitStack
import concourse.bass as bass
import concourse.mybir as mybir
from concourse.bass_utils import run_bass_kernel_spmd

F32 = mybir.dt.float32
BF16 = mybir.dt.bfloat16
AF = mybir.ActivationFunctionType
ALU = mybir.AluOpType
AX = mybir.AxisListType

ENGS = ["pe", "act", "dve", "pool", "sp"]
_ESZ = {F32: 4, BF16: 2}
_PAGE = {"SB": 256, "PSUM": 2048, "DRAM": 1 << 18}


class Buf:
    __slots__ = ("w", "r")

    def __init__(self):
        self.w = None
        self.r = []


class Chan:
    def __init__(self, key):
        self.key = key
        self.last = None
        self.count = 0


def _space(ap):
    s = str(ap.space)
    if "PSUM" in s:
        return "PSUM"
    if "DRAM" in s or "HBM" in s:
        return "DRAM"
    return "SB"


class Prog:
    SAME_ENGINE_SYNC = True
    SCHEDULE = True
    LAT = 0.25

    def __init__(self):
        self.recs = []
        self.tags = []
        self.tag = 'setup'
        self.chans = []
        self.bufs = {}
        self.nops = 0
        self.ops = {e: [] for e in ENGS}

    def chan(self):
        c = Chan("ch%d" % len(self.chans))
        self.chans.append(c)
        return c

    def regs(self, ap):
        sp = _space(ap)
        esz = _ESZ[ap.dtype]
        pat = ap.ap
        off = ap.offset
        if sp == "DRAM":
            f0 = off
            dims = pat
        else:
            pstep = pat[0][0]
            f0 = off % pstep if pstep > 0 else off
            dims = pat[1:]
        ext = 0
        for s, c in dims:
            ext += abs(s) * (c - 1)
        lo = f0 * esz
        hi = (f0 + ext) * esz + esz - 1
        pg = _PAGE[sp]
        name = ap.tensor.name
        out = []
        for p in range(lo // pg, hi // pg + 1):
            k = (name, p)
            b = self.bufs.get(k)
            if b is None:
                b = self.bufs[k] = Buf()
            out.append(b)
        return out

    def op(self, eng, fn, reads=(), writes=(), chan=None, dur=0.3, xfer=0.0):
        rb = []
        wb = []
        for a in reads:
            (wb if _space(a) == "PSUM" else rb).extend(self.regs(a))
        for a in writes:
            wb.extend(self.regs(a))
        oid = len(self.recs)
        deps = set()
        for b in rb:
            if b.w is not None:
                deps.add(b.w)
        for b in wb:
            if b.w is not None:
                deps.add(b.w)
            deps.update(b.r)
        if chan is not None:
            if chan.last is not None:
                deps.add(chan.last)
            chan.last = oid
        deps.discard(oid)
        for b in rb:
            b.r.append(oid)
        for b in wb:
            b.w = oid
            b.r = []
        self.recs.append((eng, fn, deps, chan, dur, xfer))
        self.tags.append(self.tag)
        self.ops[eng].append(oid)
        self.nops += 1
        return oid

    def schedule(self):
        import heapq
        recs = self.recs
        n = len(recs)
        if not self.SCHEDULE:
            order = {e: list(self.ops[e]) for e in ENGS}
            return order, 0.0
        succ = [[] for _ in range(n)]
        indeg = [0] * n
        for i, r in enumerate(recs):
            indeg[i] = len(r[2])
            for d in r[2]:
                succ[d].append(i)
        ready_t = [0.0] * n
        pend = {e: [] for e in ENGS}
        avail = {e: [] for e in ENGS}
        free = {e: 0.0 for e in ENGS}
        order = {e: [] for e in ENGS}
        for i in range(n):
            if indeg[i] == 0:
                heapq.heappush(pend[recs[i][0]], (0.0, i))
        done = 0
        tend = 0.0
        while done < n:
            best = None
            for e in ENGS:
                pe_, av = pend[e], avail[e]
                while pe_ and pe_[0][0] <= free[e]:
                    heapq.heappush(av, heapq.heappop(pe_)[1])
                if av:
                    cand = (free[e], av[0], e, True)
                elif pe_:
                    cand = (pe_[0][0], pe_[0][1], e, False)
                else:
                    continue
                if best is None or cand[:2] < best[:2]:
                    best = cand
            start, i, e, from_av = best
            if from_av:
                heapq.heappop(avail[e])
            else:
                heapq.heappop(pend[e])
            eng, fn, deps, chan, dur, xfer = recs[i]
            end_eng = start + dur
            end = end_eng + xfer
            free[e] = end_eng
            order[e].append(i)
            tend = max(tend, end)
            done += 1
            for s_ in succ[i]:
                lat = self.LAT if (recs[s_][0] != e or chan is not None) else 0.05
                t_ = end + lat
                if t_ > ready_t[s_]:
                    ready_t[s_] = t_
                indeg[s_] -= 1
                if indeg[s_] == 0:
                    heapq.heappush(pend[recs[s_][0]], (ready_t[s_], s_))
        return order, tend

    def simulate(self, order):
        recs = self.recs
        endt = [None] * len(recs)
        pos = {e: 0 for e in ENGS}
        free = {e: 0.0 for e in ENGS}
        busy = {e: 0.0 for e in ENGS}
        remaining = sum(len(order[e]) for e in ENGS)
        tend = 0.0
        while remaining:
            progressed = False
            for e in ENGS:
                while pos[e] < len(order[e]):
                    i = order[e][pos[e]]
                    eng, fn, deps, chan, dur, xfer = recs[i]
                    t = free[e]
                    ok = True
                    for d in deps:
                        if endt[d] is None:
                            ok = False
                            break
                        lat = self.LAT if (recs[d][0] != e or recs[d][3] is not None) else 0.05
                        t = max(t, endt[d] + lat)
                    if not ok:
                        break
                    free[e] = t + dur
                    busy[e] += dur
                    endt[i] = t + dur + xfer
                    tend = max(tend, endt[i])
                    pos[e] += 1
                    remaining -= 1
                    progressed = True
            assert progressed, "deadlock in simulated order"
        return tend, busy

    def emit(self, nc, st):
        order, tend = self.schedule()
        self.sim_us, self.busy_us = self.simulate(order)
        self.predicted_us = tend
        recs = self.recs
        tok = [None] * len(recs)
        cnt = {e: 0 for e in ENGS}
        ccnt = {c.key: 0 for c in self.chans}
        for i, r in enumerate(recs):
            if r[3] is not None:
                ccnt[r[3].key] += 1
                tok[i] = (r[3].key, ccnt[r[3].key] * 16)
        for e in ENGS:
            for i in order[e]:
                if recs[i][3] is None:
                    cnt[e] += 1
                    tok[i] = (e, cnt[e])
        sems = {}
        for e in ENGS:
            sems[e] = st.enter_context(nc.semaphore("s_" + e))
        for c in self.chans:
            sems[c.key] = st.enter_context(nc.semaphore("s_" + c.key))
        block = st.enter_context(nc.Block())
        finals = {}
        for e in ENGS:
            if cnt[e] > 0:
                finals[e] = cnt[e]
        for c in self.chans:
            if ccnt[c.key] > 0:
                finals[c.key] = ccnt[c.key] * 16

        def mk(ename):
            def body(eng):
                known = {}
                for i in order[ename]:
                    e_, fn, deps, chan, dur, xfer = recs[i]
                    waits = {}
                    for d in deps:
                        key, val = tok[d]
                        if key == ename and chan is None and recs[d][3] is None:
                            if ename == "pe" or not self.SAME_ENGINE_SYNC:
                                continue
                        if known.get(key, 0) >= val:
                            continue
                        if waits.get(key, 0) < val:
                            waits[key] = val
                    for k, v in sorted(waits.items()):
                        known[k] = v
                        eng.wait_ge(sems[k], v)
                    ins = fn(eng)
                    if chan is not None:
                        ins.then_inc(sems[chan.key], 16)
                    else:
                        ins.then_inc(sems[ename], 1)
                if ename == "sp":
                    for k, v in finals.items():
                        eng.wait_ge(sems[k], v)
            return body

        block.tensor(mk("pe"))
        block.scalar(mk("act"))
        block.vector(mk("dve"))
        block.gpsimd(mk("pool"))
        block.sync(mk("sp"))


def _isap(x):
    return hasattr(x, "tensor") and hasattr(x, "ap")


PP_GMIX0, PP_GMLP0, PP_GMIX1, PP_GMLP1 = 0, 8, 16, 24
PP_CONVW = 32
PP_CONVB = 96
PP_BQ = 112
PP_BK = 120
PP_BO = 122
PP_NG = 130
NPP = 138
BC_LNG, BC_LNB, BC_GF = 0, 1024, 2048
BC_BV = 3072
BC_DTB, BC_ALOG, BC_DSK, BC_SINK = 3200, 3216, 3232, 3248
NBC = 3264

NB = 51
NW = 4
EPS = 1e-5


def build(NT=8, stage="full"):
    nc = bass.Bass("TRN2", target_bir_lowering=False)
    st = ExitStack()
    P = Prog()

    def dram(name, shape, dt, kind):
        return nc.dram_tensor(name, shape, dt, kind=kind).ap()

    def sb(name, shape, dt):
        return st.enter_context(nc.sbuf_tensor(name, shape, dt))

    x_d = dram("x", [4096, 1024], F32, "ExternalInput")
    out_d = dram("out", [4096, 1024], F32, "ExternalOutput")
    w_in_d = dram("w_in", [1024, 5136], F32, "ExternalInput")
    w_out_d = dram("w_out", [2048, 1024], F32, "ExternalInput")
    w_qkv_d = dram("w_qkv", [1024, 1408], F32, "ExternalInput")
    w_o_d = dram("w_o", [1024, 1024], F32, "ExternalInput")
    w_up_d = dram("w_up", [2, 1024, 4096], F32, "ExternalInput")
    w_down_d = dram("w_down", [2, 4096, 1024], F32, "ExternalInput")
    pp_d = dram("pp", [128, NPP], F32, "ExternalInput")
    bc_d = dram("bc", [1, NBC], F32, "ExternalInput")
    gmw_d = dram("gmw", [8, 128, 128], F32, "ExternalInput")
    gmb_d = dram("gmb", [1, 1024], F32, "ExternalInput")
    wscr = dram("wscr", [NB, 128, 4096], BF16, "Internal")
    dbg_d = None
    if stage != "full":
        dbg_d = dram("dbg", [NT, 16, 128, 512], BF16, "ExternalOutput")

    blocks = []

    def addblk(ap2d, kc, cols):
        blocks.append((ap2d.rearrange("(kc p) c -> p kc c", p=128), kc, cols))

    for i in range(10):
        addblk(w_in_d[:, 512 * i:512 * (i + 1)], 8, 512)
    for i in range(4):
        addblk(w_out_d[:, 256 * i:256 * (i + 1)], 16, 256)
    for i in range(8):
        addblk(w_up_d[0, :, 512 * i:512 * (i + 1)], 8, 512)
    for i in range(8):
        addblk(w_down_d[0, :, 128 * i:128 * (i + 1)], 32, 128)
    addblk(w_qkv_d[:, 0:512], 8, 512)
    addblk(w_qkv_d[:, 512:1024], 8, 512)
    addblk(w_qkv_d[:, 1024:1408], 8, 384)
    for i in range(2):
        addblk(w_o_d[:, 512 * i:512 * (i + 1)], 8, 512)
    for i in range(8):
        addblk(w_up_d[1, :, 512 * i:512 * (i + 1)], 8, 512)
    for i in range(8):
        addblk(w_down_d[1, :, 128 * i:128 * (i + 1)], 32, 128)
    assert len(blocks) == NB
    B_WIN, B_WOUT, B_UP0, B_DN0, B_QKV, B_WO, B_UP1, B_DN1 = 0, 10, 14, 22, 30, 33, 35, 43

    hTs = [sb("hT0", [128, 8, 512], F32)]
    hT = hTs[0]
    yT = sb("yT", [128, 8, 512], BF16)
    wring = [sb("wring%d" % i, [128, 4096], BF16) for i in range(NW)]
    xin = [sb("xin%d" % i, [128, 1024], F32) for i in range(2)]
    osb = [sb("osb%d" % i, [128, 1024], F32) for i in range(2)]
    pp = sb("pp_sb", [128, NPP], F32)
    bc = sb("bc_sb", [128, 1, NBC], F32)
    ident_bf = sb("ident_bf", [128, 128], BF16)
    ident_f = sb("ident_f", [128, 128], F32)
    ones_bf = sb("ones_bf", [128, 128], BF16)
    ones_f = sb("ones_f", [128, 128], F32)
    triu_f = sb("triu_f", [128, 128], F32)
    slow_f = sb("slow_f", [128, 128], F32)
    maskb = sb("maskb", [128, 2, 256], BF16)
    maskb0 = sb("maskb0", [128, 2, 256], BF16)
    gmWT = sb("gmWT", [128, 8, 128], BF16)
    bs_hi = sb("bs_hi", [1, 1024], BF16)
    bs_lo = sb("bs_lo", [1, 1024], BF16)
    wdt = sb("wdt", [128, 8, 16], BF16)
    identD = sb("identD", [128, 16, 128], BF16)
    aneg = sb("aneg", [128, 16], F32)
    negsink = sb("negsink", [128, 16], F32)
    prevT = sb("prevT", [128, 1024], F32)
    prev_bf = sb("prev_bf", [128, 1024], BF16)
    tails = sb("tails", [128, 16, 3], F32)
    kT = sb("kT", [128, 2, 640], BF16)
    vtok = sb("vtok", [128, 5, 128], BF16)
    small = sb("small", [128, 16, 64], F32)
    junk = sb("junk", [128, 512], BF16)

    SCR_BYTES = 84 * 1024
    scr = sb("scr", [128, SCR_BYTES // 2], BF16)

    def sview(off, shape, dt):
        n = 1
        for s in shape:
            n *= s
        nb = n * _ESZ[dt]
        assert off % 4 == 0 and off + nb <= SCR_BYTES, (off, nb)
        v = scr[:, off // 2:(off + nb) // 2]
        if dt == F32:
            v = v.bitcast(F32)
        if len(shape) == 2:
            return v.rearrange("p (a b) -> p a b", b=shape[1])
        if len(shape) == 3:
            return v.rearrange("p (a b c) -> p a b c", b=shape[1], c=shape[2])
        return v

    K = 1024
    mixT = sview(0, [16, 512], BF16)
    xcT = sview(16 * K, [16, 512], BF16)
    sz = sview(32 * K, [4, 1024], BF16)
    xst = [sview(40 * K + i * 2304, [576], F32) for i in range(2)]
    acc = [sview(45 * K + i * 2048, [512], F32) for i in range(2)]
    guT = sview(49 * K, [8, 512], BF16)
    vn = sview(57 * K, [4, 1024], BF16)
    gv = sview(65 * K, [4, 1024], F32)
    gmw_ld = sview(0, [8, 128], F32)
    bs_f = sview(4 * K, [1024], F32)
    wdt_f = sview(8 * K, [8, 16], F32)
    maskf = sview(9 * K, [256], F32)
    yv = sview(40 * K, [1024], F32)
    t1 = sview(45 * K, [1024], F32)
    xtok = [sview(49 * K + i * 2048, [1024], BF16) for i in range(2)]
    xw = [sview(59 * K + i * 2048, [1024], BF16) for i in range(2)]
    Btok = [sview(63 * K + i * 1024, [512], BF16) for i in range(2)]
    cbm = sview(65 * K, [4, 128], F32)
    Rb = [sview(67 * K + i * 2048, [4, 128], F32) for i in range(2)]
    wT = [sview(71 * K + i * 4096, [16, 128], BF16) for i in range(2)]
    bout = [sview(79 * K + i * 2048, [1024], BF16) for i in range(2)]
    nsq = sview(66 * K, [8, 512], BF16)
    rstd_sb = sview(74 * K, [512], F32)
    tmpn = [sview(76 * K + i * 2048, [512], F32) for i in range(2)]
    hid = sview(0, [32, 512], BF16)
    r32 = [sview(32 * K + i * 2048, [512], F32) for i in range(2)]
    qT = sview(0, [8, 512], BF16)
    aoT = sview(8 * K, [8, 512], BF16)
    Pb = [sview(16 * K + i * 1024, [2, 256], BF16) for i in range(2)]
    PT = [sview(18 * K + i * 1024, [4, 128], BF16) for i in range(2)]
    attn = [sview(20 * K + i * 2048, [1024], BF16) for i in range(2)]

    pairs = [st.enter_context(nc.psum_tensor("ps%d" % i, [128, 1024], F32)) for i in range(4)]

    def bank(b):
        return pairs[b // 2][:, (b % 2) * 512:(b % 2 + 1) * 512]

    def bank16(b):
        return pairs[b // 2][:].bitcast(BF16)[:, (b % 2) * 1024:(b % 2 + 1) * 1024]

    rr = {"b": 0}

    def nextbank():
        b = rr["b"]
        rr["b"] = (b + 1) % 8
        return bank(b)

    def nfree(ap):
        n = 1
        for s in ap.shape[1:]:
            n *= s
        return n

    def c_act(ap, accum=False):
        return 0.2 + nfree(ap) * 0.00085 + (0.1 if accum else 0.0)

    def c_dve(ap, k=1.0):
        return 0.07 + nfree(ap) * 0.00100 * k

    def c_mm(rhs, lhsT):
        n_ = nfree(rhs)
        f = 4.0 if rhs.dtype == F32 else 1.0
        return 0.025 + max(n_, 64) * 0.00042 * f

    def DMA(out, in_, chan, eng="sp", issue=0.1):
        nbytes = nfree(out) * out.shape[0] * max(_ESZ[out.dtype], _ESZ[in_.dtype])
        P.op(eng, lambda e: e.dma_start(out=out, in_=in_), reads=[in_], writes=[out], chan=chan,
             dur=issue, xfer=2.0 + nbytes / 200e3)

    def ACT(out, in_, func, bias=None, scale=None, accum=None):
        kw = {}
        rd = [in_]
        wr = [out]
        if bias is not None:
            kw["bias"] = bias
            if _isap(bias):
                rd.append(bias)
        if scale is not None:
            kw["scale"] = scale
            if _isap(scale):
                rd.append(scale)
        if accum is not None:
            kw["accum_out"] = accum
            wr.append(accum)
        P.op("act", lambda e: e.activation(out=out, in_=in_, func=func, **kw), reads=rd, writes=wr,
             dur=c_act(in_, accum is not None))

    def COPY(eng, out, in_):
        if eng == "act":
            ACT(out, in_, AF.Copy)
        else:
            P.op(eng, lambda e: e.tensor_copy(out=out, in_=in_), reads=[in_], writes=[out],
                 dur=c_dve(in_, 3.0 if eng == "pool" else 1.0))

    def TT(eng, out, in0, in1, op):
        P.op(eng, lambda e: e.tensor_tensor(out=out, in0=in0, in1=in1, op=op), reads=[in0, in1], writes=[out],
             dur=c_dve(out, 2.5 if eng == "pool" else 1.05))

    def TS(eng, out, in0, s1, op0, s2=None, op1=None):
        rd = [in0] + [s for s in (s1, s2) if _isap(s)]
        d_ = c_dve(out, 10.0 if eng == "pool" else 1.0)
        if op1 is None:
            P.op(eng, lambda e: e.tensor_scalar(out=out, in0=in0, scalar1=s1, scalar2=None, op0=op0), reads=rd, writes=[out], dur=d_)
        else:
            P.op(eng, lambda e: e.tensor_scalar(out=out, in0=in0, scalar1=s1, scalar2=s2, op0=op0, op1=op1), reads=rd, writes=[out], dur=d_)

    def STT(out, in0, scalar, in1, op0, op1):
        rd = [in0, in1] + ([scalar] if _isap(scalar) else [])
        both_sb = _space(in0) == "SB" and _space(in1) == "SB"
        P.op("dve", lambda e: e.scalar_tensor_tensor(out=out, in0=in0, scalar=scalar, in1=in1, op0=op0, op1=op1), reads=rd, writes=[out],
             dur=c_dve(out, 1.25 if both_sb else 1.0))

    def DVEOP(fn, reads, writes, dur):
        P.op("dve", fn, reads=reads, writes=writes, dur=dur)

    def MEMSET(eng, ap, val):
        P.op(eng, lambda e: e.memset(ap, val), writes=[ap], dur=c_dve(ap, 2.0))

    def ASEL(out, in_, step, n, cmp, fill, base, cm):
        P.op("pool", lambda e: e.affine_select(out=out, in_=in_, pattern=[[step, n]], compare_op=cmp, fill=fill, base=base, channel_multiplier=cm), reads=[in_], writes=[out], dur=0.5)

    def PE(mms, extra_reads=(), writes=None):
        rd = list(extra_reads)
        wr = []
        d_ = 0.02
        for (o, l, r, s0, s1) in mms:
            rd.append(l)
            rd.append(r)
            wr.append(o)
            d_ += c_mm(r, l)
        if writes is not None:
            wr = writes

        def fn(e):
            ins = None
            for (o, l, r, s0, s1) in mms:
                ins = e.matmul(o, lhsT=l, rhs=r, start=s0, stop=s1)
            return ins
        P.op("pe", fn, reads=rd, writes=wr, dur=d_)

    def MMG(out, pairs_):
        n = len(pairs_)
        PE([(out, l, r, i == 0, i == n - 1) for i, (l, r) in enumerate(pairs_)])

    def TR(items, writes):
        rd = []
        d_ = 0.05
        for (o, i_, idn) in items:
            rd.append(i_)
            rd.append(idn)
            d_ += 0.11 * (2.0 if i_.dtype == F32 else 1.0)

        def fn(e):
            ins = None
            for (o, i_, idn) in items:
                ins = e.transpose(out=o, in_=i_, identity=idn)
            return ins
        P.op("pe", fn, reads=rd, writes=writes, dur=d_)

    def kouter4(wv, nk=8):
        bks = [nextbank() for _ in range(4)]
        for k in range(nk):
            PE([(bks[o], wv[:, k, o * 128:(o + 1) * 128], yT[:, k, :], k == 0, k == nk - 1) for o in range(4)], writes=bks)
        return bks

    ppc = lambda c: pp[:, c:c + 1]
    bcv = lambda c, n: bc[:, 0, c:c + n]

    ch_setup = [P.chan() for _ in range(4)]
    DMA(pp[:], pp_d[:, :], ch_setup[0])
    DMA(bc[:], bc_d.partition_broadcast(128), ch_setup[1])
    DMA(gmw_ld, gmw_d.rearrange("g t s -> t g s"), ch_setup[2])
    DMA(bs_f[0:1, :], gmb_d[:, :], ch_setup[3])
    DMA(wdt_f, w_in_d[:, 5120:5136].rearrange("(kc p) c -> p kc c", p=128), ch_setup[0])

    MEMSET("pool", ident_f[:], 1.0)
    ASEL(ident_f[:], ident_f[:], -1, 128, ALU.is_equal, 0.0, 0, 1)
    COPY("pool", ident_bf[:], ident_f[:])
    MEMSET("pool", ones_f[:], 1.0)
    MEMSET("pool", ones_bf[:], 1.0)
    MEMSET("pool", triu_f[:], 1.0)
    ASEL(triu_f[:], triu_f[:], 1, 128, ALU.is_ge, 0.0, 0, -1)
    MEMSET("pool", slow_f[:], 1.0)
    ASEL(slow_f[:], slow_f[:], -1, 128, ALU.is_ge, 0.0, -1, 1)
    MEMSET("pool", maskf, 0.0)
    ASEL(maskf[:, 0:128], maskf[:, 0:128], 1, 128, ALU.is_ge, -30000.0, -1, -1)
    ASEL(maskf[:, 128:256], maskf[:, 128:256], -1, 128, ALU.is_ge, -30000.0, 0, 1)
    for e_ in range(2):
        COPY("pool", maskb[:, e_, :], maskf)
        COPY("pool", maskb0[:, e_, :], maskf)
        MEMSET("pool", maskb0[:, e_, 0:128], -30000.0)
    MEMSET("pool", prevT[:], 0.0)
    MEMSET("pool", prev_bf[:], 0.0)
    MEMSET("pool", tails[:], 0.0)
    MEMSET("pool", kT[:], 0.0)
    MEMSET("pool", vtok[:], 0.0)
    COPY("pool", wdt[:], wdt_f)
    COPY("dve", bs_hi[:], bs_f[0:1, :])
    TT("dve", bs_lo[:], bs_f[0:1, :], bs_hi[:], ALU.subtract)
    ACT(aneg[:], bcv(BC_ALOG, 16), AF.Exp)
    TS("dve", aneg[:], aneg[:], -1.0, ALU.mult)
    TS("dve", negsink[:], bcv(BC_SINK, 16), -1.0, ALU.mult)
    for h_ in range(16):
        TS("dve", identD[:, h_, :], ident_f[:], bcv(BC_DSK + h_, 1), ALU.mult)
    for g in range(8):
        bk = nextbank()
        TR([(bk[:, 0:128], gmw_ld[:, g, :], ident_f[:])], writes=[bk[:, 0:128]])
        TT("dve", gmWT[:, g, :], bk[:, 0:128], triu_f[:], ALU.mult)

    order = ["h0", "h1", "h2", "h3", "full"]
    lvl = order.index(stage)
    NBU = {0: B_UP0, 1: B_QKV, 2: B_UP1, 3: NB, 4: NB}[lvl]
    total_w = NT * NBU
    ws = {"rec": 0, "sc": 0, "cast": 0}
    ch_ring = [P.chan() for _ in range(NW)]
    ch_ring_sw = [P.chan() for _ in range(NW)]
    ch_wst = [P.chan() for _ in range(NW)]
    cast_engs = ["act", "dve"]
    NQ = 4

    def wrec(j):
        tt, bb = divmod(j, NBU)
        slot = j % NW
        src_, kc, cols = blocks[bb]
        n = kc * cols
        dst = wring[slot][:, 0:n]
        if tt == 0:
            DMA(dst.rearrange("p (k c) -> p k c", c=cols), src_, ch_ring_sw[slot], eng="pool", issue=1.5)
            if B_WOUT <= bb < B_WOUT + 4:
                dv = dst.rearrange("p (k c) -> p k c", c=cols)
                for j in range(8):
                    TS("dve", dv[:, 8 + j, :], dv[:, 8 + j, :], ppc(PP_NG + j), ALU.mult)
            if NT > 1:
                DMA(wscr[bb, :, 0:n], dst, ch_wst[slot])
        else:
            DMA(dst, wscr[bb, :, 0:n], ch_ring[slot])

    def wget(t, b):
        idx = t * NBU + b
        lim = min(idx + NW, total_w)
        while ws["rec"] < lim:
            wrec(ws["rec"])
            ws["rec"] += 1
        _, kc, cols = blocks[b]
        return wring[idx % NW][:, 0:kc * cols].rearrange("p (k c) -> p k c", c=cols)

    ch_x = [P.chan() for _ in range(2)]
    ch_o = [P.chan() for _ in range(2)]
    ch_dbg = P.chan()
    sm = {"i": 0}

    def smalls(n):
        i = sm["i"]
        sm["i"] = (i + 1) % 16
        return small[:, i, 0:n]

    def load_x(t):
        for c in range(4):
            xs_ = xin[c % 2]
            DMA(xs_[:], x_d[t * 512 + c * 128:t * 512 + (c + 1) * 128, :], ch_x[c % 2])
            pr = pairs[c % 2]
            TR([(pr[:, j * 128:(j + 1) * 128], xs_[:, j * 128:(j + 1) * 128], ident_f[:]) for j in range(8)], writes=[pr[:]])
            cs = slice(c * 128, (c + 1) * 128)
            COPY("act", hT[:, 0:4, cs], pr[:, 0:512].rearrange("p (j t) -> p j t", t=128))
            COPY("dve", hT[:, 4:8, cs], pr[:, 512:1024].rearrange("p (j t) -> p j t", t=128))

    nrm = {"i": 0}

    def rmsnorm(gbase):
        bk = nextbank()
        for j in range(8):
            if j % 2 == 0:
                ACT(nsq[:, j, :], hT[:, j, :], AF.Square)
                TS("dve", yT[:, j, :], hT[:, j, :], ppc(gbase + j), ALU.mult)
            else:
                TT("dve", nsq[:, j, :], hT[:, j, :], hT[:, j, :], ALU.mult)
                ACT(yT[:, j, :], hT[:, j, :], AF.Copy, scale=ppc(gbase + j))
        MMG(bk, [(ones_bf[:], nsq[:, j, :]) for j in range(8)])
        ACT(bk, bk, AF.Ln, bias=EPS, scale=1.0 / 1024)
        ACT(bk, bk, AF.Exp, scale=-0.5)
        ACT(rstd_sb, bk, AF.Copy)

        def fin():
            for j in range(8):
                STT(yT[:, j, :], hT[:, j, :], ppc(gbase + j), bk, ALU.mult, ALU.mult)
        return fin

    def rescaled(bk):
        tmp = tmpn[nrm["i"] % 2]
        nrm["i"] += 1
        TT("dve", tmp, bk, rstd_sb, ALU.mult)
        return tmp

    def dump_or_final(t, final):
        for c in range(4):
            cs = slice(c * 128, (c + 1) * 128)
            pr = pairs[2 + c % 2]
            TR([(pr[:, j * 128:(j + 1) * 128], hT[:, j, cs], ident_f[:]) for j in range(8)], writes=[pr[:]])
            ob = osb[c % 2]
            if not final:
                COPY("act", ob[:, 0:512], pr[:, 0:512])
                COPY("dve", ob[:, 512:1024], pr[:, 512:1024])
            else:
                ss = smalls(4)
                ACT(junk[:], pr[:, 0:512], AF.Square, accum=ss[:, 0:1])
                ACT(junk[:], pr[:, 512:1024], AF.Square, accum=ss[:, 1:2])
                TT("dve", ss[:, 2:3], ss[:, 0:1], ss[:, 1:2], ALU.add)
                ACT(ss[:, 3:4], ss[:, 2:3], AF.Ln, bias=EPS, scale=1.0 / 1024)
                ACT(ss[:, 2:3], ss[:, 3:4], AF.Exp, scale=-0.5)
                STT(ob[:, 0:512], pr[:, 0:512], ss[:, 2:3], bcv(BC_GF, 512), ALU.mult, ALU.mult)
                STT(ob[:, 512:1024], pr[:, 512:1024], ss[:, 2:3], bcv(BC_GF + 512, 512), ALU.mult, ALU.mult)
            DMA(out_d[t * 512 + c * 128:t * 512 + (c + 1) * 128, :], ob[:], ch_o[c % 2])

    def layer0_mixer(t):
        fin = rmsnorm(PP_GMIX0)
        P.tag = 'l0.u'
        for bi in range(2):
            wv = wget(t, B_WIN + bi)
            bks = None
            if bi == 0:
                bks = kouter4(wv)
                fin()
            for o in range(4):
                oc = bi * 4 + o
                if bks is None:
                    bk = nextbank()
                    MMG(bk, [(wv[:, k, o * 128:(o + 1) * 128], yT[:, k, :]) for k in range(8)])
                else:
                    bk = rescaled(bks[o])
                ACT(guT[:, oc, :], bk, AF.Gelu_apprx_tanh)
        P.tag = 'l0.v'
        for bi in range(2):
            wv = wget(t, B_WIN + 2 + bi)
            for tc in range(4):
                bk = nextbank()
                MMG(bk, [(yT[:, k, tc * 128:(tc + 1) * 128], wv[:, k, :]) for k in range(8)])
                ACT(gv[:, tc, bi * 512:(bi + 1) * 512], bk, AF.Gelu_apprx_tanh)
                if bi == 1:
                    layernorm_chunk(tc)
        P.tag = 'l0.z'
        for bi in range(2):
            wv = wget(t, B_WIN + 4 + bi)
            for tc in range(4):
                bk = nextbank()
                MMG(bk, [(yT[:, k, tc * 128:(tc + 1) * 128], wv[:, k, :]) for k in range(8)])
                ACT(sz[:, tc, bi * 512:(bi + 1) * 512], bk, AF.Silu)
        bkd = nextbank()
        PE([(bkd[:, tc * 16:(tc + 1) * 16], yT[:, k, tc * 128:(tc + 1) * 128], wdt[:, k, :], k == 0, k == 7)
            for tc in range(4) for k in range(8)], writes=[bkd[:, 0:64]])
        dtx = smalls(64)
        dtv = dtx.rearrange("p (c h) -> p c h", h=16)
        TT("dve", dtv, bkd[:, 0:64].rearrange("p (c h) -> p c h", h=16),
           bcv(BC_DTB, 16).unsqueeze(1).to_broadcast([128, 4, 16]), ALU.add)
        ex = smalls(64)
        ACT(ex, dtx, AF.Exp)
        dt_all = smalls(64)
        ACT(dt_all, ex, AF.Ln, bias=1.0)
        dta_all = smalls(64)
        TT("dve", dta_all.rearrange("p (c h) -> p c h", h=16), dt_all.rearrange("p (c h) -> p c h", h=16),
           aneg[:].unsqueeze(1).to_broadcast([128, 4, 16]), ALU.mult)
        bka = nextbank()
        PE([(bka[:, tc * 16:(tc + 1) * 16], triu_f[:], dta_all[:, tc * 16:(tc + 1) * 16], True, True) for tc in range(4)]
           + [(bka[:, 64 + tc * 16:64 + (tc + 1) * 16], ones_f[:], dta_all[:, tc * 16:(tc + 1) * 16], True, True) for tc in range(4)],
           writes=[bka[:, 0:128]])
        acum = smalls(64)
        COPY("dve", acum, bka[:, 0:64])
        ea_all = smalls(64)
        ACT(ea_all, bka[:, 0:64], AF.Exp)
        cd_all = smalls(64)
        ACT(cd_all, bka[:, 64:128], AF.Exp)
        te_all = smalls(64)
        TT("dve", te_all, bka[:, 64:128], acum, ALU.subtract)
        ACT(te_all, te_all, AF.Exp)
        TT("dve", te_all, te_all, dt_all, ALU.mult)
        P.tag = 'l0.conv'
        wv_x = {}

        def conv_c1(oc):
            bi, o = divmod(oc, 4)
            if o == 0:
                wv_x[bi] = wget(t, B_WIN + 6 + bi)
            wv = wv_x[bi]
            bk = nextbank()
            MMG(bk, [(wv[:, k, o * 128:(o + 1) * 128], yT[:, k, :]) for k in range(8)])
            xs_ = xst[oc % 2]
            COPY("pool", xs_[:, 0:3], tails[:, oc, :])
            ACT(xs_[:, 3:515], bk, AF.Copy)
            ACT(acc[oc % 2], bk, AF.Identity, bias=ppc(PP_CONVB + oc), scale=ppc(PP_CONVW + 3 * 16 + oc))

        def conv_c2(oc):
            xs_ = xst[oc % 2]
            ac = acc[oc % 2]
            for tap in (2, 1, 0):
                STT(ac, xs_[:, tap:tap + 512], ppc(PP_CONVW + tap * 16 + oc), ac, ALU.mult, ALU.add)
            COPY("pool", tails[:, oc, :], xs_[:, 512:515])
            ACT(xcT[:, oc, :], ac, AF.Silu)

        conv_c1(0)
        for oc in range(16):
            if oc + 1 < 16:
                conv_c1(oc + 1)
            conv_c2(oc)

        P.tag = 'l0.gmlp'
        for g in range(8):
            bk = nextbank()
            mms = []
            for tc in range(4):
                o = bk[:, tc * 128:(tc + 1) * 128]
                mms.append((o, vn[:, tc, g * 128:(g + 1) * 128], gmWT[:, g, :], True, False))
                mms.append((o, ones_bf[0:1, :], bs_hi[0:1, g * 128:(g + 1) * 128], False, False))
                mms.append((o, ones_bf[0:1, :], bs_lo[0:1, g * 128:(g + 1) * 128], False, True))
            PE(mms, writes=[bk])
            TT("dve", mixT[:, g, :], bk, guT[:, g, :], ALU.mult)
        P.tag = 'l0.ssd'
        hs = lambda a_: a_.rearrange("p (h d) -> p h d", d=64)

        def bc16(ap16):
            return ap16.unsqueeze(2).to_broadcast([128, 16, 64])

        def ssd_s1(tc):
            i = tc % 2
            cs = slice(tc * 128, (tc + 1) * 128)
            b0 = bank16(0)
            TR([(b0[:, j * 128:(j + 1) * 128], xcT[:, j, cs], ident_bf[:]) for j in range(8)], writes=[b0])
            COPY("act", xtok[i], b0)
            b1 = bank16(1)
            TR([(b1[:, g * 128:(g + 1) * 128], xcT[:, 8 + g, cs], ident_bf[:]) for g in range(4)], writes=[b1[:, 0:512]])
            COPY("act", Btok[i], b1[:, 0:512])
            TT("dve", hs(xw[i]), hs(xtok[i]), bc16(te_all[:, tc * 16:(tc + 1) * 16]), ALU.mult)
            b2 = bank(2)
            PE([(b2[:, g * 128:(g + 1) * 128], xcT[:, 8 + g, cs], xcT[:, 12 + g, cs], True, True) for g in range(4)], writes=[b2])
            TT("dve", cbm, b2.rearrange("p (g l) -> p g l", l=128), triu_f[:].unsqueeze(1).to_broadcast([128, 4, 128]), ALU.mult)

            def rbuild(g):
                rb = Rb[g % 2]
                for hh in range(4):
                    h = 4 * g + hh
                    sc_ = dta_all[:, tc * 16 + h:tc * 16 + h + 1]
                    if hh == 3:
                        ACT(rb[:, hh, :], triu_f[:], AF.Copy, scale=sc_)
                    else:
                        TS("dve", rb[:, hh, :], triu_f[:], sc_, ALU.mult)
            rbuild(0)
            rbuild(1)
            for g in range(4):
                rb = Rb[g % 2]
                sb_ = bank(3) if g % 2 == 0 else bank(2)
                PE([(sb_, slow_f[:], rb.rearrange("p h l -> p (h l)"), True, True)], writes=[sb_])
                ACT(sb_, sb_, AF.Exp)
                for hh in range(4):
                    h = 4 * g + hh
                    STT(wT[i][:, h, :], sb_[:, hh * 128:(hh + 1) * 128], dt_all[:, tc * 16 + h:tc * 16 + h + 1], cbm[:, g, :], ALU.mult, ALU.mult)
                if g + 2 < 4:
                    rbuild(g + 2)

        def ssd_s2(tc):
            i = tc % 2
            cs = slice(tc * 128, (tc + 1) * 128)
            Y = pairs[2]
            mms = []
            for h in range(16):
                o = Y[:, h * 64:(h + 1) * 64]
                mms.append((o, wT[i][:, h, :], xtok[i][:, h * 64:(h + 1) * 64], True, False))
                mms.append((o, identD[:, h, :], xtok[i][:, h * 64:(h + 1) * 64], False, True))
            PE(mms, writes=[Y[:]])
            Z = pairs[3]
            PE([(Z[:, g * 256:(g + 1) * 256], xcT[:, 12 + g, cs], prev_bf[:, g * 256:(g + 1) * 256], True, True) for g in range(4)], writes=[Z[:]])
            TT("dve", hs(t1), hs(Z[:]), bc16(ea_all[:, tc * 16:(tc + 1) * 16]), ALU.mult)
            TT("dve", yv, Y[:], t1, ALU.add)
            TT("dve", yv, yv, sz[:, tc, :], ALU.mult)
            ss4 = smalls(16)
            for g in range(4):
                ACT(junk[:, 0:256], yv[:, g * 256:(g + 1) * 256], AF.Square, accum=ss4[:, g:g + 1])
            ACT(ss4[:, 4:8], ss4[:, 0:4], AF.Ln, bias=EPS, scale=1.0 / 256)
            ACT(ss4[:, 8:12], ss4[:, 4:8], AF.Exp, scale=-0.5)
            for g in range(4):
                ACT(bout[i][:, g * 256:(g + 1) * 256], yv[:, g * 256:(g + 1) * 256], AF.Copy, scale=ss4[:, 8 + g:9 + g])

        def ssd_s3(tc):
            i = tc % 2
            cs = slice(tc * 128, (tc + 1) * 128)
            Sx = pairs[3]
            PE([(Sx[:, g * 256:(g + 1) * 256], Btok[i][:, g * 128:(g + 1) * 128], xw[i][:, g * 256:(g + 1) * 256], True, True) for g in range(4)], writes=[Sx[:]])
            TT("dve", hs(prevT[:]), hs(prevT[:]), bc16(cd_all[:, tc * 16:(tc + 1) * 16]), ALU.mult)
            TT("dve", prevT[:], Sx[:], prevT[:], ALU.add)
            COPY("act", prev_bf[:], prevT[:])
            b0 = bank16(0)
            TR([(b0[:, j * 128:(j + 1) * 128], bout[i][:, j * 128:(j + 1) * 128], ident_bf[:]) for j in range(8)], writes=[b0])
            COPY("act", mixT[:, 8:16, cs], b0.rearrange("p (j t) -> p j t", t=128))

        ssd_s1(0)
        for tc in range(4):
            if tc + 1 < 4:
                ssd_s1(tc + 1)
            ssd_s2(tc)
            ssd_s3(tc)
        if dbg_d is not None:
            DMA(dbg_d[t].rearrange("j p t -> p j t"), mixT, ch_dbg)
        P.tag = 'l0.wout'
        for i in range(4):
            wv = wget(t, B_WOUT + i)
            for o in range(2):
                oc = 2 * i + o
                bk = nextbank()
                MMG(bk, [(wv[:, k, o * 128:(o + 1) * 128], mixT[:, k, :]) for k in range(16)])
                TT("dve", hT[:, oc, :], bk, hT[:, oc, :], ALU.add)

    def layernorm_chunk(tc):
        g_ = gv[:, tc, :]
        st6 = smalls(12)
        P.op("dve", lambda e: e.bn_stats(out=st6[:, 0:6], in_=g_[:, 0:512]), reads=[g_[:, 0:512]], writes=[st6[:, 0:6]], dur=0.65)
        P.op("dve", lambda e: e.bn_stats(out=st6[:, 6:12], in_=g_[:, 512:1024]), reads=[g_[:, 512:1024]], writes=[st6[:, 6:12]], dur=0.65)
        mv = smalls(4)
        P.op("dve", lambda e: e.bn_aggr(out=mv[:, 0:2], in_=st6), reads=[st6], writes=[mv[:, 0:2]], dur=0.15)
        ACT(mv[:, 2:3], mv[:, 1:2], AF.Sqrt, bias=EPS)
        P.op("dve", lambda e: e.reciprocal(out=mv[:, 3:4], in_=mv[:, 2:3]), reads=[mv[:, 2:3]], writes=[mv[:, 3:4]], dur=0.17)
        TS("dve", g_, g_, mv[:, 0:1], ALU.subtract, mv[:, 3:4], ALU.mult)
        TT("dve", g_, g_, bcv(BC_LNG, 1024), ALU.mult)
        TT("dve", vn[:, tc, :], g_, bcv(BC_LNB, 1024), ALU.add)

    def mlp(t, l):
        fin = rmsnorm(PP_GMLP0 if l == 0 else PP_GMLP1)
        bu = B_UP0 if l == 0 else B_UP1
        bd = B_DN0 if l == 0 else B_DN1
        for i in range(8):
            wv = wget(t, bu + i)
            bks = None
            if i == 0:
                bks = kouter4(wv)
                fin()
            for o in range(4):
                f = 4 * i + o
                if bks is None:
                    bk = nextbank()
                    MMG(bk, [(wv[:, k, o * 128:(o + 1) * 128], yT[:, k, :]) for k in range(8)])
                else:
                    bk = rescaled(bks[o])
                r = r32[f % 2]
                ACT(r, bk, AF.Relu)
                TT("dve", hid[:, f, :], r, r, ALU.mult)
        for i in range(8):
            wv = wget(t, bd + i)
            bk = nextbank()
            MMG(bk, [(wv[:, k, :], hid[:, k, :]) for k in range(32)])
            TT("dve", hT[:, i, :], bk, hT[:, i, :], ALU.add)

    def layer1_mixer(t):
        fin = rmsnorm(PP_GMIX1)
        COPY("pool", kT[:, :, 0:128], kT[:, :, 512:640])
        COPY("pool", vtok[:, 0, :], vtok[:, 4, :])
        for bi in range(2):
            wv = wget(t, B_QKV + bi)
            bks = None
            if bi == 0:
                bks = kouter4(wv)
                fin()
            for o in range(4):
                hp = bi * 4 + o
                if bks is None:
                    bk = nextbank()
                    MMG(bk, [(wv[:, k, o * 128:(o + 1) * 128], yT[:, k, :]) for k in range(8)])
                else:
                    bk = rescaled(bks[o])
                ACT(qT[:, hp, :], bk, AF.Identity, bias=ppc(PP_BQ + hp))
        wv = wget(t, B_QKV + 2)
        for kv in range(2):
            bk = nextbank()
            MMG(bk, [(wv[:, k, kv * 128:(kv + 1) * 128], yT[:, k, :]) for k in range(8)])
            ACT(kT[:, kv, 128:640], bk, AF.Identity, bias=ppc(PP_BK + kv))
        bk = nextbank()
        PE([(bk[:, tc * 128:(tc + 1) * 128], yT[:, k, tc * 128:(tc + 1) * 128], wv[:, k, 256:384], k == 0, k == 7)
            for tc in range(4) for k in range(8)], writes=[bk])
        TT("dve", vtok[:, 1:5, :], bk.rearrange("p (c d) -> p c d", d=128),
           bcv(BC_BV, 128).unsqueeze(1).to_broadcast([128, 4, 128]), ALU.add)
        P.tag = 'l1.attn'
        items = [(tc, hp) for tc in range(4) for hp in range(8)]
        sls = {}

        def sc_pair(n):
            return pairs[0] if n % 2 == 0 else pairs[3]

        def att_a1(n):
            tc, hp = items[n]
            cs = slice(tc * 128, (tc + 1) * 128)
            first = (t == 0 and tc == 0)
            mk_ = maskb0 if first else maskb
            kv = hp // 4
            pr = sc_pair(n)
            mms = []
            for e_ in range(2):
                mms.append((pr[:, e_ * 512:e_ * 512 + 256], qT[64 * e_:64 * e_ + 64, hp, cs],
                            kT[64 * e_:64 * e_ + 64, kv, tc * 128:tc * 128 + 256], True, False))
            for e_ in range(2):
                mms.append((pr[:, e_ * 512:e_ * 512 + 256], ident_bf[:], mk_[:, e_, :], False, True))
            PE(mms, writes=[pr[:]])
            sls[n] = smalls(16)

        def att_b125(n):
            tc, hp = items[n]
            sl = sls[n]
            pr = sc_pair(n)
            sc_v = pr[:].rearrange("p (e s) -> p e s", s=512)[:, :, 0:256]
            P.op("dve", (lambda o_, i_: lambda e: e.tensor_reduce(out=o_, in_=i_, axis=AX.X, op=ALU.max))(sl[:, 0:2], sc_v),
                 reads=[pr[:]], writes=[sl[:, 0:2]], dur=0.67)
            STT(sl[:, 2:4], sl[:, 0:2], -0.125, negsink[:, 2 * hp:2 * hp + 2], ALU.mult, ALU.min)
            TT("dve", sl[:, 6:8], bcv(BC_SINK + 2 * hp, 2), sl[:, 2:4], ALU.add)

        def att_c35(n):
            tc, hp = items[n]
            sl = sls[n]
            pr = sc_pair(n)
            pb = Pb[n % 2]
            for e_ in range(2):
                ACT(pb[:, e_, :], pr[:, e_ * 512:e_ * 512 + 256], AF.Exp, bias=sl[:, 2 + e_:3 + e_], scale=0.125,
                    accum=sl[:, 4 + e_:5 + e_])
            ACT(sl[:, 8:10], sl[:, 6:8], AF.Exp)

        def att_a4(n):
            pb = Pb[n % 2]
            ptb = bank16(2 + n % 2)
            TR([(ptb[:, (2 * e_ + hf) * 128:(2 * e_ + hf + 1) * 128], pb[:, e_, hf * 128:(hf + 1) * 128], ident_bf[:])
                for e_ in range(2) for hf in range(2)], writes=[ptb[:, 0:512]])

        def att_copy(n):
            ptb = bank16(2 + n % 2)
            COPY("dve", PT[n % 2], ptb[:, 0:512].rearrange("p (a b) -> p a b", b=128))

        def att_b67(n):
            sl = sls[n]
            TT("dve", sl[:, 10:12], sl[:, 8:10], sl[:, 4:6], ALU.add)
            P.op("dve", (lambda o_, i_: lambda e: e.reciprocal(out=o_, in_=i_))(sl[:, 12:14], sl[:, 10:12]),
                 reads=[sl[:, 10:12]], writes=[sl[:, 12:14]], dur=0.17)

        def att_a7(n):
            tc, hp = items[n]
            kv = hp // 4
            pt = PT[n % 2]
            O = pairs[2]
            mms = []
            for e_ in range(2):
                h = 2 * hp + e_
                o = O[:, h * 64:(h + 1) * 64]
                for hf in range(2):
                    mms.append((o, pt[:, 2 * e_ + hf, :], vtok[:, tc + hf, kv * 64:(kv + 1) * 64], hf == 0, hf == 1))
            PE(mms, writes=[O[:, 2 * hp * 64:(2 * hp + 2) * 64]])

        def att_b8(n):
            tc, hp = items[n]
            cs = slice(tc * 128, (tc + 1) * 128)
            sl = sls[n]
            O = pairs[2]
            at = attn[tc % 2]
            TT("dve", at[:, 2 * hp * 64:(2 * hp + 2) * 64].rearrange("p (e d) -> p e d", d=64),
               O[:, 2 * hp * 64:(2 * hp + 2) * 64].rearrange("p (e d) -> p e d", d=64),
               sl[:, 12:14].unsqueeze(2).to_broadcast([128, 2, 64]), ALU.mult)
            if hp == 7:
                b6 = bank16(3)
                TR([(b6[:, j * 128:(j + 1) * 128], at[:, j * 128:(j + 1) * 128], ident_bf[:]) for j in range(8)], writes=[b6])
                COPY("act", aoT[:, :, cs], b6.rearrange("p (j t) -> p j t", t=128))

        NI = len(items)
        for k in range(NI + 2):
            if k < NI:
                att_a1(k)
            if 0 <= k - 2 < NI:
                att_a7(k - 2)
            if 0 <= k - 1 < NI:
                att_a4(k - 1)
            if k < NI:
                att_b125(k)
                att_c35(k)
            if 0 <= k - 1 < NI:
                att_b67(k - 1)
            if 0 <= k - 2 < NI:
                att_b8(k - 2)
            if 0 <= k - 1 < NI:
                att_copy(k - 1)
        if dbg_d is not None and stage == "h2":
            DMA(dbg_d[t, 0:8].rearrange("j p t -> p j t"), aoT, ch_dbg)
        P.tag = 'l1.wo'
        for bi in range(2):
            wv = wget(t, B_WO + bi)
            for o in range(4):
                oc = bi * 4 + o
                bk = nextbank()
                MMG(bk, [(wv[:, k, o * 128:(o + 1) * 128], aoT[:, k, :]) for k in range(8)])
                STT(hT[:, oc, :], bk, ppc(PP_BO + oc), hT[:, oc, :], ALU.add, ALU.add)

    for t in range(NT):
        hT = hTs[0]
        P.tag = "load_x"
        load_x(t)
        P.tag = "l0"
        layer0_mixer(t)
        if lvl >= 1:
            P.tag = "mlp0"
            mlp(t, 0)
        if lvl >= 2:
            P.tag = "l1"
            layer1_mixer(t)
        if lvl >= 3:
            P.tag = "mlp1"
            mlp(t, 1)
        P.tag = "final"
        dump_or_final(t, final=(lvl == 4))

    P.sbuf_left = nc.sbuf_bytes_remaining
    P.emit(nc, st)
    st.close()
    return nc, P


def _host_inputs(inp):
    f = lambda k: np.ascontiguousarray(np.asarray(inp[k], dtype=np.float32))
    col = lambda v: np.ascontiguousarray(v.reshape(-1, 128).T)
    bq = f("b_qkv")[0]
    k0, k1 = bq[1024:1088], bq[1088:1152]
    pp = np.concatenate([
        col(f("norm_mix_g")[0]), col(f("norm_mlp_g")[0]), col(f("norm_mix_g")[1]), col(f("norm_mlp_g")[1]),
        np.concatenate([col(f("ssm_conv_w")[0][i]) for i in range(4)], axis=1),
        col(f("ssm_conv_b")[0]),
        col(bq[0:1024]),
        np.stack([np.concatenate([k0, k0]), np.concatenate([k1, k1])], axis=1),
        col(f("b_o")[0]),
        col(f("ssm_norm_g")[0]),
    ], axis=1)
    assert pp.shape == (128, NPP), pp.shape
    bc = np.concatenate([
        f("gm_ln_g")[0], f("gm_ln_b")[0], f("final_norm_g"),
        bq[1152:1280], f("ssm_dt_bias")[0], f("ssm_a_log")[0], f("ssm_d")[0], f("attn_sinks")[0],
    ])[None, :]
    assert bc.shape == (1, NBC), bc.shape
    wq = f("w_qkv")[0]
    wq_ext = np.concatenate([wq[:, 0:1024], wq[:, 1024:1088], wq[:, 1024:1088], wq[:, 1088:1152], wq[:, 1088:1152],
                             wq[:, 1152:1280]], axis=1)
    shared = {
        "w_in": f("w_in_even")[0], "w_out": f("w_out_even")[0], "w_qkv": np.ascontiguousarray(wq_ext),
        "w_o": f("w_o")[0], "w_up": f("w_up"), "w_down": f("w_down"),
        "pp": np.ascontiguousarray(pp), "bc": np.ascontiguousarray(bc),
        "gmw": f("gm_w_s")[0], "gmb": f("gm_b_s")[0].reshape(1, 1024),
    }
    return shared


def kernel(**inputs):
    x = np.asarray(inputs["x"], dtype=np.float32)
    shared = _host_inputs(inputs)
    nc, _ = build(NT=8, stage="full")
    in_maps = []
    for b in range(8):
        m = dict(shared)
        m["x"] = np.ascontiguousarray(x[b])
        in_maps.append(m)
    res = run_bass_kernel_spmd(nc, in_maps, core_ids=list(range(8)))
    return np.stack([np.asarray(res.results[b]["out"], dtype=np.float32) for b in range(8)], axis=0)
```

```python
import numpy as np
from contextlib import ExitStack
import concourse.bass as bass
import concourse.mybir as mybir
from concourse.bass_utils import run_bass_kernel_spmd

F32 = mybir.dt.float32
BF16 = mybir.dt.bfloat16
AF = mybir.ActivationFunctionType
ALU = mybir.AluOpType
AX = mybir.AxisListType

ENGS = ["pe", "act", "dve", "pool", "sp"]
_ESZ = {F32: 4, BF16: 2}
_PAGE = {"SB": 256, "PSUM": 2048, "DRAM": 1 << 18}


class Buf:
    __slots__ = ("w", "r")

    def __init__(self):
        self.w = None
        self.r = []


class Chan:
    def __init__(self, key):
        self.key = key
        self.last = None
        self.count = 0


def _space(ap):
    s = str(ap.space)
    if "PSUM" in s:
        return "PSUM"
    if "DRAM" in s or "HBM" in s:
        return "DRAM"
    return "SB"


class Prog:
    SAME_ENGINE_SYNC = True
    SCHEDULE = True
    FUSE_WAIT = True
    CPW = 20.0
    LAT = 0.25

    def __init__(self):
        self.recs = []
        self.tags = []
        self.single = []
        self.tag = 'setup'
        self.chans = []
        self.bufs = {}
        self.nops = 0
        self.ops = {e: [] for e in ENGS}

    def chan(self):
        c = Chan("ch%d" % len(self.chans))
        self.chans.append(c)
        return c

    def regs(self, ap):
        sp = _space(ap)
        esz = _ESZ[ap.dtype]
        pat = ap.ap
        off = ap.offset
        if sp == "DRAM":
            f0 = off
            dims = pat
        else:
            pstep = pat[0][0]
            f0 = off % pstep if pstep > 0 else off
            dims = pat[1:]
        ext = 0
        for s, c in dims:
            ext += abs(s) * (c - 1)
        lo = f0 * esz
        hi = (f0 + ext) * esz + esz - 1
        pg = _PAGE[sp]
        name = ap.tensor.name
        out = []
        for p in range(lo // pg, hi // pg + 1):
            k = (name, p)
            b = self.bufs.get(k)
            if b is None:
                b = self.bufs[k] = Buf()
            out.append(b)
        return out

    def op(self, eng, fn, reads=(), writes=(), chan=None, dur=0.3, xfer=0.0, single=False):
        rb = []
        wb = []
        for a in reads:
            (wb if _space(a) == "PSUM" else rb).extend(self.regs(a))
        for a in writes:
            wb.extend(self.regs(a))
        oid = len(self.recs)
        deps = set()
        for b in rb:
            if b.w is not None:
                deps.add(b.w)
        for b in wb:
            if b.w is not None:
                deps.add(b.w)
            deps.update(b.r)
        if chan is not None:
            if chan.last is not None:
                deps.add(chan.last)
            chan.last = oid
        deps.discard(oid)
        for b in rb:
            b.r.append(oid)
        for b in wb:
            b.w = oid
            b.r = []
        self.recs.append((eng, fn, deps, chan, dur, xfer))
        self.tags.append(self.tag)
        self.single.append(single)
        self.ops[eng].append(oid)
        self.nops += 1
        return oid

    def schedule(self):
        import heapq
        recs = self.recs
        n = len(recs)
        if not self.SCHEDULE:
            order = {e: list(self.ops[e]) for e in ENGS}
            return order, 0.0
        succ = [[] for _ in range(n)]
        indeg = [0] * n
        for i, r in enumerate(recs):
            indeg[i] = len(r[2])
            for d in r[2]:
                succ[d].append(i)
        blev = [0.0] * n
        for i in range(n - 1, -1, -1):
            m_ = 0.0
            for s_ in succ[i]:
                if blev[s_] > m_:
                    m_ = blev[s_]
            blev[i] = m_ + recs[i][4] + recs[i][5] + 0.2
        prio = [(i - self.CPW * blev[i], i) for i in range(n)]
        ready_t = [0.0] * n
        pend = {e: [] for e in ENGS}
        avail = {e: [] for e in ENGS}
        free = {e: 0.0 for e in ENGS}
        order = {e: [] for e in ENGS}
        for i in range(n):
            if indeg[i] == 0:
                heapq.heappush(pend[recs[i][0]], (0.0, i))
        done = 0
        tend = 0.0
        while done < n:
            best = None
            for e in ENGS:
                pe_, av = pend[e], avail[e]
                while pe_ and pe_[0][0] <= free[e]:
                    heapq.heappush(av, prio[heapq.heappop(pe_)[1]])
                if av:
                    cand = (free[e], av[0][1], e, True)
                elif pe_:
                    cand = (pe_[0][0], pe_[0][1], e, False)
                else:
                    continue
                if best is None or cand[:2] < best[:2]:
                    best = cand
            start, i, e, from_av = best
            if from_av:
                heapq.heappop(avail[e])
            else:
                heapq.heappop(pend[e])
            eng, fn, deps, chan, dur, xfer = recs[i]
            end_eng = start + dur
            end = end_eng + xfer
            free[e] = end_eng
            order[e].append(i)
            tend = max(tend, end)
            done += 1
            for s_ in succ[i]:
                lat = self.LAT if (recs[s_][0] != e or chan is not None) else 0.05
                t_ = end + lat
                if t_ > ready_t[s_]:
                    ready_t[s_] = t_
                indeg[s_] -= 1
                if indeg[s_] == 0:
                    heapq.heappush(pend[recs[s_][0]], (ready_t[s_], s_))
        return order, tend

    def simulate(self, order):
        recs = self.recs
        endt = [None] * len(recs)
        pos = {e: 0 for e in ENGS}
        free = {e: 0.0 for e in ENGS}
        busy = {e: 0.0 for e in ENGS}
        remaining = sum(len(order[e]) for e in ENGS)
        tend = 0.0
        while remaining:
            progressed = False
            for e in ENGS:
                while pos[e] < len(order[e]):
                    i = order[e][pos[e]]
                    eng, fn, deps, chan, dur, xfer = recs[i]
                    t = free[e]
                    ok = True
                    for d in deps:
                        if endt[d] is None:
                            ok = False
                            break
                        lat = self.LAT if (recs[d][0] != e or recs[d][3] is not None) else 0.05
                        t = max(t, endt[d] + lat)
                    if not ok:
                        break
                    free[e] = t + dur
                    busy[e] += dur
                    endt[i] = t + dur + xfer
                    tend = max(tend, endt[i])
                    pos[e] += 1
                    remaining -= 1
                    progressed = True
            assert progressed, "deadlock in simulated order"
        return tend, busy

    def emit(self, nc, st):
        order, tend = self.schedule()
        self.sim_us, self.busy_us = self.simulate(order)
        self.predicted_us = tend
        recs = self.recs
        tok = [None] * len(recs)
        cnt = {e: 0 for e in ENGS}
        ccnt = {c.key: 0 for c in self.chans}
        for i, r in enumerate(recs):
            if r[3] is not None:
                ccnt[r[3].key] += 1
                tok[i] = (r[3].key, ccnt[r[3].key] * 16)
        for e in ENGS:
            for i in order[e]:
                if recs[i][3] is None:
                    cnt[e] += 1
                    tok[i] = (e, cnt[e])
        sems = {}
        for e in ENGS:
            sems[e] = st.enter_context(nc.semaphore("s_" + e))
        for c in self.chans:
            sems[c.key] = st.enter_context(nc.semaphore("s_" + c.key))
        block = st.enter_context(nc.Block())
        finals = {}
        for e in ENGS:
            if cnt[e] > 0:
                finals[e] = cnt[e]
        for c in self.chans:
            if ccnt[c.key] > 0:
                finals[c.key] = ccnt[c.key] * 16

        def mk(ename):
            def body(eng):
                known = {}
                for i in order[ename]:
                    e_, fn, deps, chan, dur, xfer = recs[i]
                    waits = {}
                    for d in deps:
                        key, val = tok[d]
                        if key == ename and chan is None and recs[d][3] is None:
                            if ename == "pe" or not self.SAME_ENGINE_SYNC:
                                continue
                        if known.get(key, 0) >= val:
                            continue
                        if waits.get(key, 0) < val:
                            waits[key] = val
                    wl = sorted(waits.items())
                    for k, v in wl:
                        known[k] = v
                    fused = wl.pop() if (wl and self.single[i] and self.FUSE_WAIT) else None
                    for k, v in wl:
                        eng.wait_ge(sems[k], v)
                    ins = fn(eng)
                    if fused is not None:
                        ins._wait_ge(sems[fused[0]], fused[1])
                    if chan is not None:
                        ins.then_inc(sems[chan.key], 16)
                    else:
                        ins.then_inc(sems[ename], 1)
                if ename == "sp":
                    for k, v in finals.items():
                        eng.wait_ge(sems[k], v)
            return body

        block.tensor(mk("pe"))
        block.scalar(mk("act"))
        block.vector(mk("dve"))
        block.gpsimd(mk("pool"))
        block.sync(mk("sp"))


def _isap(x):
    return hasattr(x, "tensor") and hasattr(x, "ap")


PP_GMIX0, PP_GMLP0, PP_GMIX1, PP_GMLP1 = 0, 8, 16, 24
PP_CONVW = 32
PP_CONVB = 96
PP_BQ = 112
PP_BK = 120
PP_BO = 122
PP_NG = 130
NPP = 138
BC_LNG, BC_LNB, BC_GF = 0, 1024, 2048
BC_BV = 3072
BC_DTB, BC_ALOG, BC_DSK, BC_SINK = 3200, 3216, 3232, 3248
NBC = 3264

NB = 51
NW = 4
EPS = 1e-5


def build(NT=8, stage="full"):
    nc = bass.Bass("TRN2", target_bir_lowering=False)
    st = ExitStack()
    P = Prog()

    def dram(name, shape, dt, kind):
        return nc.dram_tensor(name, shape, dt, kind=kind).ap()

    def sb(name, shape, dt):
        return st.enter_context(nc.sbuf_tensor(name, shape, dt))

    x_d = dram("x", [4096, 1024], F32, "ExternalInput")
    out_d = dram("out", [4096, 1024], F32, "ExternalOutput")
    w_in_d = dram("w_in", [1024, 5136], F32, "ExternalInput")
    w_out_d = dram("w_out", [2048, 1024], F32, "ExternalInput")
    w_qkv_d = dram("w_qkv", [1024, 1408], F32, "ExternalInput")
    w_o_d = dram("w_o", [1024, 1024], F32, "ExternalInput")
    w_up_d = dram("w_up", [2, 1024, 4096], F32, "ExternalInput")
    w_down_d = dram("w_down", [2, 4096, 1024], F32, "ExternalInput")
    pp_d = dram("pp", [128, NPP], F32, "ExternalInput")
    bc_d = dram("bc", [1, NBC], F32, "ExternalInput")
    gmw_d = dram("gmw", [8, 128, 128], F32, "ExternalInput")
    gmb_d = dram("gmb", [1, 1024], F32, "ExternalInput")
    wscr = dram("wscr", [NB, 128, 4096], BF16, "Internal")
    dbg_d = None
    if stage != "full":
        dbg_d = dram("dbg", [NT, 16, 128, 512], BF16, "ExternalOutput")

    blocks = []

    def addblk(ap2d, kc, cols):
        blocks.append((ap2d.rearrange("(kc p) c -> p kc c", p=128), kc, cols))

    for i in range(10):
        addblk(w_in_d[:, 512 * i:512 * (i + 1)], 8, 512)
    for i in range(4):
        addblk(w_out_d[:, 256 * i:256 * (i + 1)], 16, 256)
    for i in range(8):
        addblk(w_up_d[0, :, 512 * i:512 * (i + 1)], 8, 512)
    for i in range(8):
        addblk(w_down_d[0, :, 128 * i:128 * (i + 1)], 32, 128)
    addblk(w_qkv_d[:, 0:512], 8, 512)
    addblk(w_qkv_d[:, 512:1024], 8, 512)
    addblk(w_qkv_d[:, 1024:1408], 8, 384)
    for i in range(2):
        addblk(w_o_d[:, 512 * i:512 * (i + 1)], 8, 512)
    for i in range(8):
        addblk(w_up_d[1, :, 512 * i:512 * (i + 1)], 8, 512)
    for i in range(8):
        addblk(w_down_d[1, :, 128 * i:128 * (i + 1)], 32, 128)
    assert len(blocks) == NB
    B_WIN, B_WOUT, B_UP0, B_DN0, B_QKV, B_WO, B_UP1, B_DN1 = 0, 10, 14, 22, 30, 33, 35, 43

    hTs = [sb("hT0", [128, 8, 512], F32)]
    hT = hTs[0]
    yT = sb("yT", [128, 8, 512], BF16)
    wring = [sb("wring%d" % i, [128, 4096], BF16) for i in range(NW)]
    xin = [sb("xin%d" % i, [128, 1024], F32) for i in range(2)]
    osb = [sb("osb%d" % i, [128, 1024], F32) for i in range(2)]
    pp = sb("pp_sb", [128, NPP], F32)
    bc = sb("bc_sb", [128, 1, NBC], F32)
    ident_bf = sb("ident_bf", [128, 128], BF16)
    ident_f = sb("ident_f", [128, 128], F32)
    ones_bf = sb("ones_bf", [128, 128], BF16)
    ones_f = sb("ones_f", [128, 128], F32)
    triu_f = sb("triu_f", [128, 128], F32)
    slow_f = sb("slow_f", [128, 128], F32)
    maskb = sb("maskb", [128, 2, 256], BF16)
    maskb0 = sb("maskb0", [128, 2, 256], BF16)
    gmWT = sb("gmWT", [128, 8, 128], BF16)
    bs_hi = sb("bs_hi", [1, 1024], BF16)
    bs_lo = sb("bs_lo", [1, 1024], BF16)
    wdt = sb("wdt", [128, 8, 16], BF16)
    identD = sb("identD", [128, 16, 128], BF16)
    aneg = sb("aneg", [128, 16], F32)
    negsink = sb("negsink", [128, 16], F32)
    prevT = sb("prevT", [128, 1024], F32)
    prev_bf = sb("prev_bf", [128, 1024], BF16)
    tails = sb("tails", [128, 16, 3], F32)
    kT = sb("kT", [128, 2, 640], BF16)
    vtok = sb("vtok", [128, 5, 128], BF16)
    small = sb("small", [128, 16, 64], F32)
    junk = sb("junk", [128, 512], BF16)

    SCR_BYTES = 84 * 1024
    scr = sb("scr", [128, SCR_BYTES // 2], BF16)

    def sview(off, shape, dt):
        n = 1
        for s in shape:
            n *= s
        nb = n * _ESZ[dt]
        assert off % 4 == 0 and off + nb <= SCR_BYTES, (off, nb)
        v = scr[:, off // 2:(off + nb) // 2]
        if dt == F32:
            v = v.bitcast(F32)
        if len(shape) == 2:
            return v.rearrange("p (a b) -> p a b", b=shape[1])
        if len(shape) == 3:
            return v.rearrange("p (a b c) -> p a b c", b=shape[1], c=shape[2])
        return v

    K = 1024
    mixT = sview(0, [16, 512], BF16)
    xcT = sview(16 * K, [16, 512], BF16)
    sz = sview(32 * K, [4, 1024], BF16)
    xst = [sview(40 * K + i * 2304, [576], F32) for i in range(2)]
    acc = [sview(45 * K + i * 2048, [512], F32) for i in range(2)]
    guT = sview(49 * K, [8, 512], BF16)
    vn = sview(57 * K, [4, 1024], BF16)
    gv = sview(65 * K, [4, 1024], F32)
    gmw_ld = sview(0, [8, 128], F32)
    bs_f = sview(4 * K, [1024], F32)
    wdt_f = sview(8 * K, [8, 16], F32)
    maskf = sview(9 * K, [256], F32)
    yv = sview(40 * K, [1024], F32)
    t1 = sview(45 * K, [1024], F32)
    xtok = [sview(49 * K + i * 2048, [1024], BF16) for i in range(2)]
    xw = [sview(59 * K + i * 2048, [1024], BF16) for i in range(2)]
    Btok = [sview(63 * K + i * 1024, [512], BF16) for i in range(2)]
    cbm = sview(65 * K, [4, 128], F32)
    Rb = [sview(67 * K + i * 2048, [4, 128], F32) for i in range(2)]
    wT = [sview(71 * K + i * 4096, [16, 128], BF16) for i in range(2)]
    bout = [sview(79 * K + i * 2048, [1024], BF16) for i in range(2)]
    nsq = sview(66 * K, [8, 512], BF16)
    rstd_sb = sview(74 * K, [512], F32)
    tmpn = [sview(76 * K + i * 2048, [512], F32) for i in range(2)]
    hid = sview(0, [32, 512], BF16)
    r32 = [sview(32 * K + i * 2048, [512], F32) for i in range(2)]
    qT = sview(0, [8, 512], BF16)
    aoT = sview(8 * K, [8, 512], BF16)
    Pb = [sview(16 * K + i * 1024, [2, 256], BF16) for i in range(2)]
    PT = [sview(18 * K + i * 1024, [4, 128], BF16) for i in range(2)]
    attn = [sview(20 * K + i * 2048, [1024], BF16) for i in range(2)]

    pairs = [st.enter_context(nc.psum_tensor("ps%d" % i, [128, 1024], F32)) for i in range(4)]

    def bank(b):
        return pairs[b // 2][:, (b % 2) * 512:(b % 2 + 1) * 512]

    def bank16(b):
        return pairs[b // 2][:].bitcast(BF16)[:, (b % 2) * 1024:(b % 2 + 1) * 1024]

    rr = {"b": 0}

    def nextbank():
        b = rr["b"]
        rr["b"] = (b + 1) % 8
        return bank(b)

    def nfree(ap):
        n = 1
        for s in ap.shape[1:]:
            n *= s
        return n

    def c_act(ap, accum=False):
        return 0.2 + nfree(ap) * 0.00085 + (0.1 if accum else 0.0)

    def c_dve(ap, k=1.0):
        return 0.07 + nfree(ap) * 0.00100 * k

    def c_mm(rhs, lhsT):
        n_ = nfree(rhs)
        f = 4.0 if rhs.dtype == F32 else 1.0
        return 0.025 + max(n_, 64) * 0.00042 * f

    def DMA(out, in_, chan, eng="sp", issue=0.1):
        nbytes = nfree(out) * out.shape[0] * max(_ESZ[out.dtype], _ESZ[in_.dtype])
        P.op(eng, lambda e: e.dma_start(out=out, in_=in_), reads=[in_], writes=[out], chan=chan,
             dur=issue, xfer=2.0 + nbytes / 200e3, single=(eng == "sp"))

    def ACT(out, in_, func, bias=None, scale=None, accum=None):
        kw = {}
        rd = [in_]
        wr = [out]
        if bias is not None:
            kw["bias"] = bias
            if _isap(bias):
                rd.append(bias)
        if scale is not None:
            kw["scale"] = scale
            if _isap(scale):
                rd.append(scale)
        if accum is not None:
            kw["accum_out"] = accum
            wr.append(accum)
        P.op("act", lambda e: e.activation(out=out, in_=in_, func=func, **kw), reads=rd, writes=wr,
             dur=c_act(in_, accum is not None), single=(accum is None))

    def COPY(eng, out, in_):
        if eng == "act":
            ACT(out, in_, AF.Copy)
        else:
            P.op(eng, lambda e: e.tensor_copy(out=out, in_=in_), reads=[in_], writes=[out],
                 dur=c_dve(in_, 3.0 if eng == "pool" else 1.0), single=True)

    def TT(eng, out, in0, in1, op):
        P.op(eng, lambda e: e.tensor_tensor(out=out, in0=in0, in1=in1, op=op), reads=[in0, in1], writes=[out],
             dur=c_dve(out, 2.5 if eng == "pool" else 1.05), single=True)

    def TS(eng, out, in0, s1, op0, s2=None, op1=None):
        rd = [in0] + [s for s in (s1, s2) if _isap(s)]
        d_ = c_dve(out, 10.0 if eng == "pool" else 1.0)
        if op1 is None:
            P.op(eng, lambda e: e.tensor_scalar(out=out, in0=in0, scalar1=s1, scalar2=None, op0=op0), reads=rd, writes=[out], dur=d_, single=True)
        else:
            P.op(eng, lambda e: e.tensor_scalar(out=out, in0=in0, scalar1=s1, scalar2=s2, op0=op0, op1=op1), reads=rd, writes=[out], dur=d_, single=True)

    def STT(out, in0, scalar, in1, op0, op1):
        rd = [in0, in1] + ([scalar] if _isap(scalar) else [])
        both_sb = _space(in0) == "SB" and _space(in1) == "SB"
        P.op("dve", lambda e: e.scalar_tensor_tensor(out=out, in0=in0, scalar=scalar, in1=in1, op0=op0, op1=op1), reads=rd, writes=[out],
             dur=c_dve(out, 1.25 if both_sb else 1.0), single=True)

    def DVEOP(fn, reads, writes, dur):
        P.op("dve", fn, reads=reads, writes=writes, dur=dur)

    def MEMSET(eng, ap, val):
        P.op(eng, lambda e: e.memset(ap, val), writes=[ap], dur=c_dve(ap, 2.0))

    def ASEL(out, in_, step, n, cmp, fill, base, cm):
        P.op("pool", lambda e: e.affine_select(out=out, in_=in_, pattern=[[step, n]], compare_op=cmp, fill=fill, base=base, channel_multiplier=cm), reads=[in_], writes=[out], dur=0.5)

    def PE(mms, extra_reads=(), writes=None):
        rd = list(extra_reads)
        wr = []
        d_ = 0.02
        for (o, l, r, s0, s1) in mms:
            rd.append(l)
            rd.append(r)
            wr.append(o)
            d_ += c_mm(r, l)
        if writes is not None:
            wr = writes

        def fn(e):
            ins = None
            for (o, l, r, s0, s1) in mms:
                ins = e.matmul(o, lhsT=l, rhs=r, start=s0, stop=s1)
            return ins
        P.op("pe", fn, reads=rd, writes=wr, dur=d_)

    def MMG(out, pairs_):
        n = len(pairs_)
        PE([(out, l, r, i == 0, i == n - 1) for i, (l, r) in enumerate(pairs_)])

    def TR(items, writes):
        rd = []
        d_ = 0.05
        for (o, i_, idn) in items:
            rd.append(i_)
            rd.append(idn)
            d_ += 0.11 * (2.0 if i_.dtype == F32 else 1.0)

        def fn(e):
            ins = None
            for (o, i_, idn) in items:
                ins = e.transpose(out=o, in_=i_, identity=idn)
            return ins
        P.op("pe", fn, reads=rd, writes=writes, dur=d_)

    def kouter4(wv, nk=8):
        bks = [nextbank() for _ in range(4)]
        for k in range(nk):
            PE([(bks[o], wv[:, k, o * 128:(o + 1) * 128], yT[:, k, :], k == 0, k == nk - 1) for o in range(4)], writes=bks)
        return bks

    ppc = lambda c: pp[:, c:c + 1]
    bcv = lambda c, n: bc[:, 0, c:c + n]

    ch_setup = [P.chan() for _ in range(4)]
    DMA(pp[:], pp_d[:, :], ch_setup[0])
    DMA(bc[:], bc_d.partition_broadcast(128), ch_setup[1])
    DMA(gmw_ld, gmw_d.rearrange("g t s -> t g s"), ch_setup[2])
    DMA(bs_f[0:1, :], gmb_d[:, :], ch_setup[3])
    DMA(wdt_f, w_in_d[:, 5120:5136].rearrange("(kc p) c -> p kc c", p=128), ch_setup[0])

    MEMSET("pool", ident_f[:], 1.0)
    ASEL(ident_f[:], ident_f[:], -1, 128, ALU.is_equal, 0.0, 0, 1)
    COPY("pool", ident_bf[:], ident_f[:])
    MEMSET("pool", ones_f[:], 1.0)
    MEMSET("pool", ones_bf[:], 1.0)
    MEMSET("pool", triu_f[:], 1.0)
    ASEL(triu_f[:], triu_f[:], 1, 128, ALU.is_ge, 0.0, 0, -1)
    MEMSET("pool", slow_f[:], 1.0)
    ASEL(slow_f[:], slow_f[:], -1, 128, ALU.is_ge, 0.0, -1, 1)
    MEMSET("pool", maskf, 0.0)
    ASEL(maskf[:, 0:128], maskf[:, 0:128], 1, 128, ALU.is_ge, -30000.0, -1, -1)
    ASEL(maskf[:, 128:256], maskf[:, 128:256], -1, 128, ALU.is_ge, -30000.0, 0, 1)
    for e_ in range(2):
        COPY("pool", maskb[:, e_, :], maskf)
        COPY("pool", maskb0[:, e_, :], maskf)
        MEMSET("pool", maskb0[:, e_, 0:128], -30000.0)
    MEMSET("pool", prevT[:], 0.0)
    MEMSET("pool", prev_bf[:], 0.0)
    MEMSET("pool", tails[:], 0.0)
    MEMSET("pool", kT[:], 0.0)
    MEMSET("pool", vtok[:], 0.0)
    COPY("pool", wdt[:], wdt_f)
    COPY("dve", bs_hi[:], bs_f[0:1, :])
    TT("dve", bs_lo[:], bs_f[0:1, :], bs_hi[:], ALU.subtract)
    ACT(aneg[:], bcv(BC_ALOG, 16), AF.Exp)
    TS("dve", aneg[:], aneg[:], -1.0, ALU.mult)
    TS("dve", negsink[:], bcv(BC_SINK, 16), -1.0, ALU.mult)
    for h_ in range(16):
        TS("dve", identD[:, h_, :], ident_f[:], bcv(BC_DSK + h_, 1), ALU.mult)
    for g in range(8):
        bk = nextbank()
        TR([(bk[:, 0:128], gmw_ld[:, g, :], ident_f[:])], writes=[bk[:, 0:128]])
        TT("dve", gmWT[:, g, :], bk[:, 0:128], triu_f[:], ALU.mult)

    order = ["h0", "h1", "h2", "h3", "full"]
    lvl = order.index(stage)
    NBU = {0: B_UP0, 1: B_QKV, 2: B_UP1, 3: NB, 4: NB}[lvl]
    total_w = NT * NBU
    ws = {"rec": 0, "sc": 0, "cast": 0}
    ch_ring = [P.chan() for _ in range(NW)]
    ch_ring_sw = [P.chan() for _ in range(NW)]
    ch_wst = [P.chan() for _ in range(NW)]
    cast_engs = ["act", "dve"]
    NQ = 4

    def wrec(j):
        tt, bb = divmod(j, NBU)
        slot = j % NW
        src_, kc, cols = blocks[bb]
        n = kc * cols
        dst = wring[slot][:, 0:n]
        if tt == 0:
            DMA(dst.rearrange("p (k c) -> p k c", c=cols), src_, ch_ring_sw[slot], eng="pool", issue=1.5)
            if B_WOUT <= bb < B_WOUT + 4:
                dv = dst.rearrange("p (k c) -> p k c", c=cols)
                for j in range(8):
                    TS("dve", dv[:, 8 + j, :], dv[:, 8 + j, :], ppc(PP_NG + j), ALU.mult)
            if NT > 1:
                DMA(wscr[bb, :, 0:n], dst, ch_wst[slot])
        else:
            DMA(dst, wscr[bb, :, 0:n], ch_ring[slot])

    def wget(t, b):
        idx = t * NBU + b
        lim = min(idx + NW, total_w)
        while ws["rec"] < lim:
            wrec(ws["rec"])
            ws["rec"] += 1
        _, kc, cols = blocks[b]
        return wring[idx % NW][:, 0:kc * cols].rearrange("p (k c) -> p k c", c=cols)

    ch_x = [P.chan() for _ in range(2)]
    ch_o = [P.chan() for _ in range(2)]
    ch_dbg = P.chan()
    sm = {"i": 0}

    def smalls(n):
        i = sm["i"]
        sm["i"] = (i + 1) % 16
        return small[:, i, 0:n]

    def load_x(t):
        for c in range(4):
            xs_ = xin[c % 2]
            DMA(xs_[:], x_d[t * 512 + c * 128:t * 512 + (c + 1) * 128, :], ch_x[c % 2])
            pr = pairs[c % 2]
            TR([(pr[:, j * 128:(j + 1) * 128], xs_[:, j * 128:(j + 1) * 128], ident_f[:]) for j in range(8)], writes=[pr[:]])
            cs = slice(c * 128, (c + 1) * 128)
            COPY("act", hT[:, 0:4, cs], pr[:, 0:512].rearrange("p (j t) -> p j t", t=128))
            COPY("dve", hT[:, 4:8, cs], pr[:, 512:1024].rearrange("p (j t) -> p j t", t=128))

    nrm = {"i": 0}

    def rmsnorm(gbase):
        bk = nextbank()
        for j in range(8):
            if j % 2 == 0:
                ACT(nsq[:, j, :], hT[:, j, :], AF.Square)
                TS("dve", yT[:, j, :], hT[:, j, :], ppc(gbase + j), ALU.mult)
            else:
                TT("dve", nsq[:, j, :], hT[:, j, :], hT[:, j, :], ALU.mult)
                ACT(yT[:, j, :], hT[:, j, :], AF.Copy, scale=ppc(gbase + j))
        MMG(bk, [(ones_bf[:], nsq[:, j, :]) for j in range(8)])
        ACT(bk, bk, AF.Ln, bias=EPS, scale=1.0 / 1024)
        ACT(bk, bk, AF.Exp, scale=-0.5)
        ACT(rstd_sb, bk, AF.Copy)

        def fin():
            for j in range(8):
                STT(yT[:, j, :], hT[:, j, :], ppc(gbase + j), bk, ALU.mult, ALU.mult)
        return fin

    def rescaled(bk):
        tmp = tmpn[nrm["i"] % 2]
        nrm["i"] += 1
        TT("dve", tmp, bk, rstd_sb, ALU.mult)
        return tmp

    def dump_or_final(t, final):
        for c in range(4):
            cs = slice(c * 128, (c + 1) * 128)
            pr = pairs[2 + c % 2]
            TR([(pr[:, j * 128:(j + 1) * 128], hT[:, j, cs], ident_f[:]) for j in range(8)], writes=[pr[:]])
            ob = osb[c % 2]
            if not final:
                COPY("act", ob[:, 0:512], pr[:, 0:512])
                COPY("dve", ob[:, 512:1024], pr[:, 512:1024])
            else:
                ss = smalls(4)
                ACT(junk[:], pr[:, 0:512], AF.Square, accum=ss[:, 0:1])
                ACT(junk[:], pr[:, 512:1024], AF.Square, accum=ss[:, 1:2])
                TT("dve", ss[:, 2:3], ss[:, 0:1], ss[:, 1:2], ALU.add)
                ACT(ss[:, 3:4], ss[:, 2:3], AF.Ln, bias=EPS, scale=1.0 / 1024)
                ACT(ss[:, 2:3], ss[:, 3:4], AF.Exp, scale=-0.5)
                STT(ob[:, 0:512], pr[:, 0:512], ss[:, 2:3], bcv(BC_GF, 512), ALU.mult, ALU.mult)
                STT(ob[:, 512:1024], pr[:, 512:1024], ss[:, 2:3], bcv(BC_GF + 512, 512), ALU.mult, ALU.mult)
            DMA(out_d[t * 512 + c * 128:t * 512 + (c + 1) * 128, :], ob[:], ch_o[c % 2])

    def layer0_mixer(t):
        fin = rmsnorm(PP_GMIX0)
        P.tag = 'l0.u'
        for bi in range(2):
            wv = wget(t, B_WIN + bi)
            bks = None
            if bi == 0:
                bks = kouter4(wv)
                fin()
            for o in range(4):
                oc = bi * 4 + o
                if bks is None:
                    bk = nextbank()
                    MMG(bk, [(wv[:, k, o * 128:(o + 1) * 128], yT[:, k, :]) for k in range(8)])
                else:
                    bk = rescaled(bks[o])
                ACT(guT[:, oc, :], bk, AF.Gelu_apprx_tanh)
        P.tag = 'l0.v'
        for bi in range(2):
            wv = wget(t, B_WIN + 2 + bi)
            for tc in range(4):
                bk = nextbank()
                MMG(bk, [(yT[:, k, tc * 128:(tc + 1) * 128], wv[:, k, :]) for k in range(8)])
                ACT(gv[:, tc, bi * 512:(bi + 1) * 512], bk, AF.Gelu_apprx_tanh)
                if bi == 1:
                    layernorm_chunk(tc)
        P.tag = 'l0.z'
        for bi in range(2):
            wv = wget(t, B_WIN + 4 + bi)
            for tc in range(4):
                bk = nextbank()
                MMG(bk, [(yT[:, k, tc * 128:(tc + 1) * 128], wv[:, k, :]) for k in range(8)])
                ACT(sz[:, tc, bi * 512:(bi + 1) * 512], bk, AF.Silu)
        bkd = nextbank()
        PE([(bkd[:, tc * 16:(tc + 1) * 16], yT[:, k, tc * 128:(tc + 1) * 128], wdt[:, k, :], k == 0, k == 7)
            for tc in range(4) for k in range(8)], writes=[bkd[:, 0:64]])
        dtx = smalls(64)
        dtv = dtx.rearrange("p (c h) -> p c h", h=16)
        TT("dve", dtv, bkd[:, 0:64].rearrange("p (c h) -> p c h", h=16),
           bcv(BC_DTB, 16).unsqueeze(1).to_broadcast([128, 4, 16]), ALU.add)
        ex = smalls(64)
        ACT(ex, dtx, AF.Exp)
        dt_all = smalls(64)
        ACT(dt_all, ex, AF.Ln, bias=1.0)
        dta_all = smalls(64)
        TT("dve", dta_all.rearrange("p (c h) -> p c h", h=16), dt_all.rearrange("p (c h) -> p c h", h=16),
           aneg[:].unsqueeze(1).to_broadcast([128, 4, 16]), ALU.mult)
        bka = nextbank()
        PE([(bka[:, tc * 16:(tc + 1) * 16], triu_f[:], dta_all[:, tc * 16:(tc + 1) * 16], True, True) for tc in range(4)]
           + [(bka[:, 64 + tc * 16:64 + (tc + 1) * 16], ones_f[:], dta_all[:, tc * 16:(tc + 1) * 16], True, True) for tc in range(4)],
           writes=[bka[:, 0:128]])
        acum = smalls(64)
        COPY("dve", acum, bka[:, 0:64])
        ea_all = smalls(64)
        ACT(ea_all, bka[:, 0:64], AF.Exp)
        cd_all = smalls(64)
        ACT(cd_all, bka[:, 64:128], AF.Exp)
        te_all = smalls(64)
        TT("dve", te_all, bka[:, 64:128], acum, ALU.subtract)
        ACT(te_all, te_all, AF.Exp)
        TT("dve", te_all, te_all, dt_all, ALU.mult)
        P.tag = 'l0.conv'
        wv_x = {}

        def conv_c1(oc):
            bi, o = divmod(oc, 4)
            if o == 0:
                wv_x[bi] = wget(t, B_WIN + 6 + bi)
            wv = wv_x[bi]
            bk = nextbank()
            MMG(bk, [(wv[:, k, o * 128:(o + 1) * 128], yT[:, k, :]) for k in range(8)])
            xs_ = xst[oc % 2]
            COPY("pool", xs_[:, 0:3], tails[:, oc, :])
            ACT(xs_[:, 3:515], bk, AF.Copy)
            ACT(acc[oc % 2], bk, AF.Identity, bias=ppc(PP_CONVB + oc), scale=ppc(PP_CONVW + 3 * 16 + oc))

        def conv_c2(oc):
            xs_ = xst[oc % 2]
            ac = acc[oc % 2]
            for tap in (2, 1, 0):
                STT(ac, xs_[:, tap:tap + 512], ppc(PP_CONVW + tap * 16 + oc), ac, ALU.mult, ALU.add)
            COPY("pool", tails[:, oc, :], xs_[:, 512:515])
            ACT(xcT[:, oc, :], ac, AF.Silu)

        conv_c1(0)
        for oc in range(16):
            if oc + 1 < 16:
                conv_c1(oc + 1)
            conv_c2(oc)

        P.tag = 'l0.gmlp'
        for g in range(8):
            bk = nextbank()
            mms = []
            for tc in range(4):
                o = bk[:, tc * 128:(tc + 1) * 128]
                mms.append((o, vn[:, tc, g * 128:(g + 1) * 128], gmWT[:, g, :], True, False))
                mms.append((o, ones_bf[0:1, :], bs_hi[0:1, g * 128:(g + 1) * 128], False, False))
                mms.append((o, ones_bf[0:1, :], bs_lo[0:1, g * 128:(g + 1) * 128], False, True))
            PE(mms, writes=[bk])
            TT("dve", mixT[:, g, :], bk, guT[:, g, :], ALU.mult)
        P.tag = 'l0.ssd'
        hs = lambda a_: a_.rearrange("p (h d) -> p h d", d=64)

        def bc16(ap16):
            return ap16.unsqueeze(2).to_broadcast([128, 16, 64])

        def ssd_s1(tc):
            i = tc % 2
            cs = slice(tc * 128, (tc + 1) * 128)
            b0 = bank16(0)
            TR([(b0[:, j * 128:(j + 1) * 128], xcT[:, j, cs], ident_bf[:]) for j in range(8)], writes=[b0])
            COPY("act", xtok[i], b0)
            b1 = bank16(1)
            TR([(b1[:, g * 128:(g + 1) * 128], xcT[:, 8 + g, cs], ident_bf[:]) for g in range(4)], writes=[b1[:, 0:512]])
            COPY("act", Btok[i], b1[:, 0:512])
            TT("dve", hs(xw[i]), hs(xtok[i]), bc16(te_all[:, tc * 16:(tc + 1) * 16]), ALU.mult)
            b2 = bank(2)
            PE([(b2[:, g * 128:(g + 1) * 128], xcT[:, 8 + g, cs], xcT[:, 12 + g, cs], True, True) for g in range(4)], writes=[b2])
            TT("dve", cbm, b2.rearrange("p (g l) -> p g l", l=128), triu_f[:].unsqueeze(1).to_broadcast([128, 4, 128]), ALU.mult)

            def rbuild(g):
                rb = Rb[g % 2]
                for hh in range(4):
                    h = 4 * g + hh
                    sc_ = dta_all[:, tc * 16 + h:tc * 16 + h + 1]
                    if hh == 3:
                        ACT(rb[:, hh, :], triu_f[:], AF.Copy, scale=sc_)
                    else:
                        TS("dve", rb[:, hh, :], triu_f[:], sc_, ALU.mult)
            rbuild(0)
            rbuild(1)
            for g in range(4):
                rb = Rb[g % 2]
                sb_ = bank(3) if g % 2 == 0 else bank(2)
                PE([(sb_, slow_f[:], rb.rearrange("p h l -> p (h l)"), True, True)], writes=[sb_])
                ACT(sb_, sb_, AF.Exp)
                for hh in range(4):
                    h = 4 * g + hh
                    STT(wT[i][:, h, :], sb_[:, hh * 128:(hh + 1) * 128], dt_all[:, tc * 16 + h:tc * 16 + h + 1], cbm[:, g, :], ALU.mult, ALU.mult)
                if g + 2 < 4:
                    rbuild(g + 2)

        def ssd_s2(tc):
            i = tc % 2
            cs = slice(tc * 128, (tc + 1) * 128)
            Y = pairs[2]
            mms = []
            for h in range(16):
                o = Y[:, h * 64:(h + 1) * 64]
                mms.append((o, wT[i][:, h, :], xtok[i][:, h * 64:(h + 1) * 64], True, False))
                mms.append((o, identD[:, h, :], xtok[i][:, h * 64:(h + 1) * 64], False, True))
            PE(mms, writes=[Y[:]])
            Z = pairs[3]
            PE([(Z[:, g * 256:(g + 1) * 256], xcT[:, 12 + g, cs], prev_bf[:, g * 256:(g + 1) * 256], True, True) for g in range(4)], writes=[Z[:]])
            TT("dve", hs(t1), hs(Z[:]), bc16(ea_all[:, tc * 16:(tc + 1) * 16]), ALU.mult)
            TT("dve", yv, Y[:], t1, ALU.add)
            TT("dve", yv, yv, sz[:, tc, :], ALU.mult)
            ss4 = smalls(16)
            for g in range(4):
                ACT(junk[:, 0:256], yv[:, g * 256:(g + 1) * 256], AF.Square, accum=ss4[:, g:g + 1])
            ACT(ss4[:, 4:8], ss4[:, 0:4], AF.Ln, bias=EPS, scale=1.0 / 256)
            ACT(ss4[:, 8:12], ss4[:, 4:8], AF.Exp, scale=-0.5)
            for g in range(4):
                ACT(bout[i][:, g * 256:(g + 1) * 256], yv[:, g * 256:(g + 1) * 256], AF.Copy, scale=ss4[:, 8 + g:9 + g])

        def ssd_s3(tc):
            i = tc % 2
            cs = slice(tc * 128, (tc + 1) * 128)
            Sx = pairs[3]
            PE([(Sx[:, g * 256:(g + 1) * 256], Btok[i][:, g * 128:(g + 1) * 128], xw[i][:, g * 256:(g + 1) * 256], True, True) for g in range(4)], writes=[Sx[:]])
            TT("dve", hs(prevT[:]), hs(prevT[:]), bc16(cd_all[:, tc * 16:(tc + 1) * 16]), ALU.mult)
            TT("dve", prevT[:], Sx[:], prevT[:], ALU.add)
            COPY("act", prev_bf[:], prevT[:])
            b0 = bank16(0)
            TR([(b0[:, j * 128:(j + 1) * 128], bout[i][:, j * 128:(j + 1) * 128], ident_bf[:]) for j in range(8)], writes=[b0])
            COPY("act", mixT[:, 8:16, cs], b0.rearrange("p (j t) -> p j t", t=128))

        ssd_s1(0)
        for tc in range(4):
            if tc + 1 < 4:
                ssd_s1(tc + 1)
            ssd_s2(tc)
            ssd_s3(tc)
        if dbg_d is not None:
            DMA(dbg_d[t].rearrange("j p t -> p j t"), mixT, ch_dbg)
        P.tag = 'l0.wout'
        for i in range(4):
            wv = wget(t, B_WOUT + i)
            for o in range(2):
                oc = 2 * i + o
                bk = nextbank()
                MMG(bk, [(wv[:, k, o * 128:(o + 1) * 128], mixT[:, k, :]) for k in range(16)])
                TT("dve", hT[:, oc, :], bk, hT[:, oc, :], ALU.add)

    def layernorm_chunk(tc):
        g_ = gv[:, tc, :]
        st6 = smalls(12)
        P.op("dve", lambda e: e.bn_stats(out=st6[:, 0:6], in_=g_[:, 0:512]), reads=[g_[:, 0:512]], writes=[st6[:, 0:6]], dur=0.65)
        P.op("dve", lambda e: e.bn_stats(out=st6[:, 6:12], in_=g_[:, 512:1024]), reads=[g_[:, 512:1024]], writes=[st6[:, 6:12]], dur=0.65)
        mv = smalls(4)
        P.op("dve", lambda e: e.bn_aggr(out=mv[:, 0:2], in_=st6), reads=[st6], writes=[mv[:, 0:2]], dur=0.15)
        ACT(mv[:, 2:3], mv[:, 1:2], AF.Sqrt, bias=EPS)
        P.op("dve", lambda e: e.reciprocal(out=mv[:, 3:4], in_=mv[:, 2:3]), reads=[mv[:, 2:3]], writes=[mv[:, 3:4]], dur=0.17)
        TS("dve", g_, g_, mv[:, 0:1], ALU.subtract, mv[:, 3:4], ALU.mult)
        TT("dve", g_, g_, bcv(BC_LNG, 1024), ALU.mult)
        TT("dve", vn[:, tc, :], g_, bcv(BC_LNB, 1024), ALU.add)

    def mlp(t, l):
        fin = rmsnorm(PP_GMLP0 if l == 0 else PP_GMLP1)
        bu = B_UP0 if l == 0 else B_UP1
        bd = B_DN0 if l == 0 else B_DN1
        for i in range(8):
            wv = wget(t, bu + i)
            bks = None
            if i == 0:
                bks = kouter4(wv)
                fin()
            for o in range(4):
                f = 4 * i + o
                if bks is None:
                    bk = nextbank()
                    MMG(bk, [(wv[:, k, o * 128:(o + 1) * 128], yT[:, k, :]) for k in range(8)])
                else:
                    bk = rescaled(bks[o])
                r = r32[f % 2]
                ACT(r, bk, AF.Relu)
                TT("dve", hid[:, f, :], r, r, ALU.mult)
        for i in range(8):
            wv = wget(t, bd + i)
            bk = nextbank()
            MMG(bk, [(wv[:, k, :], hid[:, k, :]) for k in range(32)])
            TT("dve", hT[:, i, :], bk, hT[:, i, :], ALU.add)

    def layer1_mixer(t):
        fin = rmsnorm(PP_GMIX1)
        COPY("pool", kT[:, :, 0:128], kT[:, :, 512:640])
        COPY("pool", vtok[:, 0, :], vtok[:, 4, :])
        for bi in range(2):
            wv = wget(t, B_QKV + bi)
            bks = None
            if bi == 0:
                bks = kouter4(wv)
                fin()
            for o in range(4):
                hp = bi * 4 + o
                if bks is None:
                    bk = nextbank()
                    MMG(bk, [(wv[:, k, o * 128:(o + 1) * 128], yT[:, k, :]) for k in range(8)])
                else:
                    bk = rescaled(bks[o])
                ACT(qT[:, hp, :], bk, AF.Identity, bias=ppc(PP_BQ + hp))
        wv = wget(t, B_QKV + 2)
        for kv in range(2):
            bk = nextbank()
            MMG(bk, [(wv[:, k, kv * 128:(kv + 1) * 128], yT[:, k, :]) for k in range(8)])
            ACT(kT[:, kv, 128:640], bk, AF.Identity, bias=ppc(PP_BK + kv))
        bk = nextbank()
        PE([(bk[:, tc * 128:(tc + 1) * 128], yT[:, k, tc * 128:(tc + 1) * 128], wv[:, k, 256:384], k == 0, k == 7)
            for tc in range(4) for k in range(8)], writes=[bk])
        TT("dve", vtok[:, 1:5, :], bk.rearrange("p (c d) -> p c d", d=128),
           bcv(BC_BV, 128).unsqueeze(1).to_broadcast([128, 4, 128]), ALU.add)
        P.tag = 'l1.attn'
        items = [(tc, hp) for tc in range(4) for hp in range(8)]
        sls = {}

        def sc_pair(n):
            return pairs[0] if n % 2 == 0 else pairs[3]

        def att_a1(n):
            tc, hp = items[n]
            cs = slice(tc * 128, (tc + 1) * 128)
            first = (t == 0 and tc == 0)
            mk_ = maskb0 if first else maskb
            kv = hp // 4
            pr = sc_pair(n)
            mms = []
            for e_ in range(2):
                mms.append((pr[:, e_ * 512:e_ * 512 + 256], qT[64 * e_:64 * e_ + 64, hp, cs],
                            kT[64 * e_:64 * e_ + 64, kv, tc * 128:tc * 128 + 256], True, False))
            for e_ in range(2):
                mms.append((pr[:, e_ * 512:e_ * 512 + 256], ident_bf[:], mk_[:, e_, :], False, True))
            PE(mms, writes=[pr[:]])
            sls[n] = smalls(16)

        def att_b125(n):
            tc, hp = items[n]
            sl = sls[n]
            pr = sc_pair(n)
            sc_v = pr[:].rearrange("p (e s) -> p e s", s=512)[:, :, 0:256]
            P.op("dve", (lambda o_, i_: lambda e: e.tensor_reduce(out=o_, in_=i_, axis=AX.X, op=ALU.max))(sl[:, 0:2], sc_v),
                 reads=[pr[:]], writes=[sl[:, 0:2]], dur=0.67)
            STT(sl[:, 2:4], sl[:, 0:2], -0.125, negsink[:, 2 * hp:2 * hp + 2], ALU.mult, ALU.min)
            TT("dve", sl[:, 6:8], bcv(BC_SINK + 2 * hp, 2), sl[:, 2:4], ALU.add)

        def att_c35(n):
            tc, hp = items[n]
            sl = sls[n]
            pr = sc_pair(n)
            pb = Pb[n % 2]
            for e_ in range(2):
                ACT(pb[:, e_, :], pr[:, e_ * 512:e_ * 512 + 256], AF.Exp, bias=sl[:, 2 + e_:3 + e_], scale=0.125,
                    accum=sl[:, 4 + e_:5 + e_])
            ACT(sl[:, 8:10], sl[:, 6:8], AF.Exp)

        def att_a4(n):
            pb = Pb[n % 2]
            ptb = bank16(2 + n % 2)
            TR([(ptb[:, (2 * e_ + hf) * 128:(2 * e_ + hf + 1) * 128], pb[:, e_, hf * 128:(hf + 1) * 128], ident_bf[:])
                for e_ in range(2) for hf in range(2)], writes=[ptb[:, 0:512]])

        def att_copy(n):
            ptb = bank16(2 + n % 2)
            COPY("dve", PT[n % 2], ptb[:, 0:512].rearrange("p (a b) -> p a b", b=128))

        def att_b67(n):
            sl = sls[n]
            TT("dve", sl[:, 10:12], sl[:, 8:10], sl[:, 4:6], ALU.add)
            P.op("dve", (lambda o_, i_: lambda e: e.reciprocal(out=o_, in_=i_))(sl[:, 12:14], sl[:, 10:12]),
                 reads=[sl[:, 10:12]], writes=[sl[:, 12:14]], dur=0.17)

        def att_a7(n):
            tc, hp = items[n]
            kv = hp // 4
            pt = PT[n % 2]
            O = pairs[2]
            mms = []
            for e_ in range(2):
                h = 2 * hp + e_
                o = O[:, h * 64:(h + 1) * 64]
                for hf in range(2):
                    mms.append((o, pt[:, 2 * e_ + hf, :], vtok[:, tc + hf, kv * 64:(kv + 1) * 64], hf == 0, hf == 1))
            PE(mms, writes=[O[:, 2 * hp * 64:(2 * hp + 2) * 64]])

        def att_b8(n):
            tc, hp = items[n]
            cs = slice(tc * 128, (tc + 1) * 128)
            sl = sls[n]
            O = pairs[2]
            at = attn[tc % 2]
            TT("dve", at[:, 2 * hp * 64:(2 * hp + 2) * 64].rearrange("p (e d) -> p e d", d=64),
               O[:, 2 * hp * 64:(2 * hp + 2) * 64].rearrange("p (e d) -> p e d", d=64),
               sl[:, 12:14].unsqueeze(2).to_broadcast([128, 2, 64]), ALU.mult)
            if hp == 7:
                b6 = bank16(3)
                TR([(b6[:, j * 128:(j + 1) * 128], at[:, j * 128:(j + 1) * 128], ident_bf[:]) for j in range(8)], writes=[b6])
                COPY("act", aoT[:, :, cs], b6.rearrange("p (j t) -> p j t", t=128))

        NI = len(items)
        for k in range(NI + 2):
            if k < NI:
                att_a1(k)
            if 0 <= k - 2 < NI:
                att_a7(k - 2)
            if 0 <= k - 1 < NI:
                att_a4(k - 1)
            if k < NI:
                att_b125(k)
                att_c35(k)
            if 0 <= k - 1 < NI:
                att_b67(k - 1)
            if 0 <= k - 2 < NI:
                att_b8(k - 2)
            if 0 <= k - 1 < NI:
                att_copy(k - 1)
        if dbg_d is not None and stage == "h2":
            DMA(dbg_d[t, 0:8].rearrange("j p t -> p j t"), aoT, ch_dbg)
        P.tag = 'l1.wo'
        for bi in range(2):
            wv = wget(t, B_WO + bi)
            for o in range(4):
                oc = bi * 4 + o
                bk = nextbank()
                MMG(bk, [(wv[:, k, o * 128:(o + 1) * 128], aoT[:, k, :]) for k in range(8)])
                STT(hT[:, oc, :], bk, ppc(PP_BO + oc), hT[:, oc, :], ALU.add, ALU.add)

    for t in range(NT):
        hT = hTs[0]
        P.tag = "load_x"
        load_x(t)
        P.tag = "l0"
        layer0_mixer(t)
        if lvl >= 1:
            P.tag = "mlp0"
            mlp(t, 0)
        if lvl >= 2:
            P.tag = "l1"
            layer1_mixer(t)
        if lvl >= 3:
            P.tag = "mlp1"
            mlp(t, 1)
        P.tag = "final"
        dump_or_final(t, final=(lvl == 4))

    P.sbuf_left = nc.sbuf_bytes_remaining
    P.emit(nc, st)
    st.close()
    return nc, P


def _host_inputs(inp):
    f = lambda k: np.ascontiguousarray(np.asarray(inp[k], dtype=np.float32))
    col = lambda v: np.ascontiguousarray(v.reshape(-1, 128).T)
    bq = f("b_qkv")[0]
    k0, k1 = bq[1024:1088], bq[1088:1152]
    pp = np.concatenate([
        col(f("norm_mix_g")[0]), col(f("norm_mlp_g")[0]), col(f("norm_mix_g")[1]), col(f("norm_mlp_g")[1]),
        np.concatenate([col(f("ssm_conv_w")[0][i]) for i in range(4)], axis=1),
        col(f("ssm_conv_b")[0]),
        col(bq[0:1024]),
        np.stack([np.concatenate([k0, k0]), np.concatenate([k1, k1])], axis=1),
        col(f("b_o")[0]),
        col(f("ssm_norm_g")[0]),
    ], axis=1)
    assert pp.shape == (128, NPP), pp.shape
    bc = np.concatenate([
        f("gm_ln_g")[0], f("gm_ln_b")[0], f("final_norm_g"),
        bq[1152:1280], f("ssm_dt_bias")[0], f("ssm_a_log")[0], f("ssm_d")[0], f("attn_sinks")[0],
    ])[None, :]
    assert bc.shape == (1, NBC), bc.shape
    wq = f("w_qkv")[0]
    wq_ext = np.concatenate([wq[:, 0:1024], wq[:, 1024:1088], wq[:, 1024:1088], wq[:, 1088:1152], wq[:, 1088:1152],
                             wq[:, 1152:1280]], axis=1)
    shared = {
        "w_in": f("w_in_even")[0], "w_out": f("w_out_even")[0], "w_qkv": np.ascontiguousarray(wq_ext),
        "w_o": f("w_o")[0], "w_up": f("w_up"), "w_down": f("w_down"),
        "pp": np.ascontiguousarray(pp), "bc": np.ascontiguousarray(bc),
        "gmw": f("gm_w_s")[0], "gmb": f("gm_b_s")[0].reshape(1, 1024),
    }
    return shared


def kernel(**inputs):
    x = np.asarray(inputs["x"], dtype=np.float32)
    shared = _host_inputs(inputs)
    nc, _ = build(NT=8, stage="full")
    in_maps = []
    for b in range(8):
        m = dict(shared)
        m["x"] = np.ascontiguousarray(x[b])
        in_maps.append(m)
    res = run_bass_kernel_spmd(nc, in_maps, core_ids=list(range(8)))
    return np.stack([np.asarray(res.results[b]["out"], dtype=np.float32) for b in range(8)], axis=0)
```

```python
import numpy as np
from contextlib import ExitStack
import concourse.bass as bass
import concourse.mybir as mybir
from concourse.bass_utils import run_bass_kernel_spmd

F32 = mybir.dt.float32
BF16 = mybir.dt.bfloat16
AF = mybir.ActivationFunctionType
ALU = mybir.AluOpType
AX = mybir.AxisListType

ENGS = ["pe", "act", "dve", "pool", "sp"]
_ESZ = {F32: 4, BF16: 2}
_PAGE = {"SB": 256, "PSUM": 2048, "DRAM": 1 << 18}


class Buf:
    __slots__ = ("w", "r")

    def __init__(self):
        self.w = None
        self.r = []


class Chan:
    def __init__(self, key):
        self.key = key
        self.last = None
        self.count = 0


def _space(ap):
    s = str(ap.space)
    if "PSUM" in s:
        return "PSUM"
    if "DRAM" in s or "HBM" in s:
        return "DRAM"
    return "SB"


class Prog:
    SAME_ENGINE_SYNC = True
    SCHEDULE = True
    FUSE_WAIT = True
    VC_PRUNE = True
    CPW = 20.0
    LAT = 0.25

    def __init__(self):
        self.recs = []
        self.tags = []
        self.single = []
        self.tag = 'setup'
        self.chans = []
        self.bufs = {}
        self.nops = 0
        self.ops = {e: [] for e in ENGS}

    def chan(self):
        c = Chan("ch%d" % len(self.chans))
        self.chans.append(c)
        return c

    def regs(self, ap):
        sp = _space(ap)
        esz = _ESZ[ap.dtype]
        pat = ap.ap
        off = ap.offset
        if sp == "DRAM":
            f0 = off
            dims = pat
        else:
            pstep = pat[0][0]
            f0 = off % pstep if pstep > 0 else off
            dims = pat[1:]
        ext = 0
        for s, c in dims:
            ext += abs(s) * (c - 1)
        lo = f0 * esz
        hi = (f0 + ext) * esz + esz - 1
        pg = _PAGE[sp]
        name = ap.tensor.name
        out = []
        for p in range(lo // pg, hi // pg + 1):
            k = (name, p)
            b = self.bufs.get(k)
            if b is None:
                b = self.bufs[k] = Buf()
            out.append(b)
        return out

    def op(self, eng, fn, reads=(), writes=(), chan=None, dur=0.3, xfer=0.0, single=False):
        rb = []
        wb = []
        for a in reads:
            (wb if _space(a) == "PSUM" else rb).extend(self.regs(a))
        for a in writes:
            wb.extend(self.regs(a))
        oid = len(self.recs)
        deps = set()
        for b in rb:
            if b.w is not None:
                deps.add(b.w)
        for b in wb:
            if b.w is not None:
                deps.add(b.w)
            deps.update(b.r)
        if chan is not None:
            if chan.last is not None:
                deps.add(chan.last)
            chan.last = oid
        deps.discard(oid)
        for b in rb:
            b.r.append(oid)
        for b in wb:
            b.w = oid
            b.r = []
        self.recs.append((eng, fn, deps, chan, dur, xfer))
        self.tags.append(self.tag)
        self.single.append(single)
        self.ops[eng].append(oid)
        self.nops += 1
        return oid

    def schedule(self):
        import heapq
        recs = self.recs
        n = len(recs)
        self.sched_seq = []
        if not self.SCHEDULE:
            order = {e: list(self.ops[e]) for e in ENGS}
            self.sched_seq = list(range(n))
            return order, 0.0
        succ = [[] for _ in range(n)]
        indeg = [0] * n
        for i, r in enumerate(recs):
            indeg[i] = len(r[2])
            for d in r[2]:
                succ[d].append(i)
        blev = [0.0] * n
        for i in range(n - 1, -1, -1):
            m_ = 0.0
            for s_ in succ[i]:
                if blev[s_] > m_:
                    m_ = blev[s_]
            blev[i] = m_ + recs[i][4] + recs[i][5] + 0.2
        prio = [(i - self.CPW * blev[i], i) for i in range(n)]
        ready_t = [0.0] * n
        pend = {e: [] for e in ENGS}
        avail = {e: [] for e in ENGS}
        free = {e: 0.0 for e in ENGS}
        order = {e: [] for e in ENGS}
        for i in range(n):
            if indeg[i] == 0:
                heapq.heappush(pend[recs[i][0]], (0.0, i))
        done = 0
        tend = 0.0
        while done < n:
            best = None
            for e in ENGS:
                pe_, av = pend[e], avail[e]
                while pe_ and pe_[0][0] <= free[e]:
                    heapq.heappush(av, prio[heapq.heappop(pe_)[1]])
                if av:
                    cand = (free[e], av[0][1], e, True)
                elif pe_:
                    cand = (pe_[0][0], pe_[0][1], e, False)
                else:
                    continue
                if best is None or cand[:2] < best[:2]:
                    best = cand
            start, i, e, from_av = best
            if from_av:
                heapq.heappop(avail[e])
            else:
                heapq.heappop(pend[e])
            eng, fn, deps, chan, dur, xfer = recs[i]
            end_eng = start + dur
            end = end_eng + xfer
            free[e] = end_eng
            order[e].append(i)
            self.sched_seq.append(i)
            tend = max(tend, end)
            done += 1
            for s_ in succ[i]:
                lat = self.LAT if (recs[s_][0] != e or chan is not None) else 0.05
                t_ = end + lat
                if t_ > ready_t[s_]:
                    ready_t[s_] = t_
                indeg[s_] -= 1
                if indeg[s_] == 0:
                    heapq.heappush(pend[recs[s_][0]], (ready_t[s_], s_))
        return order, tend

    def simulate(self, order):
        recs = self.recs
        endt = [None] * len(recs)
        pos = {e: 0 for e in ENGS}
        free = {e: 0.0 for e in ENGS}
        busy = {e: 0.0 for e in ENGS}
        remaining = sum(len(order[e]) for e in ENGS)
        tend = 0.0
        while remaining:
            progressed = False
            for e in ENGS:
                while pos[e] < len(order[e]):
                    i = order[e][pos[e]]
                    eng, fn, deps, chan, dur, xfer = recs[i]
                    t = free[e]
                    ok = True
                    for d in deps:
                        if endt[d] is None:
                            ok = False
                            break
                        lat = self.LAT if (recs[d][0] != e or recs[d][3] is not None) else 0.05
                        t = max(t, endt[d] + lat)
                    if not ok:
                        break
                    free[e] = t + dur
                    busy[e] += dur
                    endt[i] = t + dur + xfer
                    tend = max(tend, endt[i])
                    pos[e] += 1
                    remaining -= 1
                    progressed = True
            assert progressed, "deadlock in simulated order"
        return tend, busy

    def emit(self, nc, st):
        order, tend = self.schedule()
        self.sim_us, self.busy_us = self.simulate(order)
        self.predicted_us = tend
        recs = self.recs
        tok = [None] * len(recs)
        cnt = {e: 0 for e in ENGS}
        ccnt = {c.key: 0 for c in self.chans}
        for i, r in enumerate(recs):
            if r[3] is not None:
                ccnt[r[3].key] += 1
                tok[i] = (r[3].key, ccnt[r[3].key] * 16)
        for e in ENGS:
            for i in order[e]:
                if recs[i][3] is None:
                    cnt[e] += 1
                    tok[i] = (e, cnt[e])
        sems = {}
        for e in ENGS:
            sems[e] = st.enter_context(nc.semaphore("s_" + e))
        for c in self.chans:
            sems[c.key] = st.enter_context(nc.semaphore("s_" + c.key))
        block = st.enter_context(nc.Block())
        finals = {}
        for e in ENGS:
            if cnt[e] > 0:
                finals[e] = cnt[e]
        for c in self.chans:
            if ccnt[c.key] > 0:
                finals[c.key] = ccnt[c.key] * 16

        seq = self.sched_seq if len(self.sched_seq) == len(recs) else list(range(len(recs)))
        kn = {e: {} for e in ENGS}
        vc = [None] * len(recs)
        fw = [None] * len(recs)
        for i in seq:
            e_, fn, deps, chan, dur, xfer = recs[i]
            K_ = kn[e_]
            cand = {}
            for d in deps:
                key, val = tok[d]
                if key == e_ and chan is None and recs[d][3] is None:
                    if e_ == "pe" or not self.SAME_ENGINE_SYNC:
                        continue
                if cand.get(key, (0, 0))[0] < val:
                    cand[key] = (val, d)
            wts = []
            for key, (val, d) in sorted(cand.items(), key=lambda kv: -kv[1][1]):
                if K_.get(key, 0) >= val:
                    continue
                wts.append((key, val))
                if self.VC_PRUNE:
                    for k2, v2 in vc[d].items():
                        if K_.get(k2, 0) < v2:
                            K_[k2] = v2
                if K_.get(key, 0) < val:
                    K_[key] = val
            fw[i] = sorted(wts)
            v_ = dict(K_)
            if v_.get(tok[i][0], 0) < tok[i][1]:
                v_[tok[i][0]] = tok[i][1]
            vc[i] = v_
        self.n_waits = sum(len(w) for w in fw)

        def mk(ename):
            def body(eng):
                for i in order[ename]:
                    e_, fn, deps, chan, dur, xfer = recs[i]
                    wl = list(fw[i])
                    fused = wl.pop() if (wl and self.single[i] and self.FUSE_WAIT) else None
                    for k, v in wl:
                        eng.wait_ge(sems[k], v)
                    ins = fn(eng)
                    if fused is not None:
                        ins._wait_ge(sems[fused[0]], fused[1])
                    if chan is not None:
                        ins.then_inc(sems[chan.key], 16)
                    else:
                        ins.then_inc(sems[ename], 1)
                if ename == "sp":
                    for k, v in finals.items():
                        eng.wait_ge(sems[k], v)
            return body

        block.tensor(mk("pe"))
        block.scalar(mk("act"))
        block.vector(mk("dve"))
        block.gpsimd(mk("pool"))
        block.sync(mk("sp"))


def _isap(x):
    return hasattr(x, "tensor") and hasattr(x, "ap")


PP_GMIX0, PP_GMLP0, PP_GMIX1, PP_GMLP1 = 0, 8, 16, 24
PP_CONVW = 32
PP_CONVB = 96
PP_BQ = 112
PP_BK = 120
PP_BO = 122
PP_NG = 130
NPP = 138
BC_LNG, BC_LNB, BC_GF = 0, 1024, 2048
BC_BV = 3072
BC_DTB, BC_ALOG, BC_DSK, BC_SINK = 3200, 3216, 3232, 3248
NBC = 3264

NB = 51
NW = 4
EPS = 1e-5


def build(NT=8, stage="full"):
    nc = bass.Bass("TRN2", target_bir_lowering=False)
    st = ExitStack()
    P = Prog()

    def dram(name, shape, dt, kind):
        return nc.dram_tensor(name, shape, dt, kind=kind).ap()

    def sb(name, shape, dt):
        return st.enter_context(nc.sbuf_tensor(name, shape, dt))

    x_d = dram("x", [4096, 1024], F32, "ExternalInput")
    out_d = dram("out", [4096, 1024], F32, "ExternalOutput")
    w_in_d = dram("w_in", [1024, 5136], F32, "ExternalInput")
    w_out_d = dram("w_out", [2048, 1024], F32, "ExternalInput")
    w_qkv_d = dram("w_qkv", [1024, 1408], F32, "ExternalInput")
    w_o_d = dram("w_o", [1024, 1024], F32, "ExternalInput")
    w_up_d = dram("w_up", [2, 1024, 4096], F32, "ExternalInput")
    w_down_d = dram("w_down", [2, 4096, 1024], F32, "ExternalInput")
    pp_d = dram("pp", [128, NPP], F32, "ExternalInput")
    bc_d = dram("bc", [1, NBC], F32, "ExternalInput")
    gmw_d = dram("gmw", [8, 128, 128], F32, "ExternalInput")
    gmb_d = dram("gmb", [1, 1024], F32, "ExternalInput")
    wscr = dram("wscr", [NB, 128, 4096], BF16, "Internal")
    dbg_d = None
    if stage != "full":
        dbg_d = dram("dbg", [NT, 16, 128, 512], BF16, "ExternalOutput")

    blocks = []

    def addblk(ap2d, kc, cols):
        blocks.append((ap2d.rearrange("(kc p) c -> p kc c", p=128), kc, cols))

    for i in range(10):
        addblk(w_in_d[:, 512 * i:512 * (i + 1)], 8, 512)
    for i in range(4):
        addblk(w_out_d[:, 256 * i:256 * (i + 1)], 16, 256)
    for i in range(8):
        addblk(w_up_d[0, :, 512 * i:512 * (i + 1)], 8, 512)
    for i in range(8):
        addblk(w_down_d[0, :, 128 * i:128 * (i + 1)], 32, 128)
    addblk(w_qkv_d[:, 0:512], 8, 512)
    addblk(w_qkv_d[:, 512:1024], 8, 512)
    addblk(w_qkv_d[:, 1024:1408], 8, 384)
    for i in range(2):
        addblk(w_o_d[:, 512 * i:512 * (i + 1)], 8, 512)
    for i in range(8):
        addblk(w_up_d[1, :, 512 * i:512 * (i + 1)], 8, 512)
    for i in range(8):
        addblk(w_down_d[1, :, 128 * i:128 * (i + 1)], 32, 128)
    assert len(blocks) == NB
    B_WIN, B_WOUT, B_UP0, B_DN0, B_QKV, B_WO, B_UP1, B_DN1 = 0, 10, 14, 22, 30, 33, 35, 43

    hTs = [sb("hT0", [128, 8, 512], F32)]
    hT = hTs[0]
    yT = sb("yT", [128, 8, 512], BF16)
    wring = [sb("wring%d" % i, [128, 4096], BF16) for i in range(NW)]
    xin = [sb("xin%d" % i, [128, 1024], F32) for i in range(2)]
    osb = [sb("osb%d" % i, [128, 1024], F32) for i in range(2)]
    pp = sb("pp_sb", [128, NPP], F32)
    bc = sb("bc_sb", [128, 1, NBC], F32)
    ident_bf = sb("ident_bf", [128, 128], BF16)
    ident_f = sb("ident_f", [128, 128], F32)
    ones_bf = sb("ones_bf", [128, 128], BF16)
    ones_f = sb("ones_f", [128, 128], F32)
    triu_f = sb("triu_f", [128, 128], F32)
    slow_f = sb("slow_f", [128, 128], F32)
    maskb = sb("maskb", [128, 2, 256], BF16)
    maskb0 = sb("maskb0", [128, 2, 256], BF16)
    gmWT = sb("gmWT", [128, 8, 128], BF16)
    bs_hi = sb("bs_hi", [1, 1024], BF16)
    bs_lo = sb("bs_lo", [1, 1024], BF16)
    wdt = sb("wdt", [128, 8, 16], BF16)
    identD = sb("identD", [128, 16, 128], BF16)
    aneg = sb("aneg", [128, 16], F32)
    negsink = sb("negsink", [128, 16], F32)
    prevT = sb("prevT", [128, 1024], F32)
    prev_bf = sb("prev_bf", [128, 1024], BF16)
    tails = sb("tails", [128, 16, 3], F32)
    kT = sb("kT", [128, 2, 640], BF16)
    vtok = sb("vtok", [128, 5, 128], BF16)
    small = sb("small", [128, 16, 64], F32)
    junk = sb("junk", [128, 512], BF16)

    SCR_BYTES = 84 * 1024
    scr = sb("scr", [128, SCR_BYTES // 2], BF16)

    def sview(off, shape, dt):
        n = 1
        for s in shape:
            n *= s
        nb = n * _ESZ[dt]
        assert off % 4 == 0 and off + nb <= SCR_BYTES, (off, nb)
        v = scr[:, off // 2:(off + nb) // 2]
        if dt == F32:
            v = v.bitcast(F32)
        if len(shape) == 2:
            return v.rearrange("p (a b) -> p a b", b=shape[1])
        if len(shape) == 3:
            return v.rearrange("p (a b c) -> p a b c", b=shape[1], c=shape[2])
        return v

    K = 1024
    mixT = sview(0, [16, 512], BF16)
    xcT = sview(16 * K, [16, 512], BF16)
    sz = sview(32 * K, [4, 1024], BF16)
    xst = [sview(40 * K + i * 2304, [576], F32) for i in range(2)]
    acc = [sview(45 * K + i * 2048, [512], F32) for i in range(2)]
    guT = sview(49 * K, [8, 512], BF16)
    vn = sview(57 * K, [4, 1024], BF16)
    gv = sview(65 * K, [4, 1024], F32)
    gmw_ld = sview(0, [8, 128], F32)
    bs_f = sview(4 * K, [1024], F32)
    wdt_f = sview(8 * K, [8, 16], F32)
    maskf = sview(9 * K, [256], F32)
    yv = sview(40 * K, [1024], F32)
    t1 = sview(45 * K, [1024], F32)
    xtok = [sview(49 * K + i * 2048, [1024], BF16) for i in range(2)]
    xw = [sview(59 * K + i * 2048, [1024], BF16) for i in range(2)]
    Btok = [sview(63 * K + i * 1024, [512], BF16) for i in range(2)]
    cbm = sview(65 * K, [4, 128], F32)
    Rb = [sview(67 * K + i * 2048, [4, 128], F32) for i in range(2)]
    wT = [sview(71 * K + i * 4096, [16, 128], BF16) for i in range(2)]
    bout = [sview(79 * K + i * 2048, [1024], BF16) for i in range(2)]
    nsq = sview(66 * K, [8, 512], BF16)
    rstd_sb = sview(74 * K, [512], F32)
    tmpn = [sview(76 * K + i * 2048, [512], F32) for i in range(2)]
    hid = sview(0, [32, 512], BF16)
    r32 = [sview(32 * K + i * 2048, [512], F32) for i in range(2)]
    qT = sview(0, [8, 512], BF16)
    aoT = sview(8 * K, [8, 512], BF16)
    Pb = [sview(16 * K + i * 1024, [2, 256], BF16) for i in range(2)]
    PT = [sview(18 * K + i * 1024, [4, 128], BF16) for i in range(2)]
    attn = [sview(20 * K + i * 2048, [1024], BF16) for i in range(2)]

    pairs = [st.enter_context(nc.psum_tensor("ps%d" % i, [128, 1024], F32)) for i in range(4)]

    def bank(b):
        return pairs[b // 2][:, (b % 2) * 512:(b % 2 + 1) * 512]

    def bank16(b):
        return pairs[b // 2][:].bitcast(BF16)[:, (b % 2) * 1024:(b % 2 + 1) * 1024]

    rr = {"b": 0}

    def nextbank():
        b = rr["b"]
        rr["b"] = (b + 1) % 8
        return bank(b)

    def nfree(ap):
        n = 1
        for s in ap.shape[1:]:
            n *= s
        return n

    def c_act(ap, accum=False):
        return 0.2 + nfree(ap) * 0.00085 + (0.1 if accum else 0.0)

    def c_dve(ap, k=1.0):
        return 0.07 + nfree(ap) * 0.00100 * k

    def c_mm(rhs, lhsT):
        n_ = nfree(rhs)
        f = 4.0 if rhs.dtype == F32 else 1.0
        return 0.025 + max(n_, 64) * 0.00042 * f

    def DMA(out, in_, chan, eng="sp", issue=0.1):
        nbytes = nfree(out) * out.shape[0] * max(_ESZ[out.dtype], _ESZ[in_.dtype])
        P.op(eng, lambda e: e.dma_start(out=out, in_=in_), reads=[in_], writes=[out], chan=chan,
             dur=issue, xfer=2.0 + nbytes / 200e3, single=(eng == "sp"))

    def ACT(out, in_, func, bias=None, scale=None, accum=None):
        kw = {}
        rd = [in_]
        wr = [out]
        if bias is not None:
            kw["bias"] = bias
            if _isap(bias):
                rd.append(bias)
        if scale is not None:
            kw["scale"] = scale
            if _isap(scale):
                rd.append(scale)
        if accum is not None:
            kw["accum_out"] = accum
            wr.append(accum)
        P.op("act", lambda e: e.activation(out=out, in_=in_, func=func, **kw), reads=rd, writes=wr,
             dur=c_act(in_, accum is not None), single=True)

    def COPY(eng, out, in_):
        if eng == "act":
            ACT(out, in_, AF.Copy)
        else:
            P.op(eng, lambda e: e.tensor_copy(out=out, in_=in_), reads=[in_], writes=[out],
                 dur=c_dve(in_, 3.0 if eng == "pool" else 1.0), single=True)

    def TT(eng, out, in0, in1, op):
        P.op(eng, lambda e: e.tensor_tensor(out=out, in0=in0, in1=in1, op=op), reads=[in0, in1], writes=[out],
             dur=c_dve(out, 2.5 if eng == "pool" else 1.05), single=True)

    def TS(eng, out, in0, s1, op0, s2=None, op1=None):
        rd = [in0] + [s for s in (s1, s2) if _isap(s)]
        d_ = c_dve(out, 10.0 if eng == "pool" else 1.0)
        if op1 is None:
            P.op(eng, lambda e: e.tensor_scalar(out=out, in0=in0, scalar1=s1, scalar2=None, op0=op0), reads=rd, writes=[out], dur=d_, single=True)
        else:
            P.op(eng, lambda e: e.tensor_scalar(out=out, in0=in0, scalar1=s1, scalar2=s2, op0=op0, op1=op1), reads=rd, writes=[out], dur=d_, single=True)

    def STT(out, in0, scalar, in1, op0, op1):
        rd = [in0, in1] + ([scalar] if _isap(scalar) else [])
        both_sb = _space(in0) == "SB" and _space(in1) == "SB"
        P.op("dve", lambda e: e.scalar_tensor_tensor(out=out, in0=in0, scalar=scalar, in1=in1, op0=op0, op1=op1), reads=rd, writes=[out],
             dur=c_dve(out, 1.25 if both_sb else 1.0), single=True)

    def DVEOP(fn, reads, writes, dur):
        P.op("dve", fn, reads=reads, writes=writes, dur=dur)

    def MEMSET(eng, ap, val):
        P.op(eng, lambda e: e.memset(ap, val), writes=[ap], dur=c_dve(ap, 2.0))

    def ASEL(out, in_, step, n, cmp, fill, base, cm):
        P.op("pool", lambda e: e.affine_select(out=out, in_=in_, pattern=[[step, n]], compare_op=cmp, fill=fill, base=base, channel_multiplier=cm), reads=[in_], writes=[out], dur=0.5)

    def PE(mms, extra_reads=(), writes=None):
        rd = list(extra_reads)
        wr = []
        d_ = 0.02
        for (o, l, r, s0, s1) in mms:
            rd.append(l)
            rd.append(r)
            wr.append(o)
            d_ += c_mm(r, l)
        if writes is not None:
            wr = writes

        def fn(e):
            ins = None
            for (o, l, r, s0, s1) in mms:
                ins = e.matmul(o, lhsT=l, rhs=r, start=s0, stop=s1)
            return ins
        P.op("pe", fn, reads=rd, writes=wr, dur=d_)

    def MMG(out, pairs_):
        n = len(pairs_)
        PE([(out, l, r, i == 0, i == n - 1) for i, (l, r) in enumerate(pairs_)])

    def TR(items, writes):
        rd = []
        d_ = 0.05
        for (o, i_, idn) in items:
            rd.append(i_)
            rd.append(idn)
            d_ += 0.11 * (2.0 if i_.dtype == F32 else 1.0)

        def fn(e):
            ins = None
            for (o, i_, idn) in items:
                ins = e.transpose(out=o, in_=i_, identity=idn)
            return ins
        P.op("pe", fn, reads=rd, writes=writes, dur=d_)

    def kouter4(wv, nk=8):
        bks = [nextbank() for _ in range(4)]
        for k in range(nk):
            PE([(bks[o], wv[:, k, o * 128:(o + 1) * 128], yT[:, k, :], k == 0, k == nk - 1) for o in range(4)], writes=bks)
        return bks

    ppc = lambda c: pp[:, c:c + 1]
    bcv = lambda c, n: bc[:, 0, c:c + n]

    ch_setup = [P.chan() for _ in range(4)]
    DMA(pp[:], pp_d[:, :], ch_setup[0])
    DMA(bc[:], bc_d.partition_broadcast(128), ch_setup[1])
    DMA(gmw_ld, gmw_d.rearrange("g t s -> t g s"), ch_setup[2])
    DMA(bs_f[0:1, :], gmb_d[:, :], ch_setup[3])
    DMA(wdt_f, w_in_d[:, 5120:5136].rearrange("(kc p) c -> p kc c", p=128), ch_setup[0])

    MEMSET("pool", ident_f[:], 1.0)
    ASEL(ident_f[:], ident_f[:], -1, 128, ALU.is_equal, 0.0, 0, 1)
    COPY("pool", ident_bf[:], ident_f[:])
    MEMSET("pool", ones_f[:], 1.0)
    MEMSET("pool", ones_bf[:], 1.0)
    MEMSET("pool", triu_f[:], 1.0)
    ASEL(triu_f[:], triu_f[:], 1, 128, ALU.is_ge, 0.0, 0, -1)
    MEMSET("pool", slow_f[:], 1.0)
    ASEL(slow_f[:], slow_f[:], -1, 128, ALU.is_ge, 0.0, -1, 1)
    MEMSET("pool", maskf, 0.0)
    ASEL(maskf[:, 0:128], maskf[:, 0:128], 1, 128, ALU.is_ge, -30000.0, -1, -1)
    ASEL(maskf[:, 128:256], maskf[:, 128:256], -1, 128, ALU.is_ge, -30000.0, 0, 1)
    for e_ in range(2):
        COPY("pool", maskb[:, e_, :], maskf)
        COPY("pool", maskb0[:, e_, :], maskf)
        MEMSET("pool", maskb0[:, e_, 0:128], -30000.0)
    MEMSET("pool", prevT[:], 0.0)
    MEMSET("pool", prev_bf[:], 0.0)
    MEMSET("pool", tails[:], 0.0)
    MEMSET("pool", kT[:], 0.0)
    MEMSET("pool", vtok[:], 0.0)
    COPY("pool", wdt[:], wdt_f)
    COPY("dve", bs_hi[:], bs_f[0:1, :])
    TT("dve", bs_lo[:], bs_f[0:1, :], bs_hi[:], ALU.subtract)
    ACT(aneg[:], bcv(BC_ALOG, 16), AF.Exp)
    TS("dve", aneg[:], aneg[:], -1.0, ALU.mult)
    TS("dve", negsink[:], bcv(BC_SINK, 16), -1.0, ALU.mult)
    for h_ in range(16):
        TS("dve", identD[:, h_, :], ident_f[:], bcv(BC_DSK + h_, 1), ALU.mult)
    for g in range(8):
        bk = nextbank()
        TR([(bk[:, 0:128], gmw_ld[:, g, :], ident_f[:])], writes=[bk[:, 0:128]])
        TT("dve", gmWT[:, g, :], bk[:, 0:128], triu_f[:], ALU.mult)

    order = ["h0", "h1", "h2", "h3", "full"]
    lvl = order.index(stage)
    NBU = {0: B_UP0, 1: B_QKV, 2: B_UP1, 3: NB, 4: NB}[lvl]
    total_w = NT * NBU
    ws = {"rec": 0, "sc": 0, "cast": 0}
    ch_ring = [P.chan() for _ in range(NW)]
    ch_ring_sw = [P.chan() for _ in range(NW)]
    ch_wst = [P.chan() for _ in range(NW)]
    cast_engs = ["act", "dve"]
    NQ = 4

    def wrec(j):
        tt, bb = divmod(j, NBU)
        slot = j % NW
        src_, kc, cols = blocks[bb]
        n = kc * cols
        dst = wring[slot][:, 0:n]
        if tt == 0:
            DMA(dst.rearrange("p (k c) -> p k c", c=cols), src_, ch_ring_sw[slot], eng="pool", issue=1.5)
            if B_WOUT <= bb < B_WOUT + 4:
                dv = dst.rearrange("p (k c) -> p k c", c=cols)
                for j in range(8):
                    TS("dve", dv[:, 8 + j, :], dv[:, 8 + j, :], ppc(PP_NG + j), ALU.mult)
            if NT > 1:
                DMA(wscr[bb, :, 0:n], dst, ch_wst[slot])
        else:
            DMA(dst, wscr[bb, :, 0:n], ch_ring[slot])

    def wget(t, b):
        idx = t * NBU + b
        lim = min(idx + NW, total_w)
        while ws["rec"] < lim:
            wrec(ws["rec"])
            ws["rec"] += 1
        _, kc, cols = blocks[b]
        return wring[idx % NW][:, 0:kc * cols].rearrange("p (k c) -> p k c", c=cols)

    ch_x = [P.chan() for _ in range(2)]
    ch_o = [P.chan() for _ in range(2)]
    ch_dbg = P.chan()
    sm = {"i": 0}

    def smalls(n):
        i = sm["i"]
        sm["i"] = (i + 1) % 16
        return small[:, i, 0:n]

    def load_x(t):
        for c in range(4):
            xs_ = xin[c % 2]
            DMA(xs_[:], x_d[t * 512 + c * 128:t * 512 + (c + 1) * 128, :], ch_x[c % 2])
            pr = pairs[c % 2]
            TR([(pr[:, j * 128:(j + 1) * 128], xs_[:, j * 128:(j + 1) * 128], ident_f[:]) for j in range(8)], writes=[pr[:]])
            cs = slice(c * 128, (c + 1) * 128)
            COPY("act", hT[:, 0:4, cs], pr[:, 0:512].rearrange("p (j t) -> p j t", t=128))
            COPY("dve", hT[:, 4:8, cs], pr[:, 512:1024].rearrange("p (j t) -> p j t", t=128))

    nrm = {"i": 0}

    def rmsnorm(gbase):
        bk = nextbank()
        for j in range(8):
            if j % 2 == 0:
                ACT(nsq[:, j, :], hT[:, j, :], AF.Square)
                TS("dve", yT[:, j, :], hT[:, j, :], ppc(gbase + j), ALU.mult)
            else:
                TT("dve", nsq[:, j, :], hT[:, j, :], hT[:, j, :], ALU.mult)
                ACT(yT[:, j, :], hT[:, j, :], AF.Copy, scale=ppc(gbase + j))
        MMG(bk, [(ones_bf[:], nsq[:, j, :]) for j in range(8)])
        ACT(bk, bk, AF.Ln, bias=EPS, scale=1.0 / 1024)
        ACT(bk, bk, AF.Exp, scale=-0.5)
        ACT(rstd_sb, bk, AF.Copy)

        def fin():
            for j in range(8):
                STT(yT[:, j, :], hT[:, j, :], ppc(gbase + j), bk, ALU.mult, ALU.mult)
        return fin

    def rescaled(bk):
        tmp = tmpn[nrm["i"] % 2]
        nrm["i"] += 1
        TT("dve", tmp, bk, rstd_sb, ALU.mult)
        return tmp

    def dump_or_final(t, final):
        for c in range(4):
            cs = slice(c * 128, (c + 1) * 128)
            pr = pairs[2 + c % 2]
            TR([(pr[:, j * 128:(j + 1) * 128], hT[:, j, cs], ident_f[:]) for j in range(8)], writes=[pr[:]])
            ob = osb[c % 2]
            if not final:
                COPY("act", ob[:, 0:512], pr[:, 0:512])
                COPY("dve", ob[:, 512:1024], pr[:, 512:1024])
            else:
                ss = smalls(4)
                ACT(junk[:], pr[:, 0:512], AF.Square, accum=ss[:, 0:1])
                ACT(junk[:], pr[:, 512:1024], AF.Square, accum=ss[:, 1:2])
                TT("dve", ss[:, 2:3], ss[:, 0:1], ss[:, 1:2], ALU.add)
                ACT(ss[:, 3:4], ss[:, 2:3], AF.Ln, bias=EPS, scale=1.0 / 1024)
                ACT(ss[:, 2:3], ss[:, 3:4], AF.Exp, scale=-0.5)
                STT(ob[:, 0:512], pr[:, 0:512], ss[:, 2:3], bcv(BC_GF, 512), ALU.mult, ALU.mult)
                STT(ob[:, 512:1024], pr[:, 512:1024], ss[:, 2:3], bcv(BC_GF + 512, 512), ALU.mult, ALU.mult)
            DMA(out_d[t * 512 + c * 128:t * 512 + (c + 1) * 128, :], ob[:], ch_o[c % 2])

    def layer0_mixer(t):
        fin = rmsnorm(PP_GMIX0)
        P.tag = 'l0.u'
        for bi in range(2):
            wv = wget(t, B_WIN + bi)
            bks = None
            if bi == 0:
                bks = kouter4(wv)
                fin()
            for o in range(4):
                oc = bi * 4 + o
                if bks is None:
                    bk = nextbank()
                    MMG(bk, [(wv[:, k, o * 128:(o + 1) * 128], yT[:, k, :]) for k in range(8)])
                else:
                    bk = rescaled(bks[o])
                ACT(guT[:, oc, :], bk, AF.Gelu_apprx_tanh)
        P.tag = 'l0.v'
        for bi in range(2):
            wv = wget(t, B_WIN + 2 + bi)
            for tc in range(4):
                bk = nextbank()
                MMG(bk, [(yT[:, k, tc * 128:(tc + 1) * 128], wv[:, k, :]) for k in range(8)])
                ACT(gv[:, tc, bi * 512:(bi + 1) * 512], bk, AF.Gelu_apprx_tanh)
                if bi == 1:
                    layernorm_chunk(tc)
        P.tag = 'l0.z'
        for bi in range(2):
            wv = wget(t, B_WIN + 4 + bi)
            for tc in range(4):
                bk = nextbank()
                MMG(bk, [(yT[:, k, tc * 128:(tc + 1) * 128], wv[:, k, :]) for k in range(8)])
                ACT(sz[:, tc, bi * 512:(bi + 1) * 512], bk, AF.Silu)
        bkd = nextbank()
        PE([(bkd[:, tc * 16:(tc + 1) * 16], yT[:, k, tc * 128:(tc + 1) * 128], wdt[:, k, :], k == 0, k == 7)
            for tc in range(4) for k in range(8)], writes=[bkd[:, 0:64]])
        dtx = smalls(64)
        dtv = dtx.rearrange("p (c h) -> p c h", h=16)
        TT("dve", dtv, bkd[:, 0:64].rearrange("p (c h) -> p c h", h=16),
           bcv(BC_DTB, 16).unsqueeze(1).to_broadcast([128, 4, 16]), ALU.add)
        ex = smalls(64)
        ACT(ex, dtx, AF.Exp)
        dt_all = smalls(64)
        ACT(dt_all, ex, AF.Ln, bias=1.0)
        dta_all = smalls(64)
        TT("dve", dta_all.rearrange("p (c h) -> p c h", h=16), dt_all.rearrange("p (c h) -> p c h", h=16),
           aneg[:].unsqueeze(1).to_broadcast([128, 4, 16]), ALU.mult)
        bka = nextbank()
        PE([(bka[:, tc * 16:(tc + 1) * 16], triu_f[:], dta_all[:, tc * 16:(tc + 1) * 16], True, True) for tc in range(4)]
           + [(bka[:, 64 + tc * 16:64 + (tc + 1) * 16], ones_f[:], dta_all[:, tc * 16:(tc + 1) * 16], True, True) for tc in range(4)],
           writes=[bka[:, 0:128]])
        acum = smalls(64)
        COPY("dve", acum, bka[:, 0:64])
        ea_all = smalls(64)
        ACT(ea_all, bka[:, 0:64], AF.Exp)
        cd_all = smalls(64)
        ACT(cd_all, bka[:, 64:128], AF.Exp)
        te_all = smalls(64)
        TT("dve", te_all, bka[:, 64:128], acum, ALU.subtract)
        ACT(te_all, te_all, AF.Exp)
        TT("dve", te_all, te_all, dt_all, ALU.mult)
        P.tag = 'l0.conv'
        wv_x = {}

        def conv_c1(oc):
            bi, o = divmod(oc, 4)
            if o == 0:
                wv_x[bi] = wget(t, B_WIN + 6 + bi)
            wv = wv_x[bi]
            bk = nextbank()
            MMG(bk, [(wv[:, k, o * 128:(o + 1) * 128], yT[:, k, :]) for k in range(8)])
            xs_ = xst[oc % 2]
            COPY("pool", xs_[:, 0:3], tails[:, oc, :])
            ACT(xs_[:, 3:515], bk, AF.Copy)
            ACT(acc[oc % 2], bk, AF.Identity, bias=ppc(PP_CONVB + oc), scale=ppc(PP_CONVW + 3 * 16 + oc))

        def conv_c2(oc):
            xs_ = xst[oc % 2]
            ac = acc[oc % 2]
            for tap in (2, 1, 0):
                STT(ac, xs_[:, tap:tap + 512], ppc(PP_CONVW + tap * 16 + oc), ac, ALU.mult, ALU.add)
            COPY("pool", tails[:, oc, :], xs_[:, 512:515])
            ACT(xcT[:, oc, :], ac, AF.Silu)

        conv_c1(0)
        for oc in range(16):
            if oc + 1 < 16:
                conv_c1(oc + 1)
            conv_c2(oc)

        P.tag = 'l0.gmlp'
        for g in range(8):
            bk = nextbank()
            mms = []
            for tc in range(4):
                o = bk[:, tc * 128:(tc + 1) * 128]
                mms.append((o, vn[:, tc, g * 128:(g + 1) * 128], gmWT[:, g, :], True, False))
                mms.append((o, ones_bf[0:1, :], bs_hi[0:1, g * 128:(g + 1) * 128], False, False))
                mms.append((o, ones_bf[0:1, :], bs_lo[0:1, g * 128:(g + 1) * 128], False, True))
            PE(mms, writes=[bk])
            TT("dve", mixT[:, g, :], bk, guT[:, g, :], ALU.mult)
        P.tag = 'l0.ssd'
        hs = lambda a_: a_.rearrange("p (h d) -> p h d", d=64)

        def bc16(ap16):
            return ap16.unsqueeze(2).to_broadcast([128, 16, 64])

        def ssd_s1(tc):
            i = tc % 2
            cs = slice(tc * 128, (tc + 1) * 128)
            b0 = bank16(0)
            TR([(b0[:, j * 128:(j + 1) * 128], xcT[:, j, cs], ident_bf[:]) for j in range(8)], writes=[b0])
            COPY("act", xtok[i], b0)
            b1 = bank16(1)
            TR([(b1[:, g * 128:(g + 1) * 128], xcT[:, 8 + g, cs], ident_bf[:]) for g in range(4)], writes=[b1[:, 0:512]])
            COPY("act", Btok[i], b1[:, 0:512])
            TT("dve", hs(xw[i]), hs(xtok[i]), bc16(te_all[:, tc * 16:(tc + 1) * 16]), ALU.mult)
            b2 = bank(2)
            PE([(b2[:, g * 128:(g + 1) * 128], xcT[:, 8 + g, cs], xcT[:, 12 + g, cs], True, True) for g in range(4)], writes=[b2])
            TT("dve", cbm, b2.rearrange("p (g l) -> p g l", l=128), triu_f[:].unsqueeze(1).to_broadcast([128, 4, 128]), ALU.mult)

            def rbuild(g):
                rb = Rb[g % 2]
                for hh in range(4):
                    h = 4 * g + hh
                    sc_ = dta_all[:, tc * 16 + h:tc * 16 + h + 1]
                    if hh == 3:
                        ACT(rb[:, hh, :], triu_f[:], AF.Copy, scale=sc_)
                    else:
                        TS("dve", rb[:, hh, :], triu_f[:], sc_, ALU.mult)
            rbuild(0)
            rbuild(1)
            for g in range(4):
                rb = Rb[g % 2]
                sb_ = bank(3) if g % 2 == 0 else bank(2)
                PE([(sb_, slow_f[:], rb.rearrange("p h l -> p (h l)"), True, True)], writes=[sb_])
                ACT(sb_, sb_, AF.Exp)
                for hh in range(4):
                    h = 4 * g + hh
                    STT(wT[i][:, h, :], sb_[:, hh * 128:(hh + 1) * 128], dt_all[:, tc * 16 + h:tc * 16 + h + 1], cbm[:, g, :], ALU.mult, ALU.mult)
                if g + 2 < 4:
                    rbuild(g + 2)

        def ssd_s2(tc):
            i = tc % 2
            cs = slice(tc * 128, (tc + 1) * 128)
            Y = pairs[2]
            mms = []
            for h in range(16):
                o = Y[:, h * 64:(h + 1) * 64]
                mms.append((o, wT[i][:, h, :], xtok[i][:, h * 64:(h + 1) * 64], True, False))
                mms.append((o, identD[:, h, :], xtok[i][:, h * 64:(h + 1) * 64], False, True))
            PE(mms, writes=[Y[:]])
            Z = pairs[3]
            PE([(Z[:, g * 256:(g + 1) * 256], xcT[:, 12 + g, cs], prev_bf[:, g * 256:(g + 1) * 256], True, True) for g in range(4)], writes=[Z[:]])
            TT("dve", hs(t1), hs(Z[:]), bc16(ea_all[:, tc * 16:(tc + 1) * 16]), ALU.mult)
            TT("dve", yv, Y[:], t1, ALU.add)
            TT("dve", yv, yv, sz[:, tc, :], ALU.mult)
            ss4 = smalls(16)
            for g in range(4):
                ACT(junk[:, 0:256], yv[:, g * 256:(g + 1) * 256], AF.Square, accum=ss4[:, g:g + 1])
            ACT(ss4[:, 4:8], ss4[:, 0:4], AF.Ln, bias=EPS, scale=1.0 / 256)
            ACT(ss4[:, 8:12], ss4[:, 4:8], AF.Exp, scale=-0.5)
            for g in range(4):
                ACT(bout[i][:, g * 256:(g + 1) * 256], yv[:, g * 256:(g + 1) * 256], AF.Copy, scale=ss4[:, 8 + g:9 + g])

        def ssd_s3(tc):
            i = tc % 2
            cs = slice(tc * 128, (tc + 1) * 128)
            Sx = pairs[3]
            PE([(Sx[:, g * 256:(g + 1) * 256], Btok[i][:, g * 128:(g + 1) * 128], xw[i][:, g * 256:(g + 1) * 256], True, True) for g in range(4)], writes=[Sx[:]])
            TT("dve", hs(prevT[:]), hs(prevT[:]), bc16(cd_all[:, tc * 16:(tc + 1) * 16]), ALU.mult)
            TT("dve", prevT[:], Sx[:], prevT[:], ALU.add)
            COPY("act", prev_bf[:], prevT[:])
            b0 = bank16(0)
            TR([(b0[:, j * 128:(j + 1) * 128], bout[i][:, j * 128:(j + 1) * 128], ident_bf[:]) for j in range(8)], writes=[b0])
            COPY("act", mixT[:, 8:16, cs], b0.rearrange("p (j t) -> p j t", t=128))

        ssd_s1(0)
        for tc in range(4):
            if tc + 1 < 4:
                ssd_s1(tc + 1)
            ssd_s2(tc)
            ssd_s3(tc)
        if dbg_d is not None:
            DMA(dbg_d[t].rearrange("j p t -> p j t"), mixT, ch_dbg)
        P.tag = 'l0.wout'
        for i in range(4):
            wv = wget(t, B_WOUT + i)
            for o in range(2):
                oc = 2 * i + o
                bk = nextbank()
                MMG(bk, [(wv[:, k, o * 128:(o + 1) * 128], mixT[:, k, :]) for k in range(16)])
                TT("dve", hT[:, oc, :], bk, hT[:, oc, :], ALU.add)

    def layernorm_chunk(tc):
        g_ = gv[:, tc, :]
        st6 = smalls(12)
        P.op("dve", lambda e: e.bn_stats(out=st6[:, 0:6], in_=g_[:, 0:512]), reads=[g_[:, 0:512]], writes=[st6[:, 0:6]], dur=0.65, single=True)
        P.op("dve", lambda e: e.bn_stats(out=st6[:, 6:12], in_=g_[:, 512:1024]), reads=[g_[:, 512:1024]], writes=[st6[:, 6:12]], dur=0.65, single=True)
        mv = smalls(4)
        P.op("dve", lambda e: e.bn_aggr(out=mv[:, 0:2], in_=st6), reads=[st6], writes=[mv[:, 0:2]], dur=0.15, single=True)
        ACT(mv[:, 2:3], mv[:, 1:2], AF.Sqrt, bias=EPS)
        P.op("dve", lambda e: e.reciprocal(out=mv[:, 3:4], in_=mv[:, 2:3]), reads=[mv[:, 2:3]], writes=[mv[:, 3:4]], dur=0.17, single=True)
        TS("dve", g_, g_, mv[:, 0:1], ALU.subtract, mv[:, 3:4], ALU.mult)
        TT("dve", g_, g_, bcv(BC_LNG, 1024), ALU.mult)
        TT("dve", vn[:, tc, :], g_, bcv(BC_LNB, 1024), ALU.add)

    def mlp(t, l):
        fin = rmsnorm(PP_GMLP0 if l == 0 else PP_GMLP1)
        bu = B_UP0 if l == 0 else B_UP1
        bd = B_DN0 if l == 0 else B_DN1
        for i in range(8):
            wv = wget(t, bu + i)
            bks = None
            if i == 0:
                bks = kouter4(wv)
                fin()
            for o in range(4):
                f = 4 * i + o
                if bks is None:
                    bk = nextbank()
                    MMG(bk, [(wv[:, k, o * 128:(o + 1) * 128], yT[:, k, :]) for k in range(8)])
                else:
                    bk = rescaled(bks[o])
                r = r32[f % 2]
                ACT(r, bk, AF.Relu)
                TT("dve", hid[:, f, :], r, r, ALU.mult)
        for i in range(8):
            wv = wget(t, bd + i)
            bk = nextbank()
            MMG(bk, [(wv[:, k, :], hid[:, k, :]) for k in range(32)])
            TT("dve", hT[:, i, :], bk, hT[:, i, :], ALU.add)

    def layer1_mixer(t):
        fin = rmsnorm(PP_GMIX1)
        COPY("pool", kT[:, :, 0:128], kT[:, :, 512:640])
        COPY("pool", vtok[:, 0, :], vtok[:, 4, :])
        for bi in range(2):
            wv = wget(t, B_QKV + bi)
            bks = None
            if bi == 0:
                bks = kouter4(wv)
                fin()
            for o in range(4):
                hp = bi * 4 + o
                if bks is None:
                    bk = nextbank()
                    MMG(bk, [(wv[:, k, o * 128:(o + 1) * 128], yT[:, k, :]) for k in range(8)])
                else:
                    bk = rescaled(bks[o])
                ACT(qT[:, hp, :], bk, AF.Identity, bias=ppc(PP_BQ + hp))
        wv = wget(t, B_QKV + 2)
        for kv in range(2):
            bk = nextbank()
            MMG(bk, [(wv[:, k, kv * 128:(kv + 1) * 128], yT[:, k, :]) for k in range(8)])
            ACT(kT[:, kv, 128:640], bk, AF.Identity, bias=ppc(PP_BK + kv))
        bk = nextbank()
        PE([(bk[:, tc * 128:(tc + 1) * 128], yT[:, k, tc * 128:(tc + 1) * 128], wv[:, k, 256:384], k == 0, k == 7)
            for tc in range(4) for k in range(8)], writes=[bk])
        TT("dve", vtok[:, 1:5, :], bk.rearrange("p (c d) -> p c d", d=128),
           bcv(BC_BV, 128).unsqueeze(1).to_broadcast([128, 4, 128]), ALU.add)
        P.tag = 'l1.attn'
        items = [(tc, hp) for tc in range(4) for hp in range(8)]
        sls = {}

        def sc_pair(n):
            return pairs[0] if n % 2 == 0 else pairs[3]

        def att_a1(n):
            tc, hp = items[n]
            cs = slice(tc * 128, (tc + 1) * 128)
            first = (t == 0 and tc == 0)
            mk_ = maskb0 if first else maskb
            kv = hp // 4
            pr = sc_pair(n)
            mms = []
            for e_ in range(2):
                mms.append((pr[:, e_ * 512:e_ * 512 + 256], qT[64 * e_:64 * e_ + 64, hp, cs],
                            kT[64 * e_:64 * e_ + 64, kv, tc * 128:tc * 128 + 256], True, False))
            for e_ in range(2):
                mms.append((pr[:, e_ * 512:e_ * 512 + 256], ident_bf[:], mk_[:, e_, :], False, True))
            PE(mms, writes=[pr[:]])
            sls[n] = smalls(16)

        def att_b125(n):
            tc, hp = items[n]
            sl = sls[n]
            pr = sc_pair(n)
            sc_v = pr[:].rearrange("p (e s) -> p e s", s=512)[:, :, 0:256]
            P.op("dve", (lambda o_, i_: lambda e: e.tensor_reduce(out=o_, in_=i_, axis=AX.X, op=ALU.max))(sl[:, 0:2], sc_v),
                 reads=[pr[:]], writes=[sl[:, 0:2]], dur=0.67, single=True)
            STT(sl[:, 2:4], sl[:, 0:2], -0.125, negsink[:, 2 * hp:2 * hp + 2], ALU.mult, ALU.min)
            TT("dve", sl[:, 6:8], bcv(BC_SINK + 2 * hp, 2), sl[:, 2:4], ALU.add)

        def att_c35(n):
            tc, hp = items[n]
            sl = sls[n]
            pr = sc_pair(n)
            pb = Pb[n % 2]
            for e_ in range(2):
                ACT(pb[:, e_, :], pr[:, e_ * 512:e_ * 512 + 256], AF.Exp, bias=sl[:, 2 + e_:3 + e_], scale=0.125,
                    accum=sl[:, 4 + e_:5 + e_])
            ACT(sl[:, 8:10], sl[:, 6:8], AF.Exp)

        def att_a4(n):
            pb = Pb[n % 2]
            ptb = bank16(2 + n % 2)
            TR([(ptb[:, (2 * e_ + hf) * 128:(2 * e_ + hf + 1) * 128], pb[:, e_, hf * 128:(hf + 1) * 128], ident_bf[:])
                for e_ in range(2) for hf in range(2)], writes=[ptb[:, 0:512]])

        def att_copy(n):
            ptb = bank16(2 + n % 2)
            COPY("dve", PT[n % 2], ptb[:, 0:512].rearrange("p (a b) -> p a b", b=128))

        def att_b67(n):
            sl = sls[n]
            TT("dve", sl[:, 10:12], sl[:, 8:10], sl[:, 4:6], ALU.add)
            P.op("dve", (lambda o_, i_: lambda e: e.reciprocal(out=o_, in_=i_))(sl[:, 12:14], sl[:, 10:12]),
                 reads=[sl[:, 10:12]], writes=[sl[:, 12:14]], dur=0.17, single=True)

        def att_a7(n):
            tc, hp = items[n]
            kv = hp // 4
            pt = PT[n % 2]
            O = pairs[2]
            mms = []
            for e_ in range(2):
                h = 2 * hp + e_
                o = O[:, h * 64:(h + 1) * 64]
                for hf in range(2):
                    mms.append((o, pt[:, 2 * e_ + hf, :], vtok[:, tc + hf, kv * 64:(kv + 1) * 64], hf == 0, hf == 1))
            PE(mms, writes=[O[:, 2 * hp * 64:(2 * hp + 2) * 64]])

        def att_b8(n):
            tc, hp = items[n]
            cs = slice(tc * 128, (tc + 1) * 128)
            sl = sls[n]
            O = pairs[2]
            at = attn[tc % 2]
            TT("dve", at[:, 2 * hp * 64:(2 * hp + 2) * 64].rearrange("p (e d) -> p e d", d=64),
               O[:, 2 * hp * 64:(2 * hp + 2) * 64].rearrange("p (e d) -> p e d", d=64),
               sl[:, 12:14].unsqueeze(2).to_broadcast([128, 2, 64]), ALU.mult)
            if hp == 7:
                b6 = bank16(3)
                TR([(b6[:, j * 128:(j + 1) * 128], at[:, j * 128:(j + 1) * 128], ident_bf[:]) for j in range(8)], writes=[b6])
                COPY("act", aoT[:, :, cs], b6.rearrange("p (j t) -> p j t", t=128))

        NI = len(items)
        for k in range(NI + 2):
            if k < NI:
                att_a1(k)
            if 0 <= k - 2 < NI:
                att_a7(k - 2)
            if 0 <= k - 1 < NI:
                att_a4(k - 1)
            if k < NI:
                att_b125(k)
                att_c35(k)
            if 0 <= k - 1 < NI:
                att_b67(k - 1)
            if 0 <= k - 2 < NI:
                att_b8(k - 2)
            if 0 <= k - 1 < NI:
                att_copy(k - 1)
        if dbg_d is not None and stage == "h2":
            DMA(dbg_d[t, 0:8].rearrange("j p t -> p j t"), aoT, ch_dbg)
        P.tag = 'l1.wo'
        for bi in range(2):
            wv = wget(t, B_WO + bi)
            for o in range(4):
                oc = bi * 4 + o
                bk = nextbank()
                MMG(bk, [(wv[:, k, o * 128:(o + 1) * 128], aoT[:, k, :]) for k in range(8)])
                STT(hT[:, oc, :], bk, ppc(PP_BO + oc), hT[:, oc, :], ALU.add, ALU.add)

    for t in range(NT):
        hT = hTs[0]
        P.tag = "load_x"
        load_x(t)
        P.tag = "l0"
        layer0_mixer(t)
        if lvl >= 1:
            P.tag = "mlp0"
            mlp(t, 0)
        if lvl >= 2:
            P.tag = "l1"
            layer1_mixer(t)
        if lvl >= 3:
            P.tag = "mlp1"
            mlp(t, 1)
        P.tag = "final"
        dump_or_final(t, final=(lvl == 4))

    P.sbuf_left = nc.sbuf_bytes_remaining
    P.emit(nc, st)
    st.close()
    return nc, P


def _host_inputs(inp):
    f = lambda k: np.ascontiguousarray(np.asarray(inp[k], dtype=np.float32))
    col = lambda v: np.ascontiguousarray(v.reshape(-1, 128).T)
    bq = f("b_qkv")[0]
    k0, k1 = bq[1024:1088], bq[1088:1152]
    pp = np.concatenate([
        col(f("norm_mix_g")[0]), col(f("norm_mlp_g")[0]), col(f("norm_mix_g")[1]), col(f("norm_mlp_g")[1]),
        np.concatenate([col(f("ssm_conv_w")[0][i]) for i in range(4)], axis=1),
        col(f("ssm_conv_b")[0]),
        col(bq[0:1024]),
        np.stack([np.concatenate([k0, k0]), np.concatenate([k1, k1])], axis=1),
        col(f("b_o")[0]),
        col(f("ssm_norm_g")[0]),
    ], axis=1)
    assert pp.shape == (128, NPP), pp.shape
    bc = np.concatenate([
        f("gm_ln_g")[0], f("gm_ln_b")[0], f("final_norm_g"),
        bq[1152:1280], f("ssm_dt_bias")[0], f("ssm_a_log")[0], f("ssm_d")[0], f("attn_sinks")[0],
    ])[None, :]
    assert bc.shape == (1, NBC), bc.shape
    wq = f("w_qkv")[0]
    wq_ext = np.concatenate([wq[:, 0:1024], wq[:, 1024:1088], wq[:, 1024:1088], wq[:, 1088:1152], wq[:, 1088:1152],
                             wq[:, 1152:1280]], axis=1)
    shared = {
        "w_in": f("w_in_even")[0], "w_out": f("w_out_even")[0], "w_qkv": np.ascontiguousarray(wq_ext),
        "w_o": f("w_o")[0], "w_up": f("w_up"), "w_down": f("w_down"),
        "pp": np.ascontiguousarray(pp), "bc": np.ascontiguousarray(bc),
        "gmw": f("gm_w_s")[0], "gmb": f("gm_b_s")[0].reshape(1, 1024),
    }
    return shared


def kernel(**inputs):
    x = np.asarray(inputs["x"], dtype=np.float32)
    shared = _host_inputs(inputs)
    nc, _ = build(NT=8, stage="full")
    in_maps = []
    for b in range(8):
        m = dict(shared)
        m["x"] = np.ascontiguousarray(x[b])
        in_maps.append(m)
    res = run_bass_kernel_spmd(nc, in_maps, core_ids=list(range(8)))
    return np.stack([np.asarray(res.results[b]["out"], dtype=np.float32) for b in range(8)], axis=0)
```

```python
import numpy as np
from contextlib import ExitStack
import concourse.bass as bass
import concourse.mybir as mybir
from concourse.bass_utils import run_bass_kernel_spmd

F32 = mybir.dt.float32
BF16 = mybir.dt.bfloat16
AF = mybir.ActivationFunctionType
ALU = mybir.AluOpType
AX = mybir.AxisListType

ENGS = ["pe", "act", "dve", "pool", "sp"]
_ESZ = {F32: 4, BF16: 2}
_PAGE = {"SB": 256, "PSUM": 2048, "DRAM": 1 << 18}


class Buf:
    __slots__ = ("w", "r")

    def __init__(self):
        self.w = None
        self.r = []


class Chan:
    def __init__(self, key):
        self.key = key
        self.last = None
        self.count = 0


def _space(ap):
    s = str(ap.space)
    if "PSUM" in s:
        return "PSUM"
    if "DRAM" in s or "HBM" in s:
        return "DRAM"
    return "SB"


class Prog:
    SAME_ENGINE_SYNC = True
    SCHEDULE = True
    FUSE_WAIT = True
    VC_PRUNE = True
    CPW = 20.0
    LAT = 0.25

    def __init__(self):
        self.recs = []
        self.tags = []
        self.single = []
        self.tag = 'setup'
        self.chans = []
        self.bufs = {}
        self.nops = 0
        self.ops = {e: [] for e in ENGS}

    def chan(self):
        c = Chan("ch%d" % len(self.chans))
        self.chans.append(c)
        return c

    def regs(self, ap):
        sp = _space(ap)
        esz = _ESZ[ap.dtype]
        pat = ap.ap
        off = ap.offset
        if sp == "DRAM":
            f0 = off
            dims = pat
        else:
            pstep = pat[0][0]
            f0 = off % pstep if pstep > 0 else off
            dims = pat[1:]
        ext = 0
        for s, c in dims:
            ext += abs(s) * (c - 1)
        lo = f0 * esz
        hi = (f0 + ext) * esz + esz - 1
        pg = _PAGE[sp]
        name = ap.tensor.name
        out = []
        for p in range(lo // pg, hi // pg + 1):
            k = (name, p)
            b = self.bufs.get(k)
            if b is None:
                b = self.bufs[k] = Buf()
            out.append(b)
        return out

    def op(self, eng, fn, reads=(), writes=(), chan=None, dur=0.3, xfer=0.0, single=False):
        rb = []
        wb = []
        for a in reads:
            (wb if _space(a) == "PSUM" else rb).extend(self.regs(a))
        for a in writes:
            wb.extend(self.regs(a))
        oid = len(self.recs)
        deps = set()
        for b in rb:
            if b.w is not None:
                deps.add(b.w)
        for b in wb:
            if b.w is not None:
                deps.add(b.w)
            deps.update(b.r)
        if chan is not None:
            if chan.last is not None:
                deps.add(chan.last)
            chan.last = oid
        deps.discard(oid)
        for b in rb:
            b.r.append(oid)
        for b in wb:
            b.w = oid
            b.r = []
        self.recs.append((eng, fn, deps, chan, dur, xfer))
        self.tags.append(self.tag)
        self.single.append(single)
        self.ops[eng].append(oid)
        self.nops += 1
        return oid

    def schedule(self):
        import heapq
        recs = self.recs
        n = len(recs)
        self.sched_seq = []
        if not self.SCHEDULE:
            order = {e: list(self.ops[e]) for e in ENGS}
            self.sched_seq = list(range(n))
            return order, 0.0
        succ = [[] for _ in range(n)]
        indeg = [0] * n
        for i, r in enumerate(recs):
            indeg[i] = len(r[2])
            for d in r[2]:
                succ[d].append(i)
        blev = [0.0] * n
        for i in range(n - 1, -1, -1):
            m_ = 0.0
            for s_ in succ[i]:
                if blev[s_] > m_:
                    m_ = blev[s_]
            blev[i] = m_ + recs[i][4] + recs[i][5] + 0.2
        prio = [(i - self.CPW * blev[i], i) for i in range(n)]
        ready_t = [0.0] * n
        pend = {e: [] for e in ENGS}
        avail = {e: [] for e in ENGS}
        free = {e: 0.0 for e in ENGS}
        order = {e: [] for e in ENGS}
        for i in range(n):
            if indeg[i] == 0:
                heapq.heappush(pend[recs[i][0]], (0.0, i))
        done = 0
        tend = 0.0
        while done < n:
            best = None
            for e in ENGS:
                pe_, av = pend[e], avail[e]
                while pe_ and pe_[0][0] <= free[e]:
                    heapq.heappush(av, prio[heapq.heappop(pe_)[1]])
                if av:
                    cand = (free[e], av[0][1], e, True)
                elif pe_:
                    cand = (pe_[0][0], pe_[0][1], e, False)
                else:
                    continue
                if best is None or cand[:2] < best[:2]:
                    best = cand
            start, i, e, from_av = best
            if from_av:
                heapq.heappop(avail[e])
            else:
                heapq.heappop(pend[e])
            eng, fn, deps, chan, dur, xfer = recs[i]
            end_eng = start + dur
            end = end_eng + xfer
            free[e] = end_eng
            order[e].append(i)
            self.sched_seq.append(i)
            tend = max(tend, end)
            done += 1
            for s_ in succ[i]:
                lat = self.LAT if (recs[s_][0] != e or chan is not None) else 0.05
                t_ = end + lat
                if t_ > ready_t[s_]:
                    ready_t[s_] = t_
                indeg[s_] -= 1
                if indeg[s_] == 0:
                    heapq.heappush(pend[recs[s_][0]], (ready_t[s_], s_))
        return order, tend

    def simulate(self, order):
        recs = self.recs
        endt = [None] * len(recs)
        pos = {e: 0 for e in ENGS}
        free = {e: 0.0 for e in ENGS}
        busy = {e: 0.0 for e in ENGS}
        remaining = sum(len(order[e]) for e in ENGS)
        tend = 0.0
        while remaining:
            progressed = False
            for e in ENGS:
                while pos[e] < len(order[e]):
                    i = order[e][pos[e]]
                    eng, fn, deps, chan, dur, xfer = recs[i]
                    t = free[e]
                    ok = True
                    for d in deps:
                        if endt[d] is None:
                            ok = False
                            break
                        lat = self.LAT if (recs[d][0] != e or recs[d][3] is not None) else 0.05
                        t = max(t, endt[d] + lat)
                    if not ok:
                        break
                    free[e] = t + dur
                    busy[e] += dur
                    endt[i] = t + dur + xfer
                    tend = max(tend, endt[i])
                    pos[e] += 1
                    remaining -= 1
                    progressed = True
            assert progressed, "deadlock in simulated order"
        return tend, busy

    def emit(self, nc, st):
        order, tend = self.schedule()
        self.sim_us, self.busy_us = self.simulate(order)
        self.predicted_us = tend
        recs = self.recs
        tok = [None] * len(recs)
        cnt = {e: 0 for e in ENGS}
        ccnt = {c.key: 0 for c in self.chans}
        for i, r in enumerate(recs):
            if r[3] is not None:
                ccnt[r[3].key] += 1
                tok[i] = (r[3].key, ccnt[r[3].key] * 16)
        for e in ENGS:
            for i in order[e]:
                if recs[i][3] is None:
                    cnt[e] += 1
                    tok[i] = (e, cnt[e])
        sems = {}
        for e in ENGS:
            sems[e] = st.enter_context(nc.semaphore("s_" + e))
        for c in self.chans:
            sems[c.key] = st.enter_context(nc.semaphore("s_" + c.key))
        block = st.enter_context(nc.Block())
        finals = {}
        for e in ENGS:
            if cnt[e] > 0:
                finals[e] = cnt[e]
        for c in self.chans:
            if ccnt[c.key] > 0:
                finals[c.key] = ccnt[c.key] * 16

        seq = self.sched_seq if len(self.sched_seq) == len(recs) else list(range(len(recs)))
        kn = {e: {} for e in ENGS}
        vc = [None] * len(recs)
        fw = [None] * len(recs)
        for i in seq:
            e_, fn, deps, chan, dur, xfer = recs[i]
            K_ = kn[e_]
            cand = {}
            for d in deps:
                key, val = tok[d]
                if key == e_ and chan is None and recs[d][3] is None:
                    if e_ == "pe" or not self.SAME_ENGINE_SYNC:
                        continue
                if cand.get(key, (0, 0))[0] < val:
                    cand[key] = (val, d)
            wts = []
            for key, (val, d) in sorted(cand.items(), key=lambda kv: -kv[1][1]):
                if K_.get(key, 0) >= val:
                    continue
                wts.append((key, val))
                if self.VC_PRUNE:
                    for k2, v2 in vc[d].items():
                        if K_.get(k2, 0) < v2:
                            K_[k2] = v2
                if K_.get(key, 0) < val:
                    K_[key] = val
            fw[i] = sorted(wts)
            v_ = dict(K_)
            if v_.get(tok[i][0], 0) < tok[i][1]:
                v_[tok[i][0]] = tok[i][1]
            vc[i] = v_
        self.n_waits = sum(len(w) for w in fw)

        def mk(ename):
            def body(eng):
                for i in order[ename]:
                    e_, fn, deps, chan, dur, xfer = recs[i]
                    wl = list(fw[i])
                    fused = wl.pop() if (wl and self.single[i] and self.FUSE_WAIT) else None
                    for k, v in wl:
                        eng.wait_ge(sems[k], v)
                    if fused is not None and self.single[i] == "first":
                        ins = fn(eng, (sems[fused[0]], fused[1]))
                    else:
                        ins = fn(eng)
                        if fused is not None:
                            ins._wait_ge(sems[fused[0]], fused[1])
                    if chan is not None:
                        ins.then_inc(sems[chan.key], 16)
                    else:
                        ins.then_inc(sems[ename], 1)
                if ename == "sp":
                    for k, v in finals.items():
                        eng.wait_ge(sems[k], v)
            return body

        block.tensor(mk("pe"))
        block.scalar(mk("act"))
        block.vector(mk("dve"))
        block.gpsimd(mk("pool"))
        block.sync(mk("sp"))


def _isap(x):
    return hasattr(x, "tensor") and hasattr(x, "ap")


PP_GMIX0, PP_GMLP0, PP_GMIX1, PP_GMLP1 = 0, 8, 16, 24
PP_CONVW = 32
PP_CONVB = 96
PP_BQ = 112
PP_BK = 120
PP_BO = 122
PP_NG = 130
NPP = 138
BC_LNG, BC_LNB, BC_GF = 0, 1024, 2048
BC_BV = 3072
BC_DTB, BC_ALOG, BC_DSK, BC_SINK = 3200, 3216, 3232, 3248
NBC = 3264

NB = 51
NW = 4
EPS = 1e-5


def build(NT=8, stage="full"):
    nc = bass.Bass("TRN2", target_bir_lowering=False)
    st = ExitStack()
    P = Prog()

    def dram(name, shape, dt, kind):
        return nc.dram_tensor(name, shape, dt, kind=kind).ap()

    def sb(name, shape, dt):
        return st.enter_context(nc.sbuf_tensor(name, shape, dt))

    x_d = dram("x", [4096, 1024], F32, "ExternalInput")
    out_d = dram("out", [4096, 1024], F32, "ExternalOutput")
    w_in_d = dram("w_in", [1024, 5136], F32, "ExternalInput")
    w_out_d = dram("w_out", [2048, 1024], F32, "ExternalInput")
    w_qkv_d = dram("w_qkv", [1024, 1408], F32, "ExternalInput")
    w_o_d = dram("w_o", [1024, 1024], F32, "ExternalInput")
    w_up_d = dram("w_up", [2, 1024, 4096], F32, "ExternalInput")
    w_down_d = dram("w_down", [2, 4096, 1024], F32, "ExternalInput")
    pp_d = dram("pp", [128, NPP], F32, "ExternalInput")
    bc_d = dram("bc", [1, NBC], F32, "ExternalInput")
    gmw_d = dram("gmw", [8, 128, 128], F32, "ExternalInput")
    gmb_d = dram("gmb", [1, 1024], F32, "ExternalInput")
    wscr = dram("wscr", [NB, 128, 4096], BF16, "Internal")
    dbg_d = None
    if stage != "full":
        dbg_d = dram("dbg", [NT, 16, 128, 512], BF16, "ExternalOutput")

    blocks = []

    def addblk(ap2d, kc, cols):
        blocks.append((ap2d.rearrange("(kc p) c -> p kc c", p=128), kc, cols))

    for i in range(10):
        addblk(w_in_d[:, 512 * i:512 * (i + 1)], 8, 512)
    for i in range(4):
        addblk(w_out_d[:, 256 * i:256 * (i + 1)], 16, 256)
    for i in range(8):
        addblk(w_up_d[0, :, 512 * i:512 * (i + 1)], 8, 512)
    for i in range(8):
        addblk(w_down_d[0, :, 128 * i:128 * (i + 1)], 32, 128)
    addblk(w_qkv_d[:, 0:512], 8, 512)
    addblk(w_qkv_d[:, 512:1024], 8, 512)
    addblk(w_qkv_d[:, 1024:1408], 8, 384)
    for i in range(2):
        addblk(w_o_d[:, 512 * i:512 * (i + 1)], 8, 512)
    for i in range(8):
        addblk(w_up_d[1, :, 512 * i:512 * (i + 1)], 8, 512)
    for i in range(8):
        addblk(w_down_d[1, :, 128 * i:128 * (i + 1)], 32, 128)
    assert len(blocks) == NB
    B_WIN, B_WOUT, B_UP0, B_DN0, B_QKV, B_WO, B_UP1, B_DN1 = 0, 10, 14, 22, 30, 33, 35, 43

    hTs = [sb("hT0", [128, 8, 512], F32)]
    hT = hTs[0]
    yT = sb("yT", [128, 8, 512], BF16)
    wring = [sb("wring%d" % i, [128, 4096], BF16) for i in range(NW)]
    xin = [sb("xin%d" % i, [128, 1024], F32) for i in range(2)]
    osb = [sb("osb%d" % i, [128, 1024], F32) for i in range(2)]
    pp = sb("pp_sb", [128, NPP], F32)
    bc = sb("bc_sb", [128, 1, NBC], F32)
    ident_bf = sb("ident_bf", [128, 128], BF16)
    ident_f = sb("ident_f", [128, 128], F32)
    ones_bf = sb("ones_bf", [128, 128], BF16)
    ones_f = sb("ones_f", [128, 128], F32)
    triu_f = sb("triu_f", [128, 128], F32)
    slow_f = sb("slow_f", [128, 128], F32)
    maskb = sb("maskb", [128, 2, 256], BF16)
    maskb0 = sb("maskb0", [128, 2, 256], BF16)
    gmWT = sb("gmWT", [128, 8, 128], BF16)
    bs_hi = sb("bs_hi", [1, 1024], BF16)
    bs_lo = sb("bs_lo", [1, 1024], BF16)
    wdt = sb("wdt", [128, 8, 16], BF16)
    identD = sb("identD", [128, 16, 128], BF16)
    aneg = sb("aneg", [128, 16], F32)
    negsink = sb("negsink", [128, 16], F32)
    prevT = sb("prevT", [128, 1024], F32)
    prev_bf = sb("prev_bf", [128, 1024], BF16)
    tails = sb("tails", [128, 16, 3], F32)
    kT = sb("kT", [128, 2, 640], BF16)
    vtok = sb("vtok", [128, 5, 128], BF16)
    small = sb("small", [128, 16, 64], F32)
    junk = sb("junk", [128, 512], BF16)

    SCR_BYTES = 84 * 1024
    scr = sb("scr", [128, SCR_BYTES // 2], BF16)

    def sview(off, shape, dt):
        n = 1
        for s in shape:
            n *= s
        nb = n * _ESZ[dt]
        assert off % 4 == 0 and off + nb <= SCR_BYTES, (off, nb)
        v = scr[:, off // 2:(off + nb) // 2]
        if dt == F32:
            v = v.bitcast(F32)
        if len(shape) == 2:
            return v.rearrange("p (a b) -> p a b", b=shape[1])
        if len(shape) == 3:
            return v.rearrange("p (a b c) -> p a b c", b=shape[1], c=shape[2])
        return v

    K = 1024
    mixT = sview(0, [16, 512], BF16)
    xcT = sview(16 * K, [16, 512], BF16)
    sz = sview(32 * K, [4, 1024], BF16)
    xst = [sview(40 * K + i * 2304, [576], F32) for i in range(2)]
    acc = [sview(45 * K + i * 2048, [512], F32) for i in range(2)]
    guT = sview(49 * K, [8, 512], BF16)
    vn = sview(57 * K, [4, 1024], BF16)
    gv = sview(65 * K, [4, 1024], F32)
    gmw_ld = sview(0, [8, 128], F32)
    bs_f = sview(4 * K, [1024], F32)
    wdt_f = sview(8 * K, [8, 16], F32)
    maskf = sview(9 * K, [256], F32)
    yv = sview(40 * K, [1024], F32)
    t1 = sview(45 * K, [1024], F32)
    xtok = [sview(49 * K + i * 2048, [1024], BF16) for i in range(2)]
    xw = [sview(59 * K + i * 2048, [1024], BF16) for i in range(2)]
    Btok = [sview(63 * K + i * 1024, [512], BF16) for i in range(2)]
    cbm = sview(65 * K, [4, 128], F32)
    Rb = [sview(67 * K + i * 2048, [4, 128], F32) for i in range(2)]
    wT = [sview(71 * K + i * 4096, [16, 128], BF16) for i in range(2)]
    bout = [sview(79 * K + i * 2048, [1024], BF16) for i in range(2)]
    nsq = sview(66 * K, [8, 512], BF16)
    rstd_sb = sview(74 * K, [512], F32)
    tmpn = [sview(76 * K + i * 2048, [512], F32) for i in range(2)]
    hid = sview(0, [32, 512], BF16)
    r32 = [sview(32 * K + i * 2048, [512], F32) for i in range(2)]
    qT = sview(0, [8, 512], BF16)
    aoT = sview(8 * K, [8, 512], BF16)
    Pb = [sview(16 * K + i * 1024, [2, 256], BF16) for i in range(2)]
    PT = [sview(18 * K + i * 1024, [4, 128], BF16) for i in range(2)]
    attn = [sview(20 * K + i * 2048, [1024], BF16) for i in range(2)]

    pairs = [st.enter_context(nc.psum_tensor("ps%d" % i, [128, 1024], F32)) for i in range(4)]

    def bank(b):
        return pairs[b // 2][:, (b % 2) * 512:(b % 2 + 1) * 512]

    def bank16(b):
        return pairs[b // 2][:].bitcast(BF16)[:, (b % 2) * 1024:(b % 2 + 1) * 1024]

    rr = {"b": 0}

    def nextbank():
        b = rr["b"]
        rr["b"] = (b + 1) % 8
        return bank(b)

    def nfree(ap):
        n = 1
        for s in ap.shape[1:]:
            n *= s
        return n

    def c_act(ap, accum=False):
        return 0.2 + nfree(ap) * 0.00085 + (0.1 if accum else 0.0)

    def c_dve(ap, k=1.0):
        return 0.07 + nfree(ap) * 0.00100 * k

    def c_mm(rhs, lhsT):
        n_ = nfree(rhs)
        f = 4.0 if rhs.dtype == F32 else 1.0
        return 0.025 + max(n_, 64) * 0.00042 * f

    def DMA(out, in_, chan, eng="sp", issue=0.1):
        nbytes = nfree(out) * out.shape[0] * max(_ESZ[out.dtype], _ESZ[in_.dtype])
        P.op(eng, lambda e: e.dma_start(out=out, in_=in_), reads=[in_], writes=[out], chan=chan,
             dur=issue, xfer=2.0 + nbytes / 200e3, single=(eng == "sp"))

    def ACT(out, in_, func, bias=None, scale=None, accum=None):
        kw = {}
        rd = [in_]
        wr = [out]
        if bias is not None:
            kw["bias"] = bias
            if _isap(bias):
                rd.append(bias)
        if scale is not None:
            kw["scale"] = scale
            if _isap(scale):
                rd.append(scale)
        if accum is not None:
            kw["accum_out"] = accum
            wr.append(accum)
        P.op("act", lambda e: e.activation(out=out, in_=in_, func=func, **kw), reads=rd, writes=wr,
             dur=c_act(in_, accum is not None), single=True)

    def COPY(eng, out, in_):
        if eng == "act":
            ACT(out, in_, AF.Copy)
        else:
            P.op(eng, lambda e: e.tensor_copy(out=out, in_=in_), reads=[in_], writes=[out],
                 dur=c_dve(in_, 3.0 if eng == "pool" else 1.0), single=True)

    def TT(eng, out, in0, in1, op):
        P.op(eng, lambda e: e.tensor_tensor(out=out, in0=in0, in1=in1, op=op), reads=[in0, in1], writes=[out],
             dur=c_dve(out, 2.5 if eng == "pool" else 1.05), single=True)

    def TS(eng, out, in0, s1, op0, s2=None, op1=None):
        rd = [in0] + [s for s in (s1, s2) if _isap(s)]
        d_ = c_dve(out, 10.0 if eng == "pool" else 1.0)
        if op1 is None:
            P.op(eng, lambda e: e.tensor_scalar(out=out, in0=in0, scalar1=s1, scalar2=None, op0=op0), reads=rd, writes=[out], dur=d_, single=True)
        else:
            P.op(eng, lambda e: e.tensor_scalar(out=out, in0=in0, scalar1=s1, scalar2=s2, op0=op0, op1=op1), reads=rd, writes=[out], dur=d_, single=True)

    def STT(out, in0, scalar, in1, op0, op1):
        rd = [in0, in1] + ([scalar] if _isap(scalar) else [])
        both_sb = _space(in0) == "SB" and _space(in1) == "SB"
        P.op("dve", lambda e: e.scalar_tensor_tensor(out=out, in0=in0, scalar=scalar, in1=in1, op0=op0, op1=op1), reads=rd, writes=[out],
             dur=c_dve(out, 1.25 if both_sb else 1.0), single=True)

    def DVEOP(fn, reads, writes, dur):
        P.op("dve", fn, reads=reads, writes=writes, dur=dur)

    def MEMSET(eng, ap, val):
        P.op(eng, lambda e: e.memset(ap, val), writes=[ap], dur=c_dve(ap, 2.0))

    def ASEL(out, in_, step, n, cmp, fill, base, cm):
        P.op("pool", lambda e: e.affine_select(out=out, in_=in_, pattern=[[step, n]], compare_op=cmp, fill=fill, base=base, channel_multiplier=cm), reads=[in_], writes=[out], dur=0.5)

    def PE(mms, extra_reads=(), writes=None):
        rd = list(extra_reads)
        wr = []
        d_ = 0.02
        for (o, l, r, s0, s1) in mms:
            rd.append(l)
            rd.append(r)
            wr.append(o)
            d_ += c_mm(r, l)
        if writes is not None:
            wr = writes

        def fn(e, w=None):
            ins = None
            for n_, (o, l, r, s0, s1) in enumerate(mms):
                ins = e.matmul(o, lhsT=l, rhs=r, start=s0, stop=s1)
                if n_ == 0 and w is not None:
                    ins._wait_ge(w[0], w[1])
            return ins
        P.op("pe", fn, reads=rd, writes=wr, dur=d_, single="first")

    def MMG(out, pairs_):
        n = len(pairs_)
        PE([(out, l, r, i == 0, i == n - 1) for i, (l, r) in enumerate(pairs_)])

    def TR(items, writes):
        rd = []
        d_ = 0.05
        for (o, i_, idn) in items:
            rd.append(i_)
            rd.append(idn)
            d_ += 0.11 * (2.0 if i_.dtype == F32 else 1.0)

        def fn(e, w=None):
            ins = None
            for n_, (o, i_, idn) in enumerate(items):
                ins = e.transpose(out=o, in_=i_, identity=idn)
                if n_ == 0 and w is not None:
                    ins._wait_ge(w[0], w[1])
            return ins
        P.op("pe", fn, reads=rd, writes=writes, dur=d_, single="first")

    def kouter4(wv, nk=8):
        bks = [nextbank() for _ in range(4)]
        for k in range(nk):
            PE([(bks[o], wv[:, k, o * 128:(o + 1) * 128], yT[:, k, :], k == 0, k == nk - 1) for o in range(4)], writes=bks)
        return bks

    ppc = lambda c: pp[:, c:c + 1]
    bcv = lambda c, n: bc[:, 0, c:c + n]

    ch_setup = [P.chan() for _ in range(4)]
    DMA(pp[:], pp_d[:, :], ch_setup[0])
    DMA(bc[:], bc_d.partition_broadcast(128), ch_setup[1])
    DMA(gmw_ld, gmw_d.rearrange("g t s -> t g s"), ch_setup[2])
    DMA(bs_f[0:1, :], gmb_d[:, :], ch_setup[3])
    DMA(wdt_f, w_in_d[:, 5120:5136].rearrange("(kc p) c -> p kc c", p=128), ch_setup[0])

    MEMSET("pool", ident_f[:], 1.0)
    ASEL(ident_f[:], ident_f[:], -1, 128, ALU.is_equal, 0.0, 0, 1)
    COPY("pool", ident_bf[:], ident_f[:])
    MEMSET("pool", ones_f[:], 1.0)
    MEMSET("pool", ones_bf[:], 1.0)
    MEMSET("pool", triu_f[:], 1.0)
    ASEL(triu_f[:], triu_f[:], 1, 128, ALU.is_ge, 0.0, 0, -1)
    MEMSET("pool", slow_f[:], 1.0)
    ASEL(slow_f[:], slow_f[:], -1, 128, ALU.is_ge, 0.0, -1, 1)
    MEMSET("pool", maskf, 0.0)
    ASEL(maskf[:, 0:128], maskf[:, 0:128], 1, 128, ALU.is_ge, -30000.0, -1, -1)
    ASEL(maskf[:, 128:256], maskf[:, 128:256], -1, 128, ALU.is_ge, -30000.0, 0, 1)
    for e_ in range(2):
        COPY("pool", maskb[:, e_, :], maskf)
        COPY("pool", maskb0[:, e_, :], maskf)
        MEMSET("pool", maskb0[:, e_, 0:128], -30000.0)
    MEMSET("pool", prevT[:], 0.0)
    MEMSET("pool", prev_bf[:], 0.0)
    MEMSET("pool", tails[:], 0.0)
    MEMSET("pool", kT[:], 0.0)
    MEMSET("pool", vtok[:], 0.0)
    COPY("pool", wdt[:], wdt_f)
    COPY("dve", bs_hi[:], bs_f[0:1, :])
    TT("dve", bs_lo[:], bs_f[0:1, :], bs_hi[:], ALU.subtract)
    ACT(aneg[:], bcv(BC_ALOG, 16), AF.Exp)
    TS("dve", aneg[:], aneg[:], -1.0, ALU.mult)
    TS("dve", negsink[:], bcv(BC_SINK, 16), -1.0, ALU.mult)
    for h_ in range(16):
        TS("dve", identD[:, h_, :], ident_f[:], bcv(BC_DSK + h_, 1), ALU.mult)
    for g in range(8):
        bk = nextbank()
        TR([(bk[:, 0:128], gmw_ld[:, g, :], ident_f[:])], writes=[bk[:, 0:128]])
        TT("dve", gmWT[:, g, :], bk[:, 0:128], triu_f[:], ALU.mult)

    order = ["h0", "h1", "h2", "h3", "full"]
    lvl = order.index(stage)
    NBU = {0: B_UP0, 1: B_QKV, 2: B_UP1, 3: NB, 4: NB}[lvl]
    total_w = NT * NBU
    ws = {"rec": 0, "sc": 0, "cast": 0}
    ch_ring = [P.chan() for _ in range(NW)]
    ch_ring_sw = [P.chan() for _ in range(NW)]
    ch_wst = [P.chan() for _ in range(NW)]
    cast_engs = ["act", "dve"]
    NQ = 4

    def wrec(j):
        tt, bb = divmod(j, NBU)
        slot = j % NW
        src_, kc, cols = blocks[bb]
        n = kc * cols
        dst = wring[slot][:, 0:n]
        if tt == 0:
            DMA(dst.rearrange("p (k c) -> p k c", c=cols), src_, ch_ring_sw[slot], eng="pool", issue=1.5)
            if B_WOUT <= bb < B_WOUT + 4:
                dv = dst.rearrange("p (k c) -> p k c", c=cols)
                for j in range(8):
                    TS("dve", dv[:, 8 + j, :], dv[:, 8 + j, :], ppc(PP_NG + j), ALU.mult)
            if NT > 1:
                DMA(wscr[bb, :, 0:n], dst, ch_wst[slot])
        else:
            DMA(dst, wscr[bb, :, 0:n], ch_ring[slot])

    def wget(t, b):
        idx = t * NBU + b
        lim = min(idx + NW, total_w)
        while ws["rec"] < lim:
            wrec(ws["rec"])
            ws["rec"] += 1
        _, kc, cols = blocks[b]
        return wring[idx % NW][:, 0:kc * cols].rearrange("p (k c) -> p k c", c=cols)

    ch_x = [P.chan() for _ in range(2)]
    ch_o = [P.chan() for _ in range(2)]
    ch_dbg = P.chan()
    sm = {"i": 0}

    def smalls(n):
        i = sm["i"]
        sm["i"] = (i + 1) % 16
        return small[:, i, 0:n]

    def load_x(t):
        for c in range(4):
            xs_ = xin[c % 2]
            DMA(xs_[:], x_d[t * 512 + c * 128:t * 512 + (c + 1) * 128, :], ch_x[c % 2])
            pr = pairs[c % 2]
            TR([(pr[:, j * 128:(j + 1) * 128], xs_[:, j * 128:(j + 1) * 128], ident_f[:]) for j in range(8)], writes=[pr[:]])
            cs = slice(c * 128, (c + 1) * 128)
            COPY("act", hT[:, 0:4, cs], pr[:, 0:512].rearrange("p (j t) -> p j t", t=128))
            COPY("dve", hT[:, 4:8, cs], pr[:, 512:1024].rearrange("p (j t) -> p j t", t=128))

    nrm = {"i": 0}

    def rmsnorm(gbase):
        bk = nextbank()
        for j in range(8):
            if j % 2 == 0:
                ACT(nsq[:, j, :], hT[:, j, :], AF.Square)
                TS("dve", yT[:, j, :], hT[:, j, :], ppc(gbase + j), ALU.mult)
            else:
                TT("dve", nsq[:, j, :], hT[:, j, :], hT[:, j, :], ALU.mult)
                ACT(yT[:, j, :], hT[:, j, :], AF.Copy, scale=ppc(gbase + j))
        MMG(bk, [(ones_bf[:], nsq[:, j, :]) for j in range(8)])
        ACT(bk, bk, AF.Ln, bias=EPS, scale=1.0 / 1024)
        ACT(bk, bk, AF.Exp, scale=-0.5)
        ACT(rstd_sb, bk, AF.Copy)

        def fin():
            for j in range(8):
                STT(yT[:, j, :], hT[:, j, :], ppc(gbase + j), bk, ALU.mult, ALU.mult)
        return fin

    def rescaled(bk):
        tmp = tmpn[nrm["i"] % 2]
        nrm["i"] += 1
        TT("dve", tmp, bk, rstd_sb, ALU.mult)
        return tmp

    def dump_or_final(t, final):
        for c in range(4):
            cs = slice(c * 128, (c + 1) * 128)
            pr = pairs[2 + c % 2]
            TR([(pr[:, j * 128:(j + 1) * 128], hT[:, j, cs], ident_f[:]) for j in range(8)], writes=[pr[:]])
            ob = osb[c % 2]
            if not final:
                COPY("act", ob[:, 0:512], pr[:, 0:512])
                COPY("dve", ob[:, 512:1024], pr[:, 512:1024])
            else:
                ss = smalls(4)
                ACT(junk[:], pr[:, 0:512], AF.Square, accum=ss[:, 0:1])
                ACT(junk[:], pr[:, 512:1024], AF.Square, accum=ss[:, 1:2])
                TT("dve", ss[:, 2:3], ss[:, 0:1], ss[:, 1:2], ALU.add)
                ACT(ss[:, 3:4], ss[:, 2:3], AF.Ln, bias=EPS, scale=1.0 / 1024)
                ACT(ss[:, 2:3], ss[:, 3:4], AF.Exp, scale=-0.5)
                STT(ob[:, 0:512], pr[:, 0:512], ss[:, 2:3], bcv(BC_GF, 512), ALU.mult, ALU.mult)
                STT(ob[:, 512:1024], pr[:, 512:1024], ss[:, 2:3], bcv(BC_GF + 512, 512), ALU.mult, ALU.mult)
            DMA(out_d[t * 512 + c * 128:t * 512 + (c + 1) * 128, :], ob[:], ch_o[c % 2])

    def layer0_mixer(t):
        fin = rmsnorm(PP_GMIX0)
        P.tag = 'l0.u'
        for bi in range(2):
            wv = wget(t, B_WIN + bi)
            bks = None
            if bi == 0:
                bks = kouter4(wv)
                fin()
            for o in range(4):
                oc = bi * 4 + o
                if bks is None:
                    bk = nextbank()
                    MMG(bk, [(wv[:, k, o * 128:(o + 1) * 128], yT[:, k, :]) for k in range(8)])
                else:
                    bk = rescaled(bks[o])
                ACT(guT[:, oc, :], bk, AF.Gelu_apprx_tanh)
        P.tag = 'l0.v'
        for bi in range(2):
            wv = wget(t, B_WIN + 2 + bi)
            for tc in range(4):
                bk = nextbank()
                MMG(bk, [(yT[:, k, tc * 128:(tc + 1) * 128], wv[:, k, :]) for k in range(8)])
                ACT(gv[:, tc, bi * 512:(bi + 1) * 512], bk, AF.Gelu_apprx_tanh)
                if bi == 1:
                    layernorm_chunk(tc)
        P.tag = 'l0.z'
        for bi in range(2):
            wv = wget(t, B_WIN + 4 + bi)
            for tc in range(4):
                bk = nextbank()
                MMG(bk, [(yT[:, k, tc * 128:(tc + 1) * 128], wv[:, k, :]) for k in range(8)])
                ACT(sz[:, tc, bi * 512:(bi + 1) * 512], bk, AF.Silu)
        bkd = nextbank()
        PE([(bkd[:, tc * 16:(tc + 1) * 16], yT[:, k, tc * 128:(tc + 1) * 128], wdt[:, k, :], k == 0, k == 7)
            for tc in range(4) for k in range(8)], writes=[bkd[:, 0:64]])
        dtx = smalls(64)
        dtv = dtx.rearrange("p (c h) -> p c h", h=16)
        TT("dve", dtv, bkd[:, 0:64].rearrange("p (c h) -> p c h", h=16),
           bcv(BC_DTB, 16).unsqueeze(1).to_broadcast([128, 4, 16]), ALU.add)
        ex = smalls(64)
        ACT(ex, dtx, AF.Exp)
        dt_all = smalls(64)
        ACT(dt_all, ex, AF.Ln, bias=1.0)
        dta_all = smalls(64)
        TT("dve", dta_all.rearrange("p (c h) -> p c h", h=16), dt_all.rearrange("p (c h) -> p c h", h=16),
           aneg[:].unsqueeze(1).to_broadcast([128, 4, 16]), ALU.mult)
        bka = nextbank()
        PE([(bka[:, tc * 16:(tc + 1) * 16], triu_f[:], dta_all[:, tc * 16:(tc + 1) * 16], True, True) for tc in range(4)]
           + [(bka[:, 64 + tc * 16:64 + (tc + 1) * 16], ones_f[:], dta_all[:, tc * 16:(tc + 1) * 16], True, True) for tc in range(4)],
           writes=[bka[:, 0:128]])
        acum = smalls(64)
        COPY("dve", acum, bka[:, 0:64])
        ea_all = smalls(64)
        ACT(ea_all, bka[:, 0:64], AF.Exp)
        cd_all = smalls(64)
        ACT(cd_all, bka[:, 64:128], AF.Exp)
        te_all = smalls(64)
        TT("dve", te_all, bka[:, 64:128], acum, ALU.subtract)
        ACT(te_all, te_all, AF.Exp)
        TT("dve", te_all, te_all, dt_all, ALU.mult)
        P.tag = 'l0.conv'
        wv_x = {}

        def conv_c1(oc):
            bi, o = divmod(oc, 4)
            if o == 0:
                wv_x[bi] = wget(t, B_WIN + 6 + bi)
            wv = wv_x[bi]
            bk = nextbank()
            MMG(bk, [(wv[:, k, o * 128:(o + 1) * 128], yT[:, k, :]) for k in range(8)])
            xs_ = xst[oc % 2]
            COPY("pool", xs_[:, 0:3], tails[:, oc, :])
            ACT(xs_[:, 3:515], bk, AF.Copy)
            ACT(acc[oc % 2], bk, AF.Identity, bias=ppc(PP_CONVB + oc), scale=ppc(PP_CONVW + 3 * 16 + oc))

        def conv_c2(oc):
            xs_ = xst[oc % 2]
            ac = acc[oc % 2]
            for tap in (2, 1, 0):
                STT(ac, xs_[:, tap:tap + 512], ppc(PP_CONVW + tap * 16 + oc), ac, ALU.mult, ALU.add)
            COPY("pool", tails[:, oc, :], xs_[:, 512:515])
            ACT(xcT[:, oc, :], ac, AF.Silu)

        conv_c1(0)
        for oc in range(16):
            if oc + 1 < 16:
                conv_c1(oc + 1)
            conv_c2(oc)

        P.tag = 'l0.gmlp'
        for g in range(8):
            bk = nextbank()
            mms = []
            for tc in range(4):
                o = bk[:, tc * 128:(tc + 1) * 128]
                mms.append((o, vn[:, tc, g * 128:(g + 1) * 128], gmWT[:, g, :], True, False))
                mms.append((o, ones_bf[0:1, :], bs_hi[0:1, g * 128:(g + 1) * 128], False, False))
                mms.append((o, ones_bf[0:1, :], bs_lo[0:1, g * 128:(g + 1) * 128], False, True))
            PE(mms, writes=[bk])
            TT("dve", mixT[:, g, :], bk, guT[:, g, :], ALU.mult)
        P.tag = 'l0.ssd'
        hs = lambda a_: a_.rearrange("p (h d) -> p h d", d=64)

        def bc16(ap16):
            return ap16.unsqueeze(2).to_broadcast([128, 16, 64])

        def ssd_s1(tc):
            i = tc % 2
            cs = slice(tc * 128, (tc + 1) * 128)
            b0 = bank16(0)
            TR([(b0[:, j * 128:(j + 1) * 128], xcT[:, j, cs], ident_bf[:]) for j in range(8)], writes=[b0])
            COPY("act", xtok[i], b0)
            b1 = bank16(1)
            TR([(b1[:, g * 128:(g + 1) * 128], xcT[:, 8 + g, cs], ident_bf[:]) for g in range(4)], writes=[b1[:, 0:512]])
            COPY("act", Btok[i], b1[:, 0:512])
            TT("dve", hs(xw[i]), hs(xtok[i]), bc16(te_all[:, tc * 16:(tc + 1) * 16]), ALU.mult)
            b2 = bank(2)
            PE([(b2[:, g * 128:(g + 1) * 128], xcT[:, 8 + g, cs], xcT[:, 12 + g, cs], True, True) for g in range(4)], writes=[b2])
            TT("dve", cbm, b2.rearrange("p (g l) -> p g l", l=128), triu_f[:].unsqueeze(1).to_broadcast([128, 4, 128]), ALU.mult)

            def rbuild(g):
                rb = Rb[g % 2]
                for hh in range(4):
                    h = 4 * g + hh
                    sc_ = dta_all[:, tc * 16 + h:tc * 16 + h + 1]
                    if hh == 3:
                        ACT(rb[:, hh, :], triu_f[:], AF.Copy, scale=sc_)
                    else:
                        TS("dve", rb[:, hh, :], triu_f[:], sc_, ALU.mult)
            rbuild(0)
            rbuild(1)
            for g in range(4):
                rb = Rb[g % 2]
                sb_ = bank(3) if g % 2 == 0 else bank(2)
                PE([(sb_, slow_f[:], rb.rearrange("p h l -> p (h l)"), True, True)], writes=[sb_])
                ACT(sb_, sb_, AF.Exp)
                for hh in range(4):
                    h = 4 * g + hh
                    STT(wT[i][:, h, :], sb_[:, hh * 128:(hh + 1) * 128], dt_all[:, tc * 16 + h:tc * 16 + h + 1], cbm[:, g, :], ALU.mult, ALU.mult)
                if g + 2 < 4:
                    rbuild(g + 2)

        def ssd_s2(tc):
            i = tc % 2
            cs = slice(tc * 128, (tc + 1) * 128)
            Y = pairs[2]
            mms = []
            for h in range(16):
                o = Y[:, h * 64:(h + 1) * 64]
                mms.append((o, wT[i][:, h, :], xtok[i][:, h * 64:(h + 1) * 64], True, False))
                mms.append((o, identD[:, h, :], xtok[i][:, h * 64:(h + 1) * 64], False, True))
            PE(mms, writes=[Y[:]])
            Z = pairs[3]
            PE([(Z[:, g * 256:(g + 1) * 256], xcT[:, 12 + g, cs], prev_bf[:, g * 256:(g + 1) * 256], True, True) for g in range(4)], writes=[Z[:]])
            TT("dve", hs(t1), hs(Z[:]), bc16(ea_all[:, tc * 16:(tc + 1) * 16]), ALU.mult)
            TT("dve", yv, Y[:], t1, ALU.add)
            TT("dve", yv, yv, sz[:, tc, :], ALU.mult)
            ss4 = smalls(16)
            for g in range(4):
                ACT(junk[:, 0:256], yv[:, g * 256:(g + 1) * 256], AF.Square, accum=ss4[:, g:g + 1])
            ACT(ss4[:, 4:8], ss4[:, 0:4], AF.Ln, bias=EPS, scale=1.0 / 256)
            ACT(ss4[:, 8:12], ss4[:, 4:8], AF.Exp, scale=-0.5)
            for g in range(4):
                ACT(bout[i][:, g * 256:(g + 1) * 256], yv[:, g * 256:(g + 1) * 256], AF.Copy, scale=ss4[:, 8 + g:9 + g])

        def ssd_s3(tc):
            i = tc % 2
            cs = slice(tc * 128, (tc + 1) * 128)
            Sx = pairs[3]
            PE([(Sx[:, g * 256:(g + 1) * 256], Btok[i][:, g * 128:(g + 1) * 128], xw[i][:, g * 256:(g + 1) * 256], True, True) for g in range(4)], writes=[Sx[:]])
            TT("dve", hs(prevT[:]), hs(prevT[:]), bc16(cd_all[:, tc * 16:(tc + 1) * 16]), ALU.mult)
            TT("dve", prevT[:], Sx[:], prevT[:], ALU.add)
            COPY("act", prev_bf[:], prevT[:])
            b0 = bank16(0)
            TR([(b0[:, j * 128:(j + 1) * 128], bout[i][:, j * 128:(j + 1) * 128], ident_bf[:]) for j in range(8)], writes=[b0])
            COPY("act", mixT[:, 8:16, cs], b0.rearrange("p (j t) -> p j t", t=128))

        ssd_s1(0)
        for tc in range(4):
            if tc + 1 < 4:
                ssd_s1(tc + 1)
            ssd_s2(tc)
            ssd_s3(tc)
        if dbg_d is not None:
            DMA(dbg_d[t].rearrange("j p t -> p j t"), mixT, ch_dbg)
        P.tag = 'l0.wout'
        for i in range(4):
            wv = wget(t, B_WOUT + i)
            for o in range(2):
                oc = 2 * i + o
                bk = nextbank()
                MMG(bk, [(wv[:, k, o * 128:(o + 1) * 128], mixT[:, k, :]) for k in range(16)])
                TT("dve", hT[:, oc, :], bk, hT[:, oc, :], ALU.add)

    def layernorm_chunk(tc):
        g_ = gv[:, tc, :]
        st6 = smalls(12)
        P.op("dve", lambda e: e.bn_stats(out=st6[:, 0:6], in_=g_[:, 0:512]), reads=[g_[:, 0:512]], writes=[st6[:, 0:6]], dur=0.65, single=True)
        P.op("dve", lambda e: e.bn_stats(out=st6[:, 6:12], in_=g_[:, 512:1024]), reads=[g_[:, 512:1024]], writes=[st6[:, 6:12]], dur=0.65, single=True)
        mv = smalls(4)
        P.op("dve", lambda e: e.bn_aggr(out=mv[:, 0:2], in_=st6), reads=[st6], writes=[mv[:, 0:2]], dur=0.15, single=True)
        ACT(mv[:, 2:3], mv[:, 1:2], AF.Sqrt, bias=EPS)
        P.op("dve", lambda e: e.reciprocal(out=mv[:, 3:4], in_=mv[:, 2:3]), reads=[mv[:, 2:3]], writes=[mv[:, 3:4]], dur=0.17, single=True)
        TS("dve", g_, g_, mv[:, 0:1], ALU.subtract, mv[:, 3:4], ALU.mult)
        TT("dve", g_, g_, bcv(BC_LNG, 1024), ALU.mult)
        TT("dve", vn[:, tc, :], g_, bcv(BC_LNB, 1024), ALU.add)

    def mlp(t, l):
        fin = rmsnorm(PP_GMLP0 if l == 0 else PP_GMLP1)
        bu = B_UP0 if l == 0 else B_UP1
        bd = B_DN0 if l == 0 else B_DN1
        for i in range(8):
            wv = wget(t, bu + i)
            bks = None
            if i == 0:
                bks = kouter4(wv)
                fin()
            for o in range(4):
                f = 4 * i + o
                if bks is None:
                    bk = nextbank()
                    MMG(bk, [(wv[:, k, o * 128:(o + 1) * 128], yT[:, k, :]) for k in range(8)])
                else:
                    bk = rescaled(bks[o])
                r = r32[f % 2]
                ACT(r, bk, AF.Relu)
                TT("dve", hid[:, f, :], r, r, ALU.mult)
        for i in range(8):
            wv = wget(t, bd + i)
            bk = nextbank()
            MMG(bk, [(wv[:, k, :], hid[:, k, :]) for k in range(32)])
            TT("dve", hT[:, i, :], bk, hT[:, i, :], ALU.add)

    def layer1_mixer(t):
        fin = rmsnorm(PP_GMIX1)
        COPY("pool", kT[:, :, 0:128], kT[:, :, 512:640])
        COPY("pool", vtok[:, 0, :], vtok[:, 4, :])
        for bi in range(2):
            wv = wget(t, B_QKV + bi)
            bks = None
            if bi == 0:
                bks = kouter4(wv)
                fin()
            for o in range(4):
                hp = bi * 4 + o
                if bks is None:
                    bk = nextbank()
                    MMG(bk, [(wv[:, k, o * 128:(o + 1) * 128], yT[:, k, :]) for k in range(8)])
                else:
                    bk = rescaled(bks[o])
                ACT(qT[:, hp, :], bk, AF.Identity, bias=ppc(PP_BQ + hp))
        wv = wget(t, B_QKV + 2)
        for kv in range(2):
            bk = nextbank()
            MMG(bk, [(wv[:, k, kv * 128:(kv + 1) * 128], yT[:, k, :]) for k in range(8)])
            ACT(kT[:, kv, 128:640], bk, AF.Identity, bias=ppc(PP_BK + kv))
        bk = nextbank()
        PE([(bk[:, tc * 128:(tc + 1) * 128], yT[:, k, tc * 128:(tc + 1) * 128], wv[:, k, 256:384], k == 0, k == 7)
            for tc in range(4) for k in range(8)], writes=[bk])
        TT("dve", vtok[:, 1:5, :], bk.rearrange("p (c d) -> p c d", d=128),
           bcv(BC_BV, 128).unsqueeze(1).to_broadcast([128, 4, 128]), ALU.add)
        P.tag = 'l1.attn'
        items = [(tc, hp) for tc in range(4) for hp in range(8)]
        sls = {}

        def sc_pair(n):
            return pairs[0] if n % 2 == 0 else pairs[3]

        def att_a1(n):
            tc, hp = items[n]
            cs = slice(tc * 128, (tc + 1) * 128)
            first = (t == 0 and tc == 0)
            mk_ = maskb0 if first else maskb
            kv = hp // 4
            pr = sc_pair(n)
            mms = []
            for e_ in range(2):
                mms.append((pr[:, e_ * 512:e_ * 512 + 256], qT[64 * e_:64 * e_ + 64, hp, cs],
                            kT[64 * e_:64 * e_ + 64, kv, tc * 128:tc * 128 + 256], True, False))
            for e_ in range(2):
                mms.append((pr[:, e_ * 512:e_ * 512 + 256], ident_bf[:], mk_[:, e_, :], False, True))
            PE(mms, writes=[pr[:]])
            sls[n] = smalls(16)

        def att_b125(n):
            tc, hp = items[n]
            sl = sls[n]
            pr = sc_pair(n)
            sc_v = pr[:].rearrange("p (e s) -> p e s", s=512)[:, :, 0:256]
            P.op("dve", (lambda o_, i_: lambda e: e.tensor_reduce(out=o_, in_=i_, axis=AX.X, op=ALU.max))(sl[:, 0:2], sc_v),
                 reads=[pr[:]], writes=[sl[:, 0:2]], dur=0.67, single=True)
            STT(sl[:, 2:4], sl[:, 0:2], -0.125, negsink[:, 2 * hp:2 * hp + 2], ALU.mult, ALU.min)
            TT("dve", sl[:, 6:8], bcv(BC_SINK + 2 * hp, 2), sl[:, 2:4], ALU.add)

        def att_c35(n):
            tc, hp = items[n]
            sl = sls[n]
            pr = sc_pair(n)
            pb = Pb[n % 2]
            for e_ in range(2):
                ACT(pb[:, e_, :], pr[:, e_ * 512:e_ * 512 + 256], AF.Exp, bias=sl[:, 2 + e_:3 + e_], scale=0.125,
                    accum=sl[:, 4 + e_:5 + e_])
            ACT(sl[:, 8:10], sl[:, 6:8], AF.Exp)

        def att_a4(n):
            pb = Pb[n % 2]
            ptb = bank16(2 + n % 2)
            TR([(ptb[:, (2 * e_ + hf) * 128:(2 * e_ + hf + 1) * 128], pb[:, e_, hf * 128:(hf + 1) * 128], ident_bf[:])
                for e_ in range(2) for hf in range(2)], writes=[ptb[:, 0:512]])

        def att_copy(n):
            ptb = bank16(2 + n % 2)
            COPY("dve", PT[n % 2], ptb[:, 0:512].rearrange("p (a b) -> p a b", b=128))

        def att_b67(n):
            sl = sls[n]
            TT("dve", sl[:, 10:12], sl[:, 8:10], sl[:, 4:6], ALU.add)
            P.op("dve", (lambda o_, i_: lambda e: e.reciprocal(out=o_, in_=i_))(sl[:, 12:14], sl[:, 10:12]),
                 reads=[sl[:, 10:12]], writes=[sl[:, 12:14]], dur=0.17, single=True)

        def att_a7(n):
            tc, hp = items[n]
            kv = hp // 4
            pt = PT[n % 2]
            O = pairs[2]
            mms = []
            for e_ in range(2):
                h = 2 * hp + e_
                o = O[:, h * 64:(h + 1) * 64]
                for hf in range(2):
                    mms.append((o, pt[:, 2 * e_ + hf, :], vtok[:, tc + hf, kv * 64:(kv + 1) * 64], hf == 0, hf == 1))
            PE(mms, writes=[O[:, 2 * hp * 64:(2 * hp + 2) * 64]])

        def att_b8(n):
            tc, hp = items[n]
            cs = slice(tc * 128, (tc + 1) * 128)
            sl = sls[n]
            O = pairs[2]
            at = attn[tc % 2]
            TT("dve", at[:, 2 * hp * 64:(2 * hp + 2) * 64].rearrange("p (e d) -> p e d", d=64),
               O[:, 2 * hp * 64:(2 * hp + 2) * 64].rearrange("p (e d) -> p e d", d=64),
               sl[:, 12:14].unsqueeze(2).to_broadcast([128, 2, 64]), ALU.mult)
            if hp == 7:
                b6 = bank16(3)
                TR([(b6[:, j * 128:(j + 1) * 128], at[:, j * 128:(j + 1) * 128], ident_bf[:]) for j in range(8)], writes=[b6])
                COPY("act", aoT[:, :, cs], b6.rearrange("p (j t) -> p j t", t=128))

        NI = len(items)
        for k in range(NI + 2):
            if k < NI:
                att_a1(k)
            if 0 <= k - 2 < NI:
                att_a7(k - 2)
            if 0 <= k - 1 < NI:
                att_a4(k - 1)
            if k < NI:
                att_b125(k)
                att_c35(k)
            if 0 <= k - 1 < NI:
                att_b67(k - 1)
            if 0 <= k - 2 < NI:
                att_b8(k - 2)
            if 0 <= k - 1 < NI:
                att_copy(k - 1)
        if dbg_d is not None and stage == "h2":
            DMA(dbg_d[t, 0:8].rearrange("j p t -> p j t"), aoT, ch_dbg)
        P.tag = 'l1.wo'
        for bi in range(2):
            wv = wget(t, B_WO + bi)
            for o in range(4):
                oc = bi * 4 + o
                bk = nextbank()
                MMG(bk, [(wv[:, k, o * 128:(o + 1) * 128], aoT[:, k, :]) for k in range(8)])
                STT(hT[:, oc, :], bk, ppc(PP_BO + oc), hT[:, oc, :], ALU.add, ALU.add)

    for t in range(NT):
        hT = hTs[0]
        P.tag = "load_x"
        load_x(t)
        P.tag = "l0"
        layer0_mixer(t)
        if lvl >= 1:
            P.tag = "mlp0"
            mlp(t, 0)
        if lvl >= 2:
            P.tag = "l1"
            layer1_mixer(t)
        if lvl >= 3:
            P.tag = "mlp1"
            mlp(t, 1)
        P.tag = "final"
        dump_or_final(t, final=(lvl == 4))

    P.sbuf_left = nc.sbuf_bytes_remaining
    P.emit(nc, st)
    st.close()
    return nc, P


def _host_inputs(inp):
    f = lambda k: np.ascontiguousarray(np.asarray(inp[k], dtype=np.float32))
    col = lambda v: np.ascontiguousarray(v.reshape(-1, 128).T)
    bq = f("b_qkv")[0]
    k0, k1 = bq[1024:1088], bq[1088:1152]
    pp = np.concatenate([
        col(f("norm_mix_g")[0]), col(f("norm_mlp_g")[0]), col(f("norm_mix_g")[1]), col(f("norm_mlp_g")[1]),
        np.concatenate([col(f("ssm_conv_w")[0][i]) for i in range(4)], axis=1),
        col(f("ssm_conv_b")[0]),
        col(bq[0:1024]),
        np.stack([np.concatenate([k0, k0]), np.concatenate([k1, k1])], axis=1),
        col(f("b_o")[0]),
        col(f("ssm_norm_g")[0]),
    ], axis=1)
    assert pp.shape == (128, NPP), pp.shape
    bc = np.concatenate([
        f("gm_ln_g")[0], f("gm_ln_b")[0], f("final_norm_g"),
        bq[1152:1280], f("ssm_dt_bias")[0], f("ssm_a_log")[0], f("ssm_d")[0], f("attn_sinks")[0],
    ])[None, :]
    assert bc.shape == (1, NBC), bc.shape
    wq = f("w_qkv")[0]
    wq_ext = np.concatenate([wq[:, 0:1024], wq[:, 1024:1088], wq[:, 1024:1088], wq[:, 1088:1152], wq[:, 1088:1152],
                             wq[:, 1152:1280]], axis=1)
    shared = {
        "w_in": f("w_in_even")[0], "w_out": f("w_out_even")[0], "w_qkv": np.ascontiguousarray(wq_ext),
        "w_o": f("w_o")[0], "w_up": f("w_up"), "w_down": f("w_down"),
        "pp": np.ascontiguousarray(pp), "bc": np.ascontiguousarray(bc),
        "gmw": f("gm_w_s")[0], "gmb": f("gm_b_s")[0].reshape(1, 1024),
    }
    return shared


def kernel(**inputs):
    x = np.asarray(inputs["x"], dtype=np.float32)
    shared = _host_inputs(inputs)
    nc, _ = build(NT=8, stage="full")
    in_maps = []
    for b in range(8):
        m = dict(shared)
        m["x"] = np.ascontiguousarray(x[b])
        in_maps.append(m)
    res = run_bass_kernel_spmd(nc, in_maps, core_ids=list(range(8)))
    return np.stack([np.asarray(res.results[b]["out"], dtype=np.float32) for b in range(8)], axis=0)
```
